# Optimizing a Trainium2 kernel written in Bass

```python
import math
import jax, jax.numpy as jnp
from jax import lax
import numpy as np

D_MODEL = 1024
BATCH = 8
SEQ = 2048
DEPTH = 1
DEC_BATCH = 128
DEC_SEQ = 4
PAST_LEN = 16384
PAGE_SIZE = 128

H_A = 4
DK = 256
DV = 512
RET_CHUNK = 128
ROPE_BASE = 10000.0
GM_WIDTH = 1024
GM_GROUPS = 8
GM_GC = GM_WIDTH // GM_GROUPS
GM_CHUNK = 128
D_FF = 4 * D_MODEL
PLE_DIM = 256
EPS = 1e-6

Q_W = H_A * DK
V_W = H_A * DV
IN_SPLITS = (Q_W, Q_W, V_W, V_W, GM_WIDTH, GM_WIDTH, D_MODEL, D_MODEL)
IN_W = sum(IN_SPLITS)

kernel_name = "retnet_gmlp_gated_hybrid_step"


def rmsnorm(x, g):
    xf = x.astype(jnp.float32)
    y = xf * lax.rsqrt(jnp.mean(xf * xf, axis=-1, keepdims=True) + EPS)
    return (y * g.astype(jnp.float32)).astype(x.dtype)


def layernorm(x, g):
    xf = x.astype(jnp.float32)
    mu = jnp.mean(xf, axis=-1, keepdims=True)
    xc = xf - mu
    y = xc * lax.rsqrt(jnp.mean(xc * xc, axis=-1, keepdims=True) + EPS)
    return (y * g.astype(jnp.float32)).astype(x.dtype)


def rotary(x, pos):
    half = DK // 2
    inv = ROPE_BASE ** (-jnp.arange(half, dtype=jnp.float32) / half)
    ang = pos[:, None] * inv[None, :]
    cos = jnp.cos(ang)[None, :, None, :]
    sin = jnp.sin(ang)[None, :, None, :]
    xf = x.astype(jnp.float32)
    x1, x2 = xf[..., :half], xf[..., half:]
    return jnp.concatenate([x1 * cos - x2 * sin, x2 * cos + x1 * sin], axis=-1)


def log_decays():
    return jnp.log(1.0 - 2.0 ** (-5.0 - jnp.arange(H_A, dtype=jnp.float32)))


def retention_chunked(q, k, v, s0):
    b, L = q.shape[0], q.shape[1]
    c = min(RET_CHUNK, L)
    nc = L // c
    lg = log_decays()
    idx = jnp.arange(c, dtype=jnp.float32)
    diff = idx[:, None] - idx[None, :]
    dmat = jnp.where(diff[None] >= 0, jnp.exp(jnp.maximum(diff, 0.0)[None] * lg[:, None, None]), 0.0)
    in_decay = jnp.exp((idx[:, None] + 1.0) * lg[None, :])
    st_decay = jnp.exp((c - 1.0 - idx[None, :]) * lg[:, None])
    ch_decay = jnp.exp(c * lg)

    def to_chunks(a):
        return jnp.moveaxis(a.reshape(b, nc, c, *a.shape[2:]), 1, 0)

    qc, kc, vc = to_chunks(q), to_chunks(k), to_chunks(v)

    def step(s, inp):
        qi, ki, vi = inp
        sc = jnp.einsum('bqhd,bkhd->bhqk', qi, ki) * dmat[None]
        inner = jnp.einsum('bhqk,bkhv->bqhv', sc, vi)
        cross = jnp.einsum('bqhd,bhdv->bqhv', qi, s) * in_decay[None, :, :, None]
        s_new = s * ch_decay[None, :, None, None] + jnp.einsum('bkhd,bkhv,hk->bhdv', ki, vi, st_decay)
        return s_new, inner + cross

    s_fin, o = lax.scan(step, s0, (qc, kc, vc))
    o = jnp.moveaxis(o, 0, 1).reshape(b, L, H_A, DV)
    return o, s_fin


def spatial_gating(v, w_s, b_s):
    b, L = v.shape[0], v.shape[1]
    c = min(GM_CHUNK, L)
    nc = L // c
    mask = jnp.tril(jnp.ones((c, c), dtype=bool))
    w = jnp.where(mask[None], w_s[:, :c, :c], 0.0).astype(v.dtype)
    vr = v.reshape(b, nc, c, GM_GROUPS, GM_GC)
    out = jnp.einsum('gnm,bcmgd->bcngd', w, vr) + jnp.transpose(b_s[:, :c])[None, None, :, :, None].astype(v.dtype)
    return out.reshape(b, L, GM_WIDTH)


def layer(h, p_i, s0, pos, g_mix, w_in, g_ret, w_s, b_s, g_gm, w_br_a, w_br_b, w_o,
          g_mlp, w_up, w_down, g_ple, w_pg, w_pp):
    b, L = h.shape[0], h.shape[1]
    n = rmsnorm(h, g_mix)
    z = n @ w_in
    offs = np.cumsum((0,) + IN_SPLITS)
    zq, zk, zv, zg, zu, zgv, zga, zgb = [z[..., offs[j]:offs[j + 1]] for j in range(len(IN_SPLITS))]
    q = rotary(zq.reshape(b, L, H_A, DK), pos)
    k = rotary(zk.reshape(b, L, H_A, DK), pos) * (DK ** -0.5)
    vv = zv.reshape(b, L, H_A, DV).astype(jnp.float32)
    o, s_new = retention_chunked(q, k, vv, s0.astype(jnp.float32))
    o = rmsnorm(o, g_ret.reshape(H_A, DV)).reshape(b, L, V_W).astype(h.dtype)
    y_a = jax.nn.silu(zg) * o
    u = jax.nn.gelu(zu, approximate=False)
    gv = layernorm(jax.nn.gelu(zgv, approximate=False), g_gm)
    y_b = u * spatial_gating(gv, w_s, b_s)
    merged = jax.nn.sigmoid(zga) * (y_a @ w_br_a) + jax.nn.sigmoid(zgb) * (y_b @ w_br_b)
    h = h + merged @ w_o
    n2 = rmsnorm(h, g_mlp)
    h = h + jnp.square(jax.nn.relu(n2 @ w_up)) @ w_down
    n3 = rmsnorm(h, g_ple)
    h = h + jax.nn.sigmoid(n3 @ w_pg) * (p_i @ w_pp)
    return h, s_new.astype(s0.dtype), gv


def setup_inputs(seed: int = 0) -> dict:
    key = jax.random.key(seed)
    ks = jax.random.split(key, 24)
    f = jnp.float32

    def nrm(k, shape, scale):
        return jax.random.normal(k, shape, f) * scale

    def gain(k, shape):
        return 1.0 + 0.05 * jax.random.normal(k, shape, f)

    return {
        "x_prompt": nrm(ks[0], (BATCH, SEQ, D_MODEL), 1.0),
        "x_sample": nrm(ks[1], (DEC_BATCH, DEC_SEQ, D_MODEL), 1.0),
        "p_prompt": nrm(ks[2], (DEPTH, BATCH, SEQ, PLE_DIM), 1.0),
        "p_sample": nrm(ks[3], (DEPTH, DEC_BATCH, DEC_SEQ, PLE_DIM), 1.0),
        "state_ret": nrm(ks[4], (DEPTH, DEC_BATCH, H_A, DK, DV), 0.1),
        "g_mix": gain(ks[5], (DEPTH, D_MODEL)),
        "w_in": nrm(ks[6], (DEPTH, D_MODEL, IN_W), D_MODEL ** -0.5),
        "g_ret": gain(ks[7], (DEPTH, V_W)),
        "w_s": nrm(ks[8], (DEPTH, GM_GROUPS, GM_CHUNK, GM_CHUNK), 0.5 * GM_CHUNK ** -0.5),
        "b_s": 1.0 + 0.01 * jax.random.normal(ks[9], (DEPTH, GM_GROUPS, GM_CHUNK), f),
        "g_gm": gain(ks[10], (DEPTH, GM_WIDTH)),
        "w_br_a": nrm(ks[11], (DEPTH, V_W, D_MODEL), V_W ** -0.5),
        "w_br_b": nrm(ks[12], (DEPTH, GM_WIDTH, D_MODEL), GM_WIDTH ** -0.5),
        "w_o": nrm(ks[13], (DEPTH, D_MODEL, D_MODEL), D_MODEL ** -0.5),
        "g_mlp": gain(ks[14], (DEPTH, D_MODEL)),
        "w_up": nrm(ks[15], (DEPTH, D_MODEL, D_FF), D_MODEL ** -0.5),
        "w_down": nrm(ks[16], (DEPTH, D_FF, D_MODEL), D_FF ** -0.5),
        "g_ple": gain(ks[17], (DEPTH, D_MODEL)),
        "w_pg": nrm(ks[18], (DEPTH, D_MODEL, D_MODEL), D_MODEL ** -0.5),
        "w_pp": nrm(ks[19], (DEPTH, PLE_DIM, D_MODEL), PLE_DIM ** -0.5),
        "g_final": gain(ks[20], (D_MODEL,)),
    }


def reference(x_prompt, x_sample, p_prompt, p_sample, state_ret, g_mix, w_in, g_ret, w_s, b_s,
              g_gm, w_br_a, w_br_b, w_o, g_mlp, w_up, w_down, g_ple, w_pg, w_pp, g_final):
    pos_p = jnp.arange(SEQ, dtype=jnp.float32)
    pos_s = PAST_LEN + jnp.arange(DEC_SEQ, dtype=jnp.float32)
    hp, hs = x_prompt, x_sample
    sp_list, ss_list, gv_list = [], [], []
    for i in range(DEPTH):
        params = (g_mix[i], w_in[i], g_ret[i], w_s[i], b_s[i], g_gm[i], w_br_a[i], w_br_b[i], w_o[i],
                  g_mlp[i], w_up[i], w_down[i], g_ple[i], w_pg[i], w_pp[i])
        s0_p = jnp.zeros((BATCH, H_A, DK, DV), state_ret.dtype)
        hp, sp, _ = layer(hp, p_prompt[i], s0_p, pos_p, *params)
        hs, ss, gv_s = layer(hs, p_sample[i], state_ret[i], pos_s, *params)
        sp_list.append(sp)
        ss_list.append(ss)
        gv_list.append(gv_s)
    y_prompt = rmsnorm(hp, g_final)
    y_sample = rmsnorm(hs, g_final)
    state_ret_prompt = jnp.stack(sp_list, axis=0)
    state_ret_sample = jnp.stack(ss_list, axis=0)
    state_gm_v_sample = jnp.stack(gv_list, axis=0)
    return (y_prompt, y_sample, state_ret_prompt, state_ret_sample, state_gm_v_sample)
```

```python
import os
import math
import numpy as np
import concourse.bass as bass
import concourse.mybir as mybir
from concourse.bass_utils import run_bass_kernel_spmd

F32 = mybir.dt.float32
BF16 = mybir.dt.bfloat16
AF = mybir.ActivationFunctionType
ALU = mybir.AluOpType

N_CORES = 8
D = 1024
SEQ = 2048
NS = 512
NPASS = SEQ // NS
NSAMP = 16
SC = 64
NC = NS + SC
H_A, DK, DV = 4, 256, 512
PAST = 16384
EPS = 1e-6
R_SLAB = 4
SAME_ENGINE_ALL = bool(int(os.environ.get("MK_SEA", "0")))
USE_WSCR = True
DEBUG = bool(int(os.environ.get("MK_DEBUG", "0")))
DBG_PASSES = int(os.environ.get("MK_PASSES", str(NPASS)))
STOP = os.environ.get("MK_STOP", "")


class Op:
    __slots__ = ("eng", "fn", "idx", "waits", "signal", "val", "dma", "slot", "ninc", "small")

    def __init__(self, eng, fn, idx):
        self.eng = eng
        self.fn = fn
        self.idx = idx
        self.waits = []
        self.signal = False
        self.val = None
        self.dma = False
        self.slot = None
        self.ninc = 1
        self.small = False


class Sched:
    ENGS = ("pe", "act", "dve", "pool", "sp")

    def __init__(self):
        self.ops = {e: [] for e in self.ENGS}
        self.state = {}
        self.slot_cnt = {}
        self.last_dma = {}

    def _st(self, k):
        st = self.state.get(k)
        if st is None:
            st = [None, []]
            self.state[k] = st
        return st

    def add(self, eng, fn, reads=(), writes=(), dma_slot=None, ninc=1, accum=False):
        op = Op(eng, fn, len(self.ops[eng]))
        op.small = any(k[0] == "stat" for k in writes)
        deps = []
        if eng == "pe" and not accum:
            for k in writes:
                st = self.state.get(k)
                if k[0] == "ps" and st is not None and st[0] is not None and st[0].eng == "pe" and not st[1]:
                    raise AssertionError("PSUM bank %s overwritten by PE before its result was read" % (k,))
        for k in reads:
            st = self._st(k)
            if st[0] is not None:
                deps.append((st[0], True))
        for k in writes:
            st = self._st(k)
            if st[0] is not None:
                deps.append((st[0], False))
            for r in st[1]:
                deps.append((r, False))
        best = {}
        for d, raw in deps:
            if d.dma or d.eng == eng:
                continue
            b_ = best.get(d.eng)
            if b_ is None or d.idx > b_.idx:
                best[d.eng] = d
        deps = [(d, raw) for d, raw in deps if d.dma or d.eng == eng or best[d.eng] is d]
        seen = set()
        for d, raw in deps:
            if d is op or id(d) in seen:
                continue
            if (not d.dma) and d.eng == eng:
                if eng == "pe":
                    continue
                if not SAME_ENGINE_ALL and not (raw and d.small):
                    continue
            seen.add(id(d))
            op.waits.append(d)
            d.signal = True
        for k in reads:
            self._st(k)[1].append(op)
        for k in writes:
            st = self._st(k)
            st[0] = op
            st[1] = []
        if dma_slot is not None:
            op.dma = True
            op.slot = dma_slot
            op.ninc = ninc
            n = self.slot_cnt.get(dma_slot, 0) + ninc
            self.slot_cnt[dma_slot] = n
            op.val = 16 * n
            self.last_dma[dma_slot] = op
        self.ops[eng].append(op)
        return op

    def alias(self, new_keys, old_keys):
        best = {}
        dmas = {}
        for k in old_keys:
            st = self.state.get(k)
            if st is None:
                continue
            cands = list(st[1])
            if st[0] is not None:
                cands.append(st[0])
            for o in cands:
                if o.dma:
                    dmas[id(o)] = o
                else:
                    b = best.get(o.eng)
                    if b is None or o.idx > b.idx:
                        best[o.eng] = o
            self.state[k] = [None, []]
        fence = list(best.values()) + list(dmas.values())
        for k in new_keys:
            st = self._st(k)
            have = {id(o) for o in st[1]}
            st[1] = list(st[1]) + [o for o in fence if id(o) not in have]

    def emit(self, nc, eng_sems, slot_sems, final_slots):
        for e in ("pe", "act", "dve", "pool"):
            c = 0
            for op in self.ops[e]:
                if op.dma:
                    continue
                if op.signal:
                    c += 1
                    op.val = c
        fin = Op("sp", None, len(self.ops["sp"]))
        for s in final_slots:
            if s in self.last_dma:
                fin.waits.append(self.last_dma[s])
        self.ops["sp"].append(fin)

        def sem_of(d):
            return slot_sems[d.slot] if d.dma else eng_sems[d.eng]

        def make(e):
            def body(eng):
                waited = {}
                for op in self.ops[e]:
                    for d in op.waits:
                        key = ("s", d.slot) if d.dma else ("e", d.eng)
                        if waited.get(key, 0) >= d.val:
                            continue
                        eng.wait_ge(sem_of(d), d.val)
                        waited[key] = d.val
                    if op.fn is None:
                        continue
                    inst = op.fn(eng)
                    if op.dma:
                        insts = inst if isinstance(inst, (list, tuple)) else [inst]
                        assert len(insts) == op.ninc
                        for i_ in insts:
                            i_.then_inc(slot_sems[op.slot], 16)
                    elif op.signal:
                        inst.then_inc(eng_sems[e], 1)
            return body

        with nc.Block() as block:
            block.tensor(make("pe"))
            block.scalar(make("act"))
            block.vector(make("dve"))
            block.gpsimd(make("pool"))
            block.sync(make("sp"))


def _host_consts():
    f = np.float32
    half = DK // 2
    inv = (np.float32(10000.0) ** (-np.arange(half, dtype=f) / np.float32(half))).astype(f)
    pos_p = np.arange(SEQ, dtype=f)
    pos_s = (np.float32(PAST) + np.arange(4, dtype=f)).astype(f)
    pos_s = np.tile(pos_s, NSAMP)
    pos = np.concatenate([pos_p, pos_s]).astype(f)
    ang = (pos[None, :] * inv[:, None]).astype(f)
    cosT = np.cos(ang.astype(np.float64)).astype(f)
    sinT = np.sin(ang.astype(np.float64)).astype(f)

    lg = np.log(1.0 - 2.0 ** (-5.0 - np.arange(H_A, dtype=np.float64)))
    idx = np.arange(128, dtype=np.float64)
    js = (np.arange(SC) % 4).astype(np.float64)
    smp = np.arange(SC) // 4

    cm = {}
    cm["identf"] = np.eye(128, dtype=f)
    cm["tril"] = (idx[:, None] <= idx[None, :]).astype(f)
    maskTs = np.zeros((128, H_A, SC), f)
    qdec = np.zeros((128, H_A, NS), f)
    qdec_s = np.zeros((128, H_A, SC), f)
    stdec = np.zeros((128, H_A * 4), f)
    rowsc = np.zeros((128, H_A * 4), f)
    stdec_s = np.zeros((128, H_A), f)
    tl = np.arange(NS, dtype=np.float64)
    for h in range(H_A):
        ms = ((smp[None, :] == smp[:, None]) & (js[None, :] >= js[:, None])) * \
            np.exp(-(js[:, None] + 1.0) * lg[h]) / 16.0
        maskTs[:SC, h, :] = ms.astype(f)
        qdec[:, h, :] = np.exp((tl[None, :] + 1.0) * lg[h]).astype(f)
        qdec_s[:, h, :] = np.exp((js[None, :] + 1.0) * lg[h]).astype(f)
        for ck in range(4):
            kl = ck * 128 + idx
            stdec[:, h * 4 + ck] = (np.exp((NS - 1.0 - kl) * lg[h]) / 16.0).astype(f)
            rowsc[:, h * 4 + ck] = (np.exp(-(kl + 1.0) * lg[h]) / 16.0).astype(f)
        stdec_s[:SC, h] = (np.exp((3.0 - js) * lg[h]) / 16.0).astype(f)
    cmask = np.ones((128, NS), f)
    cmask[:, :128] = (idx[None, :] >= idx[:, None]).astype(f)
    cm["cmask"] = cmask
    cm["rowsc"] = rowsc
    cm["maskTs"] = maskTs.reshape(128, -1)
    cm["qdec"] = qdec.reshape(128, -1)
    cm["qdec_s"] = qdec_s.reshape(128, -1)
    cm["stdec"] = stdec
    cm["stdec_s"] = stdec_s
    qmask = np.zeros((128, NSAMP, SC), f)
    for i in range(NSAMP):
        qmask[:, i, 4 * i:4 * i + 4] = 1.0
    cm["qmask"] = qmask.reshape(128, -1)
    kmask = np.zeros((128, NSAMP), f)
    for i in range(NSAMP):
        kmask[4 * i:4 * i + 4, i] = 1.0
    cm["kmask"] = kmask
    bms = np.zeros((128, SC), f)
    bms[:SC, :] = ((smp[:, None] == smp[None, :]) & (js[:, None] <= js[None, :])).astype(f)
    cm["bms"] = bms
    chdec = [float(np.exp(float(NS) * lg[h])) for h in range(H_A)]
    chdec_s = [float(np.exp(4.0 * lg[h])) for h in range(H_A)]
    return cosT, sinT, cm, chdec, chdec_s


CM_ORDER = ["identf", "cmask", "rowsc", "maskTs", "qdec", "qdec_s", "stdec", "stdec_s",
            "kmask", "bms"]


def _cm_layout(cm):
    off = {}
    o = 0
    for k in CM_ORDER:
        w = cm[k].shape[1]
        off[k] = (o, w)
        o += w
    return off, o


def build_program(cm_off, cm_w, chdec, chdec_s, npass=NPASS):
    nc = bass.Bass("TRN2", target_bir_lowering=False)
    S = Sched()

    def din(name, shape, dt=F32):
        return nc.dram_tensor(name, list(shape), dt, kind="ExternalInput").ap()

    def dout(name, shape, dt=F32):
        return nc.dram_tensor(name, list(shape), dt, kind="ExternalOutput").ap()

    xp = din("xp", [SEQ, D]); pp = din("pp", [SEQ, 256])
    xs = din("xs", [SC, D]); psm = din("ps", [SC, 256])
    st_in = din("st", [NSAMP, H_A, DK, DV])
    w_in = din("w_in", [D, 10240]); w_br_a = din("w_br_a", [2048, D]); w_br_b = din("w_br_b", [D, D])
    w_o = din("w_o", [D, D]); w_up = din("w_up", [D, 4096]); w_down = din("w_down", [4096, D])
    w_pg = din("w_pg", [D, D]); w_pp = din("w_pp", [256, D])
    gcols_d = din("gcols", [128, 40])
    ggm_d = din("ggm_rep", [128, D]); gfin_d = din("gfin_rep", [128, D])
    wsT_d = din("wsT", [128, 8 * 128]); wsTs_d = din("wsTs", [128, 8 * SC])
    bsr_d = din("bs_rep", [128, 8 * 128]); bsrs_d = din("bs_rep_s", [128, 8 * SC])
    cos_d = din("cosT", [128, SEQ + SC]); sin_d = din("sinT", [128, SEQ + SC])
    cm_d = din("cmisc", [128, cm_w])
    qmask_d = din("qmask", [128, NSAMP * SC])
    tril_d = din("tril", [128, 128])

    y_p = dout("y_p", [SEQ, D]); y_s = dout("y_s", [SC, D])
    s_p = dout("s_p", [H_A, DK, DV]); s_s = dout("s_s", [NSAMP, H_A, DK, DV])
    gv_s = dout("gv_s", [SC, D])

    dbg_outs = {}
    NSLABS = 47
    wscr = nc.dram_tensor("wscr", [NSLABS, 128, 4096], BF16, kind="Internal").ap()

    def kcv(w):
        return w.rearrange("(kc p) f -> p kc f", p=128)

    w_in_v = kcv(w_in); w_br_a_v = kcv(w_br_a); w_br_b_v = kcv(w_br_b); w_o_v = kcv(w_o)
    w_up_v = kcv(w_up); w_down_v = kcv(w_down); w_pg_v = kcv(w_pg); w_pp_v = kcv(w_pp)

    import contextlib
    es = contextlib.ExitStack()
    with es:
        def sb(name, shape, dt):
            return es.enter_context(nc.sbuf_tensor(name, list(shape), dt))

        hT = sb("hT", [128, 8, NC], F32)
        nT = sb("nT", [128, 8, NC], BF16)
        A16 = sb("A16", [128, 16, NC], BF16)
        A8a = sb("A8a", [128, 8, NC], BF16)
        A8b = sb("A8b", [128, 8, NC], BF16)
        HB = sb("HB", [128, 9216], BF16)
        S32 = sb("S32", [128, H_A, 2, 512], F32)
        Sbf = sb("Sbf", [128, 2, 512], BF16)
        Ss32 = sb("Ss32", [128, 2, 2, 512], F32)
        Ssb = sb("Ssb", [128, 2, 2, 512], BF16)
        SL = sb("SL", [128, R_SLAB, 4096], BF16)
        T4 = sb("T4", [128, 4, 1024], F32)
        pT = sb("pT", [128, 2, NC], BF16)
        qm = sb("qm", [128, 2, NSAMP, SC], BF16)
        scm = sb("scm", [128, 4, 512], BF16)
        yatm = sb("yatm", [128, 4, 512], BF16)
        junkb = sb("junkb", [128, 512], BF16)
        stat = sb("stat", [128, 64], F32)
        cmisc = sb("cmisc_sb", [128, cm_w], F32)
        identb = sb("identb", [128, 128], BF16)
        onesb = sb("onesb", [128, 128], BF16)
        qmaskb = sb("qmaskb", [128, NSAMP, SC], BF16)
        gcols = sb("gcols_sb", [128, 40], F32)
        ggm = sb("ggm_sb", [128, D], F32)
        gfin = sb("gfin_sb", [128, D], F32)
        wsTb = sb("wsTb", [128, 8, 128], BF16)
        wsTsb = sb("wsTsb", [128, 8, SC], BF16)
        bsr = sb("bsr", [128, 8, 128], F32)
        bsrs = sb("bsrs", [128, 8, SC], F32)
        PS = es.enter_context(nc.psum_tensor("PS", [128, 8, 512], F32))

        def cmv(name):
            o, w = cm_off[name]
            return cmisc[:, o:o + w]

        identf = cmv("identf")
        cmask = cmv("cmask")
        rowsc = cmv("rowsc")
        maskTs = cmv("maskTs").rearrange("p (h n) -> p h n", h=H_A)
        qdec = cmv("qdec").rearrange("p (h n) -> p h n", h=H_A)
        qdec_s = cmv("qdec_s").rearrange("p (h n) -> p h n", h=H_A)
        stdec = cmv("stdec"); stdec_s = cmv("stdec_s")
        kmask = cmv("kmask")

        cosv = A8b[:, 0:4, :].bitcast(F32)
        A8b_f = A8b[:, :, :].rearrange("p k n -> p (k n)").bitcast(F32)
        cosT = A8b_f[:, 0:NC]
        sinT = A8b_f[:, NC:2 * NC]
        qkT = HB[:, 0:4 * NC].rearrange("p (k n) -> p k n", k=4)
        o1 = 4 * NC
        ktm = HB[:, o1:o1 + 5 * 256].rearrange("p (c n) -> p c n", c=5)
        o2 = o1 + 5 * 256
        vtm = HB[:, o2:o2 + 5 * 512].rearrange("p (c n) -> p c n", c=5)
        o3 = o2 + 5 * 512
        gsm = HB[:, o3:o3 + 5 * 512].rearrange("p (c n) -> p c n", c=5)
        assert o3 + 5 * 512 <= 9216
        gvtm = HB[:, 0:5 * 1024].rearrange("p (c n) -> p c n", c=5)
        sga = HB[:, 0:8 * NC].rearrange("p (k n) -> p k n", k=8)
        sgb = HB[:, 8 * NC:16 * NC].rearrange("p (k n) -> p k n", k=8)
        km = A8a[:, :, :].rearrange("p k n -> p (k n)")[:, 0:NSAMP * 256].rearrange("p (i n) -> p i n", i=NSAMP)

        HB_B_KEYS = [("qk", i, g) for i in range(4) for g in range(2)] + \
                    [(n, c) for n in ("ktm", "vtm", "gs") for c in range(5)]
        HB_C_KEYS = [("gv", c) for c in range(5)]
        HB_D_KEYS = [(n, f_, g) for n in ("sga", "sgb") for f_ in range(8) for g in range(2)]
        A8A_KEYS = [("A8a", k, g) for k in range(8) for g in range(2)]
        A8B_KEYS = [("A8b", k, g) for k in range(8) for g in range(2)]

        ring_cnt = {"s": 0, "p": 0}
        SBANK = 7

        def ps1(ring=None):
            b = ring_cnt["s"] % 7
            ring_cnt["s"] += 1
            return b

        def ps2(ring=None):
            b = 2 * (ring_cnt["p"] % 3)
            ring_cnt["p"] += 1
            return b

        def psk(b, n=1):
            return [("ps", b + i) for i in range(n)]

        def t4k(sl):
            return [("T4", sl), ("T4", sl, "b")]

        def mm_group(out_ap, pairs, reads, bank, nb=1, transpose=False, ident=None):
            pairs = list(pairs)

            def fn(pe, pairs=pairs, out_ap=out_ap):
                last = None
                n = len(pairs)
                for i, (l, r) in enumerate(pairs):
                    last = pe.matmul(out_ap, l, r, start=(i == 0), stop=(i == n - 1))
                return last
            return S.add("pe", fn, reads=reads, writes=psk(bank, nb))

        def mm_multi(items, reads, bank, nb=1):
            items = list(items)

            def fn(pe, items=items):
                last = None
                for (o, l, r) in items:
                    last = pe.matmul(o, l, r, start=True, stop=True)
                return last
            return S.add("pe", fn, reads=reads, writes=psk(bank, nb))

        def tr_multi(items, reads, bank, nb=1):
            items = list(items)

            def fn(pe, items=items):
                last = None
                for (o, i_, idn) in items:
                    last = pe.transpose(o, i_, idn)
                return last
            return S.add("pe", fn, reads=reads, writes=psk(bank, nb))

        def act(out, in_, func, reads, writes, **kw):
            def fn(e, out=out, in_=in_, func=func, kw=kw):
                return e.activation(out, in_, func, **kw)
            return S.add("act", fn, reads=reads, writes=writes)

        def tt(out, in0, in1, op, reads, writes):
            def fn(e, out=out, in0=in0, in1=in1, op=op):
                return e.tensor_tensor(out, in0, in1, op)
            return S.add("dve", fn, reads=reads, writes=writes)

        def stt(out, in0, scalar, in1, op0, op1, reads, writes):
            def fn(e, out=out, in0=in0, scalar=scalar, in1=in1, op0=op0, op1=op1):
                return e.scalar_tensor_tensor(out, in0, scalar, in1, op0, op1)
            return S.add("dve", fn, reads=reads, writes=writes)

        def ts(out, in0, s1, s2, op0, op1, reads, writes):
            def fn(e, out=out, in0=in0, s1=s1, s2=s2, op0=op0, op1=op1):
                if op1 is None:
                    return e.tensor_scalar(out, in0, s1, None, op0)
                return e.tensor_scalar(out, in0, s1, s2, op0, op1)
            return S.add("dve", fn, reads=reads, writes=writes)

        def recip(out, in_, reads, writes):
            def fn(e, out=out, in_=in_):
                return e.reciprocal(out, in_)
            return S.add("dve", fn, reads=reads, writes=writes)

        def dcopy(out, in_, reads, writes):
            def fn(e, out=out, in_=in_):
                return e.tensor_copy(out, in_)
            return S.add("dve", fn, reads=reads, writes=writes)

        def dma(q, out, in_, reads, writes, slot):
            def fn(e, out=out, in_=in_):
                return e.dma_start(out=out, in_=in_)
            return S.add(q, fn, reads=reads, writes=writes, dma_slot=slot)

        def dump(name, ap, shape, reads, bf=False):
            if not DEBUG:
                return
            d = dout("dbg_" + name, shape)
            dbg_outs[name] = shape
            dma("pool" if bf else "sp", d, ap, reads, [("dbg", name)], ("dbg", name))

        slab_cnt = [0]
        slab_ids = {}
        cur_pass = [0]

        pend_st = []

        def flush_store():
            idx_, slot_, sz_ = pend_st.pop(0)
            dma("pool", wscr[idx_, :, 0:sz_], SL[:, slot_, 0:sz_], [("slab", slot_)], [("wscr", idx_)],
                ("wst", slot_))

        def load_slab(pieces, shape, sid=None):
            slot = slab_cnt[0] % R_SLAB
            slab_cnt[0] += 1
            k, n = shape
            while any(ps_[1] == slot for ps_ in pend_st):
                flush_store()
            view = SL[:, slot, 0:k * n].rearrange("p (k n) -> p k n", k=k)
            if sid is None:
                sid = tuple((a, b, str(src.tensor.name), str(src.offset), str(src.ap)) for (a, b), src in pieces)
            first = sid not in slab_ids
            if first:
                slab_ids[sid] = len(slab_ids)
                assert len(slab_ids) <= NSLABS
            idx = slab_ids[sid]
            if first or not USE_WSCR:
                def fn(e, pieces=pieces, view=view):
                    insts = []
                    for (a, b), src in pieces:
                        insts.append(e.dma_start(out=view[:, :, a:b], in_=src))
                    return insts
                S.add("pool", fn, reads=[], writes=[("slab", slot)], dma_slot=("slab", slot), ninc=len(pieces))
                if USE_WSCR and npass > 1:
                    pend_st.append((idx, slot, k * n))
                    while len(pend_st) > 2:
                        flush_store()
            else:
                dma("pool", SL[:, slot, 0:k * n], wscr[idx, :, 0:k * n], [("wscr", idx)], [("slab", slot)],
                    ("slab", slot))
            return view, ("slab", slot)

        def wslab(wv, c0, ncols=512, k0=0, k=8):
            return load_slab([((0, ncols), wv[:, k0:k0 + k, c0:c0 + ncols])], (k, ncols))

        dma("sp", cmisc[:, :], cm_d[:, :], [], [("cm",)], ("c0",))
        dma("sp", gcols[:, :], gcols_d[:, :], [], [("gcols",)], ("c1",))
        dma("sp", ggm[:, :], ggm_d[:, :], [], [("ggm",)], ("c2",))
        dma("sp", gfin[:, :], gfin_d[:, :], [], [("gfin",)], ("c3",))
        dma("sp", bsr[:, :, :], bsr_d.rearrange("p (g n) -> p g n", g=8), [], [("bsr",)], ("c4",))
        dma("sp", bsrs[:, :, :], bsrs_d.rearrange("p (g n) -> p g n", g=8), [], [("bsrs",)], ("c5",))
        T4f = T4[:, :, :].rearrange("p a b -> p (a b)")
        dma("sp", T4[:, 0, :], wsT_d[:, :], [], t4k(0), ("c6",))
        dma("sp", T4[:, 1, 0:8 * SC], wsTs_d[:, :], [], [("T4", 1)], ("c7",))
        dcopy(identb[:, :], identf, [("cm",)], [("identb",)])
        S.add("dve", lambda e: e.memset(onesb[:, :], 1.0), reads=[], writes=[("onesb",)])
        dma("pool", qmaskb[:, :, :], qmask_d.rearrange("p (i n) -> p i n", i=NSAMP), [], [("qmaskb",)], ("c8",))
        dma("sp", T4[:, 2, 0:128], tril_d[:, :], [], [("T4", 2)], ("c9",))
        tril = T4[:, 2, 0:128]
        tt(wsTb[:, :, :], T4[:, 0, :].rearrange("p (g n) -> p g n", g=8),
           tril.unsqueeze(1).broadcast_to([128, 8, 128]), ALU.mult,
           t4k(0) + [("T4", 2)], [("wsTb",)])
        tt(wsTsb[:, :, :], T4[:, 1, 0:8 * SC].rearrange("p (g n) -> p g n", g=8),
           cmv("bms").unsqueeze(1).broadcast_to([128, 8, SC]), ALU.mult,
           [("T4", 1), ("cm",)], [("wsTsb",)])

        gmix = gcols[:, 0:8]; gmlp = gcols[:, 8:16]; gple = gcols[:, 16:24]; gret = gcols[:, 24:40]

        def groups(last):
            g = [(0, 0, NS)]
            if last:
                g.append((1, NS, SC))
            return g

        def chunks(last):
            cs = [(c, c * 128, 128) for c in range(NS // 128)]
            if last:
                cs.append((4, NS, SC))
            return cs

        def hkeys(g):
            return [("hT", k, g) for k in range(8)]

        def rmsnorm_fm(gcol, last):
            for (g, c0, n) in groups(last):
                b = ps1("D")
                for k in range(8):
                    act(A16[:, k, c0:c0 + n], hT[:, k, c0:c0 + n], AF.Square, [("hT", k, g)], [("A16", k, g)])
                    S.add("pe", lambda pe, k=k, b=b, c0=c0, n=n: pe.matmul(
                        PS[:, b, 0:n], onesb[:, :], A16[:, k, c0:c0 + n], start=(k == 0), stop=(k == 7)),
                        reads=[("A16", k, g), ("onesb",)], writes=psk(b), accum=(k > 0))
                act(T4[:, 3, 0:n], PS[:, b, 0:n], AF.Sqrt, psk(b), [("T4", 3)], scale=1.0 / D, bias=EPS)
                recip(T4[:, 3, 512:512 + n], T4[:, 3, 0:n], [("T4", 3)], [("T4", 3, "b")])
                for k in range(8):
                    stt(nT[:, k, c0:c0 + n], hT[:, k, c0:c0 + n], gcol[:, k:k + 1], T4[:, 3, 512:512 + n],
                        ALU.mult, ALU.mult, [("hT", k, g), ("T4", 3, "b"), ("gcols",)], [("nT", g)])

        def fm_proj(slab, skey, ft_cols, g, c0, n, bank, rhs_buf=None, rkeys=None, K=8):
            rb = nT if rhs_buf is None else rhs_buf
            rk = [("nT", g)] if rkeys is None else rkeys
            a, b_ = ft_cols
            return mm_group(PS[:, bank, 0:n], [(slab[:, k, a:b_], rb[:, k, c0:c0 + n]) for k in range(K)],
                            [skey] + rk, bank)

        for p in range(npass):
            if STOP == "setup":
                break
            last = (p == NPASS - 1)
            while pend_st:
                flush_store()
            t0 = p * NS
            grp = groups(last)
            chk = chunks(last)

            S.alias([("cs",)], A8B_KEYS)
            dma("sp", cosT[:, 0:NS], cos_d[:, t0:t0 + NS], [], [("cs",)], ("cs0",))
            dma("sp", sinT[:, 0:NS], sin_d[:, t0:t0 + NS], [], [("cs", 1)], ("cs1",))
            dma("sp", cosT[:, NS:NC], cos_d[:, SEQ:SEQ + SC], [], [("cs", 2)], ("cs2",))
            dma("sp", sinT[:, NS:NC], sin_d[:, SEQ:SEQ + SC], [], [("cs", 3)], ("cs3",))
            CSK = [("cs",), ("cs", 1), ("cs", 2), ("cs", 3)]
            for (c, c0, n) in chunks(p == 0):
                g = 0 if c < 4 else 1
                slot = c % 2
                src = xp[t0 + c0:t0 + c0 + n, :] if c < 4 else xs[:, :]
                if not (p > 0 and c < 2):
                    dma("sp", T4[0:n, slot, :], src, [], t4k(slot), ("xs", slot))
                b = ps2("D")
                tr_multi([(PS[:, b + k // 4, (k % 4) * 128:(k % 4) * 128 + n],
                           T4[0:n, slot, k * 128:(k + 1) * 128], identf[0:n, 0:n]) for k in range(8)],
                         t4k(slot) + [("cm",)], b, 2)
                src_ps = PS[:, b:b + 2, :].rearrange("p b (j n) -> p (b j) n", j=4)[:, :, 0:n]
                act(hT[:, :, c0:c0 + n], src_ps, AF.Copy, psk(b, 2), hkeys(g))
            rmsnorm_fm(gmix, p == 0)
            if p == 0:
                dump("hT0", hT[:, :, 0:NS], [128, 8, NS], hkeys(0))
                dump("nT0", nT[:, :, 0:NS], [128, 8, NS], [("nT", 0)], bf=True)

            if STOP == "A":
                break
            S.alias(HB_B_KEYS, HB_D_KEYS + HB_C_KEYS)
            S.alias([("km",)], A8A_KEYS)
            from collections import deque
            pending = deque()

            def pump(k=1):
                for _ in range(k):
                    if pending:
                        pending.popleft()()

            slabs = {}

            def F1(h, with_s):
                qc0 = h * 256
                kc0 = 1024 + h * 256
                slab_qk, kqk = load_slab([((0, 256), w_in_v[:, :, qc0:qc0 + 256]),
                                          ((256, 512), w_in_v[:, :, kc0:kc0 + 256])], (8, 512))
                slabs[h] = (wslab(w_in_v, 2048 + h * 512), wslab(w_in_v, 4096 + h * 512))
                for (g, c0, n) in groups(with_s):
                    for which in range(2):
                        b = ps2()
                        fm_proj(slab_qk, kqk, (which * 256, which * 256 + 128), g, c0, n, b)
                        fm_proj(slab_qk, kqk, (which * 256 + 128, which * 256 + 256), g, c0, n, b + 1)
                        x1 = PS[:, b, 0:n]; x2 = PS[:, b + 1, 0:n]
                        cs_ = cosT[:, c0:c0 + n]; sn_ = sinT[:, c0:c0 + n]
                        t1 = T4[:, 0, 0:n]; t2 = T4[:, 0, 512:512 + n]; t3 = T4[:, 1, 0:n]; t4 = T4[:, 1, 512:512 + n]
                        tt(t1, x1, cs_, ALU.mult, psk(b) + CSK, [("T4", 0)])
                        tt(t2, x2, sn_, ALU.mult, psk(b + 1) + CSK, [("T4", 0, "b")])
                        tt(t3, x2, cs_, ALU.mult, psk(b + 1) + CSK, [("T4", 1)])
                        tt(t4, x1, sn_, ALU.mult, psk(b) + CSK, [("T4", 1, "b")])
                        if which == 0:
                            tt(t1, t1, t2, ALU.subtract, [("T4", 0), ("T4", 0, "b")], [("T4", 0)])
                            tt(t3, t3, t4, ALU.add, [("T4", 1), ("T4", 1, "b")], [("T4", 1)])
                            dq = qdec[:, h, :] if g == 0 else qdec_s[:, h, :]
                            tt(qkT[:, 0, c0:c0 + n], t1, dq, ALU.mult, [("T4", 0), ("cm",)], [("qk", 0, g)])
                            tt(qkT[:, 1, c0:c0 + n], t3, dq, ALU.mult, [("T4", 1), ("cm",)], [("qk", 1, g)])
                        else:
                            tt(qkT[:, 2, c0:c0 + n], t1, t2, ALU.subtract,
                               [("T4", 0), ("T4", 0, "b")], [("qk", 2, g)])
                            tt(qkT[:, 3, c0:c0 + n], t3, t4, ALU.add,
                               [("T4", 1), ("T4", 1, "b")], [("qk", 3, g)])
                        pump()

            def F2(h, with_s):
                (slab_v, kv), (slab_g, kg) = slabs[h]
                for (c, c0, n) in chunks(with_s):
                    g = 0 if c < 4 else 1
                    bv = ps1()
                    mm_group(PS[0:n, bv, :], [(nT[:, k, c0:c0 + n], slab_v[:, k, :]) for k in range(8)],
                             [("nT", g), kv], bv)
                    act(vtm[0:n, c, :], PS[0:n, bv, :], AF.Copy, psk(bv), [("vtm", c)])
                    pump()
                    bg = ps1()
                    mm_group(PS[0:n, bg, :], [(nT[:, k, c0:c0 + n], slab_g[:, k, :]) for k in range(8)],
                             [("nT", g), kg], bg)
                    act(gsm[0:n, c, :], PS[0:n, bg, :], AF.Silu, psk(bg), [("gs", c)])
                    pump()
                b = ps1()
                pst = PS[:, b, :].bitcast(BF16)
                tr_multi([(pst[:, c * 256 + dc * 128:c * 256 + (dc + 1) * 128], qkT[:, 2 + dc, c * 128:(c + 1) * 128],
                           identb[:, :]) for c in range(4) for dc in range(2)],
                         [("qk", 2, 0), ("qk", 3, 0), ("identb",)], b)
                sdb = stdec[:, h * 4:(h + 1) * 4].unsqueeze(2).broadcast_to([128, 4, 256])
                tt(ktm[:, 0:4, :], pst[:, 0:1024].rearrange("p (c n) -> p c n", c=4), sdb, ALU.mult,
                   psk(b) + [("cm",)], [("ktm", c) for c in range(4)])
                if with_s:
                    b = ps1()
                    pst = PS[:, b, :].bitcast(BF16)
                    tr_multi([(pst[0:SC, dc * 128:(dc + 1) * 128], qkT[:, 2 + dc, NS:NC], identb[:, :])
                              for dc in range(2)], [("qk", 2, 1), ("qk", 3, 1), ("identb",)], b)
                    act(ktm[0:SC, 4, :], pst[0:SC, 0:256], AF.Copy, psk(b) + [("cm",)], [("ktm", 4)],
                        scale=stdec_s[0:SC, h:h + 1])

            def B1(h):
                if p > 0:
                    act(Sbf[:, :, :], S32[:, h, :, :], AF.Copy, [("S32", h)], [("Sbf",)])
                QK = [("qk", i, 0) for i in range(4)]
                sb_ = []
                for ck in range(4):
                    nq = NS - ck * 128
                    bsc = ps1()
                    sb_.append(bsc)
                    mm_group(PS[:, bsc, 0:nq],
                             [(qkT[:, 2 + dc, ck * 128:(ck + 1) * 128], qkT[:, dc, ck * 128:NS]) for dc in range(2)],
                             QK, bsc)
                LV = int(os.environ.get("MK_B1", "99"))
                if LV < 1:
                    return
                for ck in range(4):
                    nq = NS - ck * 128
                    stt(scm[:, ck, 0:nq], PS[:, sb_[ck], 0:nq], rowsc[:, h * 4 + ck:h * 4 + ck + 1], cmask[:, 0:nq],
                        ALU.mult, ALU.mult, psk(sb_[ck]) + [("cm",)], [("scm", ck)])
                if LV < 2:
                    return
                bu = ps2()
                for dc in range(2):
                    mm_group(PS[:, bu + dc, :],
                             [(ktm[:, c, dc * 128:(dc + 1) * 128], vtm[:, c, :]) for c in range(4)],
                             [("ktm", c) for c in range(4)] + [("vtm", c) for c in range(4)], bu + dc)
                Sv = S32[:, h, :, :]
                Uv = PS[:, bu:bu + 2, :]
                if p == 0:
                    dcopy(Sv, Uv, psk(bu, 2), [("S32", h)])
                else:
                    stt(Sv, Sv, chdec[h], Uv, ALU.mult, ALU.add, psk(bu, 2) + [("S32", h)], [("S32", h)])
                if LV < 3:
                    return
                ob = []
                for cq in range(4):
                    bo = ps1()
                    ob.append(bo)
                    pairs = [(scm[:, ck, (cq - ck) * 128:(cq - ck + 1) * 128], vtm[:, ck, :]) for ck in range(cq + 1)]
                    rk = [("scm", ck) for ck in range(cq + 1)] + [("vtm", ck) for ck in range(cq + 1)]
                    if p > 0:
                        pairs += [(qkT[:, dc, cq * 128:(cq + 1) * 128], Sbf[:, dc, :]) for dc in range(2)]
                        rk += [("qk", 0, 0), ("qk", 1, 0), ("Sbf",)]
                    mm_group(PS[:, bo, :], pairs, rk, bo)
                if LV < 5:
                    return
                for cq in range(4):
                    act(junkb[:, 0:512], PS[:, ob[cq], :], AF.Square, psk(ob[cq]), [("junkb",), ("stat", 0, cq)],
                        accum_out=stat[:, cq:cq + 1])
                act(stat[:, 4:8], stat[:, 0:4], AF.Sqrt, [("stat", 0, cq) for cq in range(4)], [("stat", 1)],
                    scale=1.0 / DV, bias=EPS)
                recip(stat[:, 8:12], stat[:, 4:8], [("stat", 1)], [("stat", 2)])
                for cq in range(4):
                    stt(yatm[:, cq, :], PS[:, ob[cq], :], stat[:, 8 + cq:9 + cq], gsm[:, cq, :],
                        ALU.mult, ALU.mult, psk(ob[cq]) + [("stat", 2), ("gs", cq)], [("yatm", cq)])

            def B2(h):
                bt = ps2()
                pst = PS[:, bt:bt + 2, :].rearrange("p b n -> p (b n)").bitcast(BF16)
                pv = pst.rearrange("p (b j c n) -> p b j c n", b=2, j=4, c=4)
                items = []
                for j in range(4):
                    bb = PS[:, bt + j // 2, :].bitcast(BF16)
                    for cq in range(4):
                        o0 = (j % 2) * 512 + cq * 128
                        items.append((bb[:, o0:o0 + 128], yatm[:, cq, j * 128:(j + 1) * 128], identb[:, :]))
                tr_multi(items, [("yatm", cq) for cq in range(4)] + [("identb",)], bt, 2)
                for half in range(2):
                    bb = PS[:, bt + half, :].bitcast(BF16)
                    srcv = bb[:, 0:1024].rearrange("p (j n) -> p j n", j=2)
                    j0 = h * 4 + half * 2
                    grb = gret[:, j0:j0 + 2].unsqueeze(2).broadcast_to([128, 2, NS])
                    tt(A16[:, j0:j0 + 2, 0:NS], srcv, grb, ALU.mult,
                       psk(bt + half) + [("gcols",)], [("A16", j0 + j, 0) for j in range(2)])
                if last:
                    dst = s_p[h].rearrange("(dc p) v -> p dc v", p=128)
                    dma("sp", dst, S32[:, h, :, :], [("S32", h)], [("s_p_out",)], ("spout",))

            def sample_steps(h):
                def ptt(out, in0, in1, op, reads, writes):
                    S.add("pool", lambda e, out=out, in0=in0, in1=in1, op=op: e.tensor_tensor(out, in0, in1, op),
                          reads=reads, writes=writes)
                for dc in range(2):
                    ptt(qm[:, dc, :, :], qkT[:, dc, NS:NC].unsqueeze(1).broadcast_to([128, NSAMP, SC]),
                        qmaskb[:, :, :], ALU.mult, [("qk", dc, 1), ("qmaskb",)], [("qm", dc)])
                ptt(km[0:SC, :, :], ktm[0:SC, 4, :].unsqueeze(1).broadcast_to([SC, NSAMP, 256]),
                    kmask[0:SC, :].unsqueeze(2).broadcast_to([SC, NSAMP, 256]), ALU.mult,
                    [("ktm", 4), ("cm",)], [("km",)])

                def head():
                    bsc = ps1()
                    mm_group(PS[0:SC, bsc, 0:SC],
                             [(qkT[:, 2 + dc, NS:NC], qkT[:, dc, NS:NC]) for dc in range(2)],
                             [("qk", i, 1) for i in range(4)], bsc)
                    tt(scm[0:SC, 0, 0:SC], PS[0:SC, bsc, 0:SC], maskTs[0:SC, h, :], ALU.mult,
                       psk(bsc) + [("cm",)], [("scm", 0)])
                    S.add("pe", lambda pe: pe.matmul(PS[0:SC, SBANK, :], scm[0:SC, 0, 0:SC],
                                                     vtm[0:SC, 4, :], start=True, stop=False),
                          reads=[("scm", 0), ("vtm", 4)], writes=psk(SBANK))
                pending.append(head)

                def sload(i):
                    sl = i % 2
                    src = st_in[i, h].rearrange("(dc p) v -> p dc v", p=128)
                    dma("sp", Ss32[:, sl, :, :], src, [], [("Ss32", sl)], ("ssin", sl))
                    act(Ssb[:, sl, :, :], Ss32[:, sl, :, :], AF.Copy, [("Ss32", sl)], [("Ssb", sl)])

                def cross(i):
                    sl = i % 2
                    lastg = (i == NSAMP - 1)

                    def fn(pe, i=i, sl=sl, lastg=lastg):
                        r = None
                        for dc in range(2):
                            r = pe.matmul(PS[0:SC, SBANK, :], qm[:, dc, i, :], Ssb[:, sl, dc, :],
                                          start=False, stop=(lastg and dc == 1))
                        return r
                    S.add("pe", fn, reads=[("qm", 0), ("qm", 1), ("Ssb", sl)], writes=psk(SBANK), accum=True)

                def one(i):
                    sl = i % 2
                    if i == 0:
                        sload(0)
                    if i >= 1:
                        cross(i - 1)
                    if i + 1 < NSAMP:
                        sload(i + 1)
                    bu = ps2()
                    for dc in range(2):
                        mm_group(PS[:, bu + dc, :], [(km[0:SC, i, dc * 128:(dc + 1) * 128], vtm[0:SC, 4, :])],
                                 [("km",), ("vtm", 4)], bu + dc)
                    stt(Ss32[:, sl, :, :], Ss32[:, sl, :, :], chdec_s[h], PS[:, bu:bu + 2, :],
                        ALU.mult, ALU.add, psk(bu, 2) + [("Ss32", sl)], [("Ss32", sl)])
                    dst = s_s[i, h].rearrange("(dc p) v -> p dc v", p=128)
                    dma("sp", dst, Ss32[:, sl, :, :], [("Ss32", sl)], [("s_s_out", sl)], ("ssout", sl))
                for i in range(NSAMP):
                    pending.append(lambda i=i: one(i))

                def tail():
                    np_ = SC
                    cross(NSAMP - 1)
                    act(junkb[0:np_, 0:512], PS[0:np_, SBANK, :], AF.Square, psk(SBANK), [("junkb",), ("stat", "s0")],
                        accum_out=stat[0:np_, 40:41])
                    act(stat[0:np_, 41:42], stat[0:np_, 40:41], AF.Sqrt, [("stat", "s0")], [("stat", "s1")],
                        scale=1.0 / DV, bias=EPS)
                    recip(stat[0:np_, 42:43], stat[0:np_, 41:42], [("stat", "s1")], [("stat", "s2")])
                    stt(yatm[0:np_, 0, :], PS[0:np_, SBANK, :], stat[0:np_, 42:43], gsm[0:np_, 4, :],
                        ALU.mult, ALU.mult, psk(SBANK) + [("stat", "s2"), ("gs", 4)], [("yatm", 0)])
                    bt = ps1()
                    pst = PS[:, bt, :].bitcast(BF16)
                    tr_multi([(pst[:, j * 128:j * 128 + np_], yatm[0:np_, 0, j * 128:(j + 1) * 128],
                               identb[0:np_, 0:np_]) for j in range(4)],
                             [("yatm", 0), ("identb",)], bt)
                    srcv = pst[:, 0:512].rearrange("p (j n) -> p j n", j=4)[:, :, 0:np_]
                    grb = gret[:, h * 4:(h + 1) * 4].unsqueeze(2).broadcast_to([128, 4, np_])
                    tt(A16[:, h * 4:(h + 1) * 4, NS:NC], srcv, grb, ALU.mult,
                       psk(bt) + [("gcols",)], [("A16", h * 4 + j, 1) for j in range(4)])
                pending.append(tail)

            horder = [(p + i) % H_A for i in range(H_A)]
            BST = os.environ.get("MK_BSTOP", "")
            for hi, h in enumerate(horder):
                ws_ = (hi == 0)
                if hi == 0:
                    F1(h, ws_)
                if BST == "F1":
                    break
                F2(h, ws_)
                if BST == "F2":
                    break
                if ws_ and BST != "nosamp":
                    sample_steps(h)
                if BST == "samp":
                    break
                B1(h)
                if BST == "B1":
                    break
                if hi + 1 < H_A:
                    F1(horder[hi + 1], False)
                B2(h)
                if BST == "B2":
                    break
            while pending:
                pump()
            if p == 0:
                dump("yaT0", A16[:, :, 0:NS], [128, 16, NS], [("A16", k, 0) for k in range(16)], bf=True)

            if STOP == "B":
                break
            S.alias(HB_C_KEYS, HB_B_KEYS)
            s0, k0_ = wslab(w_in_v, 7168)
            s1, k1_ = wslab(w_in_v, 7168 + 512)
            for (c, c0, n) in chk:
                g = 0 if c < 4 else 1
                b = ps2("D")
                for s_, (sl_, sk_) in enumerate(((s0, k0_), (s1, k1_))):
                    mm_group(PS[0:n, b + s_, :], [(nT[:, k, c0:c0 + n], sl_[:, k, :]) for k in range(8)],
                             [("nT", g), sk_], b + s_)
                tsl = c % 2
                tg = T4[0:n, tsl, :]
                act(tg, PS[0:n, b:b + 2, :].rearrange("p b n -> p (b n)"), AF.Gelu, psk(b, 2), t4k(tsl))
                S.add("dve", lambda e, tg=tg, n=n: e.bn_stats(stat[0:n, 8:14], tg[:, 0:512]),
                      reads=t4k(tsl), writes=[("stat", "bs0")])
                S.add("dve", lambda e, tg=tg, n=n: e.bn_stats(stat[0:n, 14:20], tg[:, 512:1024]),
                      reads=t4k(tsl), writes=[("stat", "bs1")])
                S.add("dve", lambda e, n=n: e.bn_aggr(stat[0:n, 20:22], stat[0:n, 8:20]),
                      reads=[("stat", "bs0"), ("stat", "bs1")], writes=[("stat", "mv")])
                act(stat[0:n, 22:23], stat[0:n, 21:22], AF.Sqrt, [("stat", "mv")], [("stat", "sd")], bias=EPS)
                recip(stat[0:n, 23:24], stat[0:n, 22:23], [("stat", "sd")], [("stat", "rs")])
                stt(tg, tg, stat[0:n, 20:21], ggm[0:n, :], ALU.subtract, ALU.mult,
                    t4k(tsl) + [("stat", "mv"), ("ggm",)], t4k(tsl))
                if c < 4:
                    ts(gvtm[0:n, c, :], tg, stat[0:n, 23:24], None, ALU.mult, None,
                       t4k(tsl) + [("stat", "rs")], [("gv", c)])
                else:
                    ts(tg, tg, stat[0:n, 23:24], None, ALU.mult, None,
                       t4k(tsl) + [("stat", "rs")], t4k(tsl))
                    dma("sp", gv_s[:, :], tg, t4k(tsl), [("gv_s_out",)], ("gvout",))
                    dcopy(gvtm[0:n, c, :], tg, t4k(tsl), [("gv", c)])
            S.alias(A8A_KEYS, [("km",)])
            for s_ in range(2):
                sl_, sk_ = wslab(w_in_v, 6144 + s_ * 512)
                for ft in range(4):
                    for (g, c0, n) in grp:
                        b = ps1("D")
                        fm_proj(sl_, sk_, (ft * 128, ft * 128 + 128), g, c0, n, b)
                        act(A8a[:, s_ * 4 + ft, c0:c0 + n], PS[:, b, 0:n], AF.Gelu, psk(b),
                            [("A8a", s_ * 4 + ft, g)])
            S.alias(A8B_KEYS, CSK)
            for (c, c0, n) in chk:
                g = 0 if c < 4 else 1
                b = ps2("D")
                items = []
                for gg in range(8):
                    o_ = PS[:, b + gg // 4, (gg % 4) * 128:(gg % 4) * 128 + n]
                    if c < 4:
                        items.append((o_, gvtm[:, c, gg * 128:(gg + 1) * 128], wsTb[:, gg, :]))
                    else:
                        items.append((o_, gvtm[0:SC, c, gg * 128:(gg + 1) * 128], wsTsb[0:SC, gg, :]))
                mm_multi(items, [("gv", c), ("wsTb",), ("wsTsb",)], b, 2)
                srcv = PS[:, b:b + 2, :].rearrange("p b (j n) -> p (b j) n", j=4)[:, :, 0:n]
                tsl = 2 + c % 2
                tmp = T4[:, tsl, 0:8 * n].rearrange("p (g n) -> p g n", g=8)
                bias_ = bsr[:, :, :] if c < 4 else bsrs[:, :, :]
                tt(tmp, srcv, bias_, ALU.add, psk(b, 2) + [("bsr",), ("bsrs",)], t4k(tsl))
                tt(A8b[:, :, c0:c0 + n], tmp, A8a[:, :, c0:c0 + n], ALU.mult,
                   t4k(tsl) + [("A8a", k, g) for k in range(8)], [("A8b", k, g) for k in range(8)])
            if p == 0:
                dump("ybT0", A8b[:, :, 0:NS], [128, 8, NS], [("A8b", k, 0) for k in range(8)], bf=True)

            if STOP == "C":
                break
            S.alias(HB_D_KEYS, HB_C_KEYS)
            for which, (buf, nm) in enumerate(((sga, "sga"), (sgb, "sgb"))):
                for s_ in range(2):
                    sl_, sk_ = wslab(w_in_v, 8192 + which * 1024 + s_ * 512)
                    for ft in range(4):
                        for (g, c0, n) in grp:
                            b = ps1("D")
                            fm_proj(sl_, sk_, (ft * 128, ft * 128 + 128), g, c0, n, b)
                            act(buf[:, s_ * 4 + ft, c0:c0 + n], PS[:, b, 0:n], AF.Sigmoid, psk(b),
                                [(nm, s_ * 4 + ft, g)])
            for hf in range(2):
                wb_, kb_ = wslab(w_br_b_v, hf * 512)
                for pr in range(2):
                    wa_, ka_ = wslab(w_br_a_v, hf * 512 + pr * 256, ncols=256, k=16)
                    for fl in range(2):
                        f_ = hf * 4 + pr * 2 + fl
                        for (g, c0, n) in grp:
                            b = ps2("D")
                            mm_group(PS[:, b, 0:n],
                                     [(wa_[:, k, fl * 128:(fl + 1) * 128], A16[:, k, c0:c0 + n]) for k in range(16)],
                                     [ka_] + [("A16", k, g) for k in range(16)], b)
                            fcol = (pr * 2 + fl) * 128
                            mm_group(PS[:, b + 1, 0:n],
                                     [(wb_[:, k, fcol:fcol + 128], A8b[:, k, c0:c0 + n]) for k in range(8)],
                                     [kb_] + [("A8b", k, g) for k in range(8)], b + 1)
                            m1 = T4[:, 0, 0:n]; m2 = T4[:, 1, 0:n]
                            tt(m1, PS[:, b, 0:n], sga[:, f_, c0:c0 + n], ALU.mult,
                               psk(b) + [("sga", f_, g)], [("T4", 0)])
                            tt(m2, PS[:, b + 1, 0:n], sgb[:, f_, c0:c0 + n], ALU.mult,
                               psk(b + 1) + [("sgb", f_, g)], [("T4", 1)])
                            tt(A8a[:, f_, c0:c0 + n], m1, m2, ALU.add, [("T4", 0), ("T4", 1)],
                               [("A8a", f_, g)])
            if p == 0:
                dump("mrgT0", A8a[:, :, 0:NS], [128, 8, NS], [("A8a", k, 0) for k in range(8)], bf=True)
            for s_ in range(2):
                sl_, sk_ = wslab(w_o_v, s_ * 512)
                for ft in range(4):
                    f_ = s_ * 4 + ft
                    for (g, c0, n) in grp:
                        b = ps1("D")
                        fm_proj(sl_, sk_, (ft * 128, ft * 128 + 128), g, c0, n, b,
                                rhs_buf=A8a, rkeys=[("A8a", k, g) for k in range(8)])
                        tt(hT[:, f_, c0:c0 + n], PS[:, b, 0:n], hT[:, f_, c0:c0 + n], ALU.add,
                           psk(b) + [("hT", f_, g)], [("hT", f_, g)])
            if p == 0:
                dump("hT1", hT[:, :, 0:NS], [128, 8, NS], hkeys(0))

            if STOP == "D":
                break
            rmsnorm_fm(gmlp, last)
            for ffg in range(2):
                for s_ in range(4):
                    sl_, sk_ = wslab(w_up_v, ffg * 2048 + s_ * 512)
                    for ft in range(4):
                        kk = s_ * 4 + ft
                        for (g, c0, n) in grp:
                            b = ps1("D")
                            fm_proj(sl_, sk_, (ft * 128, ft * 128 + 128), g, c0, n, b)
                            tsl = kk % 2
                            r_ = T4[:, tsl, 0:n]
                            act(r_, PS[:, b, 0:n], AF.Relu, psk(b), [("T4", tsl)])
                            tt(A16[:, kk, c0:c0 + n], r_, r_, ALU.mult, [("T4", tsl)], [("A16", kk, g)])
                for cq in range(4):
                    sl_, sk_ = wslab(w_down_v, cq * 256, ncols=256, k0=ffg * 16, k=16)
                    for fl in range(2):
                        f_ = cq * 2 + fl
                        for (g, c0, n) in grp:
                            b = ps1("D")
                            mm_group(PS[:, b, 0:n],
                                     [(sl_[:, k, fl * 128:(fl + 1) * 128], A16[:, k, c0:c0 + n]) for k in range(16)],
                                     [sk_] + [("A16", k, g) for k in range(16)], b)
                            tt(hT[:, f_, c0:c0 + n], PS[:, b, 0:n], hT[:, f_, c0:c0 + n], ALU.add,
                               psk(b) + [("hT", f_, g)], [("hT", f_, g)])
            if p == 0:
                dump("hT2", hT[:, :, 0:NS], [128, 8, NS], hkeys(0))

            rmsnorm_fm(gple, last)
            for (c, c0, n) in chk:
                slot = c % 2
                src = pp[t0 + c0:t0 + c0 + n, :] if c < 4 else psm[:, :]
                dma("sp", T4[0:n, slot, 0:256], src, [], [("T4", slot)], ("xs", slot))
                b = ps1("D")
                tr_multi([(PS[:, b, dc * 128:dc * 128 + n], T4[0:n, slot, dc * 128:(dc + 1) * 128],
                           identf[0:n, 0:n]) for dc in range(2)], [("T4", slot), ("cm",)], b)
                srcv = PS[:, b, 0:256].rearrange("p (j n) -> p j n", j=2)[:, :, 0:n]
                act(pT[:, :, c0:c0 + n], srcv, AF.Copy, psk(b), [("pT", 0 if c < 4 else 1)])
            if p + 1 < npass:
                for c in range(2):
                    t1_ = (p + 1) * NS + c * 128
                    dma("sp", T4[0:128, c, :], xp[t1_:t1_ + 128, :], [], t4k(c), ("xs", c))
            wpp_, kpp_ = load_slab([((0, 1024), w_pp_v[:, :, :])], (2, 1024))
            for s_ in range(2):
                sl_, sk_ = wslab(w_pg_v, s_ * 512)
                for ft in range(4):
                    f_ = s_ * 4 + ft
                    for (g, c0, n) in grp:
                        b = ps2("D")
                        fm_proj(sl_, sk_, (ft * 128, ft * 128 + 128), g, c0, n, b)
                        mm_group(PS[:, b + 1, 0:n],
                                 [(wpp_[:, k, f_ * 128:(f_ + 1) * 128], pT[:, k, c0:c0 + n]) for k in range(2)],
                                 [kpp_, ("pT", g)], b + 1)
                        sg_ = T4[:, 2, 0:n]
                        act(sg_, PS[:, b, 0:n], AF.Sigmoid, psk(b), [("T4", 2)])
                        tt(sg_, PS[:, b + 1, 0:n], sg_, ALU.mult, psk(b + 1) + [("T4", 2)], [("T4", 2)])
                        tt(hT[:, f_, c0:c0 + n], sg_, hT[:, f_, c0:c0 + n], ALU.add,
                           [("T4", 2), ("hT", f_, g)], [("hT", f_, g)])

            for (c, c0, n) in chk:
                g = 0 if c < 4 else 1
                b = ps2("D")
                tr_multi([(PS[0:n, b + k // 4, (k % 4) * 128:(k % 4 + 1) * 128], hT[:, k, c0:c0 + n], identf)
                          for k in range(8)], hkeys(g) + [("cm",)], b, 2)
                hv = PS[0:n, b:b + 2, :].rearrange("p b n -> p (b n)")
                act(A8b[0:n, 0:2, 0:512], PS[0:n, b:b + 2, :], AF.Square, psk(b, 2),
                    [("A8b", 0, 0), ("A8b", 1, 0), ("stat", "f0")], accum_out=stat[0:n, 32:33])
                act(stat[0:n, 33:34], stat[0:n, 32:33], AF.Sqrt, [("stat", "f0")], [("stat", "f1")],
                    scale=1.0 / D, bias=EPS)
                recip(stat[0:n, 34:35], stat[0:n, 33:34], [("stat", "f1")], [("stat", "f2")])
                slot = 2 + c % 2
                stt(T4[0:n, slot, :], hv, stat[0:n, 34:35], gfin[0:n, :], ALU.mult, ALU.mult,
                    psk(b, 2) + [("stat", "f2"), ("gfin",)], t4k(slot))
                dst = y_p[t0 + c0:t0 + c0 + n, :] if c < 4 else y_s[:, :]
                dma("sp", dst, T4[0:n, slot, :], t4k(slot), [("y_out", slot)], ("yout", slot))

        while pend_st:
            flush_store()
        eng_sems = {}
        for e in ("pe", "act", "dve", "pool"):
            eng_sems[e] = es.enter_context(nc.semaphore("sem_" + e))
        slot_sems = {}
        for i, s_ in enumerate(sorted(S.slot_cnt.keys(), key=str)):
            slot_sems[s_] = es.enter_context(nc.semaphore("dsem_%d" % i))
        final_slots = [s_ for s_ in S.slot_cnt if s_[0] in ("yout", "ssout", "spout", "gvout", "dbg")]
        S.emit(nc, eng_sems, slot_sems, final_slots)
    return nc, dbg_outs


_CACHE = {}


def kernel(x_prompt, x_sample, p_prompt, p_sample, state_ret, g_mix, w_in, g_ret, w_s, b_s,
           g_gm, w_br_a, w_br_b, w_o, g_mlp, w_up, w_down, g_ple, w_pg, w_pp, g_final):
    f = np.float32
    A = lambda a: np.ascontiguousarray(np.asarray(a), dtype=f)
    cosT, sinT, cm, chdec, chdec_s = _host_consts()
    cm_off, cm_w = _cm_layout(cm)
    cmisc = np.concatenate([cm[k] for k in CM_ORDER], axis=1).astype(f)

    npass = DBG_PASSES if DEBUG else NPASS
    nc, dbg = build_program(cm_off, cm_w, chdec, chdec_s, npass=npass)

    def col(gv, n):
        return np.asarray(gv, dtype=f).reshape(n, 128).T
    gcols = np.concatenate([col(g_mix[0], 8), col(g_mlp[0], 8), col(g_ple[0], 8), col(g_ret[0], 16)], axis=1)
    ggm_rep = np.broadcast_to(np.asarray(g_gm[0], dtype=f)[None, :], (128, D))
    gfin_rep = np.broadcast_to(np.asarray(g_final, dtype=f)[None, :], (128, D))
    ws = np.asarray(w_s[0], dtype=f)
    wsT = np.transpose(ws, (2, 0, 1)).reshape(128, 8 * 128)
    w4 = np.transpose(ws[:, :4, :4], (2, 0, 1))
    wsTs = np.zeros((128, 8, SC), f)
    wsTs[:SC] = np.tile(w4, (NSAMP, 1, NSAMP))
    bs = np.asarray(b_s[0], dtype=f)
    bs_rep = np.broadcast_to(bs[None, :, :], (128, 8, 128)).reshape(128, -1)
    bs_rep_s = np.broadcast_to(np.tile(bs[:, :4], (1, NSAMP))[None], (128, 8, SC)).reshape(128, -1)

    shared = {
        "w_in": A(w_in[0]), "w_br_a": A(w_br_a[0]), "w_br_b": A(w_br_b[0]), "w_o": A(w_o[0]),
        "w_up": A(w_up[0]), "w_down": A(w_down[0]), "w_pg": A(w_pg[0]), "w_pp": A(w_pp[0]),
        "gcols": A(gcols), "ggm_rep": A(ggm_rep), "gfin_rep": A(gfin_rep),
        "wsT": A(wsT), "wsTs": A(wsTs.reshape(128, -1)), "bs_rep": A(bs_rep), "bs_rep_s": A(bs_rep_s),
        "cosT": A(cosT), "sinT": A(sinT), "cmisc": A(cmisc),
        "qmask": A(cm["qmask"]), "tril": A(cm["tril"]),
    }
    xpn = np.asarray(x_prompt); xsn = np.asarray(x_sample)
    ppn = np.asarray(p_prompt); psn = np.asarray(p_sample); stn = np.asarray(state_ret)
    in_maps = []
    for c in range(N_CORES):
        m = dict(shared)
        m["xp"] = A(xpn[c]); m["pp"] = A(ppn[0, c])
        m["xs"] = A(xsn[c * NSAMP:(c + 1) * NSAMP].reshape(SC, D))
        m["ps"] = A(psn[0, c * NSAMP:(c + 1) * NSAMP].reshape(SC, 256))
        m["st"] = A(stn[0, c * NSAMP:(c + 1) * NSAMP])
        in_maps.append(m)
    res = run_bass_kernel_spmd(nc, in_maps, core_ids=list(range(N_CORES)))
    R = res.results
    y_prompt = np.stack([R[c]["y_p"] for c in range(N_CORES)], axis=0).astype(f)
    y_sample = np.concatenate([R[c]["y_s"].reshape(NSAMP, 4, D) for c in range(N_CORES)], axis=0).astype(f)
    s_prompt = np.stack([R[c]["s_p"] for c in range(N_CORES)], axis=0)[None].astype(f)
    s_sample = np.concatenate([R[c]["s_s"] for c in range(N_CORES)], axis=0)[None].astype(f)
    gv_sample = np.concatenate([R[c]["gv_s"].reshape(NSAMP, 4, D) for c in range(N_CORES)], axis=0)[None].astype(f)
    if DEBUG:
        _CACHE["dbg"] = {k: [R[c]["dbg_" + k] for c in range(N_CORES)] for k in dbg}
    return (y_prompt, y_sample, s_prompt, s_sample, gv_sample)
```

```python
import os
import math
import numpy as np
import concourse.bass as bass
import concourse.mybir as mybir
from concourse.bass_utils import run_bass_kernel_spmd

F32 = mybir.dt.float32
BF16 = mybir.dt.bfloat16
AF = mybir.ActivationFunctionType
ALU = mybir.AluOpType

N_CORES = 8
D = 1024
SEQ = 2048
NS = 512
NPASS = SEQ // NS
NSAMP = 16
SC = 64
NC = NS + SC
H_A, DK, DV = 4, 256, 512
PAST = 16384
EPS = 1e-6
R_SLAB = 4
SAME_ENGINE_ALL = bool(int(os.environ.get("MK_SEA", "0")))
USE_WSCR = True
DEBUG = bool(int(os.environ.get("MK_DEBUG", "0")))
DBG_PASSES = int(os.environ.get("MK_PASSES", str(NPASS)))
STOP = os.environ.get("MK_STOP", "")


class Op:
    __slots__ = ("eng", "fn", "idx", "waits", "signal", "val", "dma", "slot", "ninc", "small")

    def __init__(self, eng, fn, idx):
        self.eng = eng
        self.fn = fn
        self.idx = idx
        self.waits = []
        self.signal = False
        self.val = None
        self.dma = False
        self.slot = None
        self.ninc = 1
        self.small = False


class Sched:
    ENGS = ("pe", "act", "dve", "pool", "sp")

    def __init__(self):
        self.ops = {e: [] for e in self.ENGS}
        self.state = {}
        self.slot_cnt = {}
        self.last_dma = {}

    def _st(self, k):
        st = self.state.get(k)
        if st is None:
            st = [None, []]
            self.state[k] = st
        return st

    def add(self, eng, fn, reads=(), writes=(), dma_slot=None, ninc=1, accum=False):
        op = Op(eng, fn, len(self.ops[eng]))
        op.small = any(k[0] == "stat" for k in writes)
        deps = []
        if eng == "pe" and not accum:
            for k in writes:
                st = self.state.get(k)
                if k[0] == "ps" and st is not None and st[0] is not None and st[0].eng == "pe" and not st[1]:
                    raise AssertionError("PSUM bank %s overwritten by PE before its result was read" % (k,))
        for k in reads:
            st = self._st(k)
            if st[0] is not None:
                deps.append((st[0], True))
        for k in writes:
            st = self._st(k)
            if st[0] is not None:
                deps.append((st[0], False))
            for r in st[1]:
                deps.append((r, False))
        best = {}
        for d, raw in deps:
            if d.dma or d.eng == eng:
                continue
            b_ = best.get(d.eng)
            if b_ is None or d.idx > b_.idx:
                best[d.eng] = d
        deps = [(d, raw) for d, raw in deps if d.dma or d.eng == eng or best[d.eng] is d]
        seen = set()
        for d, raw in deps:
            if d is op or id(d) in seen:
                continue
            if (not d.dma) and d.eng == eng:
                if eng == "pe":
                    continue
                if not SAME_ENGINE_ALL and not (raw and d.small):
                    continue
            seen.add(id(d))
            op.waits.append(d)
            d.signal = True
        for k in reads:
            self._st(k)[1].append(op)
        for k in writes:
            st = self._st(k)
            st[0] = op
            st[1] = []
        if dma_slot is not None:
            op.dma = True
            op.slot = dma_slot
            op.ninc = ninc
            n = self.slot_cnt.get(dma_slot, 0) + ninc
            self.slot_cnt[dma_slot] = n
            op.val = 16 * n
            self.last_dma[dma_slot] = op
        self.ops[eng].append(op)
        return op

    def alias(self, new_keys, old_keys):
        best = {}
        dmas = {}
        for k in old_keys:
            st = self.state.get(k)
            if st is None:
                continue
            cands = list(st[1])
            if st[0] is not None:
                cands.append(st[0])
            for o in cands:
                if o.dma:
                    dmas[id(o)] = o
                else:
                    b = best.get(o.eng)
                    if b is None or o.idx > b.idx:
                        best[o.eng] = o
            self.state[k] = [None, []]
        fence = list(best.values()) + list(dmas.values())
        for k in new_keys:
            st = self._st(k)
            have = {id(o) for o in st[1]}
            st[1] = list(st[1]) + [o for o in fence if id(o) not in have]

    def emit(self, nc, eng_sems, slot_sems, final_slots):
        for e in ("pe", "act", "dve", "pool"):
            c = 0
            for op in self.ops[e]:
                if op.dma:
                    continue
                if op.signal:
                    c += 1
                    op.val = c
        fin = Op("sp", None, len(self.ops["sp"]))
        for s in final_slots:
            if s in self.last_dma:
                fin.waits.append(self.last_dma[s])
        self.ops["sp"].append(fin)

        def sem_of(d):
            return slot_sems[d.slot] if d.dma else eng_sems[d.eng]

        def make(e):
            def body(eng):
                waited = {}
                for op in self.ops[e]:
                    for d in op.waits:
                        key = ("s", d.slot) if d.dma else ("e", d.eng)
                        if waited.get(key, 0) >= d.val:
                            continue
                        eng.wait_ge(sem_of(d), d.val)
                        waited[key] = d.val
                    if op.fn is None:
                        continue
                    inst = op.fn(eng)
                    if op.dma:
                        insts = inst if isinstance(inst, (list, tuple)) else [inst]
                        assert len(insts) == op.ninc
                        for i_ in insts:
                            i_.then_inc(slot_sems[op.slot], 16)
                    elif op.signal:
                        inst.then_inc(eng_sems[e], 1)
            return body

        with nc.Block() as block:
            block.tensor(make("pe"))
            block.scalar(make("act"))
            block.vector(make("dve"))
            block.gpsimd(make("pool"))
            block.sync(make("sp"))


def _host_consts():
    f = np.float32
    half = DK // 2
    inv = (np.float32(10000.0) ** (-np.arange(half, dtype=f) / np.float32(half))).astype(f)
    pos_p = np.arange(SEQ, dtype=f)
    pos_s = (np.float32(PAST) + np.arange(4, dtype=f)).astype(f)
    pos_s = np.tile(pos_s, NSAMP)
    pos = np.concatenate([pos_p, pos_s]).astype(f)
    ang = (pos[None, :] * inv[:, None]).astype(f)
    cosT = np.cos(ang.astype(np.float64)).astype(f)
    sinT = np.sin(ang.astype(np.float64)).astype(f)

    lg = np.log(1.0 - 2.0 ** (-5.0 - np.arange(H_A, dtype=np.float64)))
    idx = np.arange(128, dtype=np.float64)
    js = (np.arange(SC) % 4).astype(np.float64)
    smp = np.arange(SC) // 4

    cm = {}
    cm["identf"] = np.eye(128, dtype=f)
    cm["tril"] = (idx[:, None] <= idx[None, :]).astype(f)
    maskTs = np.zeros((128, H_A, SC), f)
    qdec = np.zeros((128, H_A, NS), f)
    qdec_s = np.zeros((128, H_A, SC), f)
    stdec = np.zeros((128, H_A * 4), f)
    rowsc = np.zeros((128, H_A * 4), f)
    stdec_s = np.zeros((128, H_A), f)
    tl = np.arange(NS, dtype=np.float64)
    for h in range(H_A):
        ms = ((smp[None, :] == smp[:, None]) & (js[None, :] >= js[:, None])) * \
            np.exp(-(js[:, None] + 1.0) * lg[h]) / 16.0
        maskTs[:SC, h, :] = ms.astype(f)
        qdec[:, h, :] = np.exp((tl[None, :] + 1.0) * lg[h]).astype(f)
        qdec_s[:, h, :] = np.exp((js[None, :] + 1.0) * lg[h]).astype(f)
        for ck in range(4):
            kl = ck * 128 + idx
            stdec[:, h * 4 + ck] = (np.exp((NS - 1.0 - kl) * lg[h]) / 16.0).astype(f)
            rowsc[:, h * 4 + ck] = (np.exp(-(kl + 1.0) * lg[h]) / 16.0).astype(f)
        stdec_s[:SC, h] = (np.exp((3.0 - js) * lg[h]) / 16.0).astype(f)
    cmask = np.ones((128, NS), f)
    cmask[:, :128] = (idx[None, :] >= idx[:, None]).astype(f)
    cm["cmask"] = cmask
    cm["rowsc"] = rowsc
    cm["maskTs"] = maskTs.reshape(128, -1)
    cm["qdec"] = qdec.reshape(128, -1)
    cm["qdec_s"] = qdec_s.reshape(128, -1)
    cm["stdec"] = stdec
    cm["stdec_s"] = stdec_s
    qmask = np.zeros((128, NSAMP, SC), f)
    for i in range(NSAMP):
        qmask[:, i, 4 * i:4 * i + 4] = 1.0
    cm["qmask"] = qmask.reshape(128, -1)
    kmask = np.zeros((128, NSAMP), f)
    for i in range(NSAMP):
        kmask[4 * i:4 * i + 4, i] = 1.0
    cm["kmask"] = kmask
    bms = np.zeros((128, SC), f)
    bms[:SC, :] = ((smp[:, None] == smp[None, :]) & (js[:, None] <= js[None, :])).astype(f)
    cm["bms"] = bms
    chdec = [float(np.exp(float(NS) * lg[h])) for h in range(H_A)]
    chdec_s = [float(np.exp(4.0 * lg[h])) for h in range(H_A)]
    return cosT, sinT, cm, chdec, chdec_s


CM_ORDER = ["identf", "cmask", "rowsc", "maskTs", "qdec", "qdec_s", "stdec", "stdec_s",
            "kmask", "bms"]


def _cm_layout(cm):
    off = {}
    o = 0
    for k in CM_ORDER:
        w = cm[k].shape[1]
        off[k] = (o, w)
        o += w
    return off, o


def build_program(cm_off, cm_w, chdec, chdec_s, npass=NPASS):
    nc = bass.Bass("TRN2", target_bir_lowering=False)
    S = Sched()

    def din(name, shape, dt=F32):
        return nc.dram_tensor(name, list(shape), dt, kind="ExternalInput").ap()

    def dout(name, shape, dt=F32):
        return nc.dram_tensor(name, list(shape), dt, kind="ExternalOutput").ap()

    xp = din("xp", [SEQ, D]); pp = din("pp", [SEQ, 256])
    xs = din("xs", [SC, D]); psm = din("ps", [SC, 256])
    st_in = din("st", [NSAMP, H_A, DK, DV])
    w_in = din("w_in", [D, 10240]); w_br_a = din("w_br_a", [2048, D]); w_br_b = din("w_br_b", [D, D])
    w_o = din("w_o", [D, D]); w_up = din("w_up", [D, 4096]); w_down = din("w_down", [4096, D])
    w_pg = din("w_pg", [D, D]); w_pp = din("w_pp", [256, D])
    gcols_d = din("gcols", [128, 40])
    ggm_d = din("ggm_rep", [128, D]); gfin_d = din("gfin_rep", [128, D])
    wsT_d = din("wsT", [128, 8 * 128]); wsTs_d = din("wsTs", [128, 8 * SC])
    bsr_d = din("bs_rep", [128, 8 * 128]); bsrs_d = din("bs_rep_s", [128, 8 * SC])
    cos_d = din("cosT", [128, SEQ + SC]); sin_d = din("sinT", [128, SEQ + SC])
    cm_d = din("cmisc", [128, cm_w])
    qmask_d = din("qmask", [128, NSAMP * SC])
    tril_d = din("tril", [128, 128])

    y_p = dout("y_p", [SEQ, D]); y_s = dout("y_s", [SC, D])
    s_p = dout("s_p", [H_A, DK, DV]); s_s = dout("s_s", [NSAMP, H_A, DK, DV])
    gv_s = dout("gv_s", [SC, D])

    dbg_outs = {}
    NSLABS = 47
    wscr = nc.dram_tensor("wscr", [NSLABS, 128, 4096], BF16, kind="Internal").ap()

    def kcv(w):
        return w.rearrange("(kc p) f -> p kc f", p=128)

    w_in_v = kcv(w_in); w_br_a_v = kcv(w_br_a); w_br_b_v = kcv(w_br_b); w_o_v = kcv(w_o)
    w_up_v = kcv(w_up); w_down_v = kcv(w_down); w_pg_v = kcv(w_pg); w_pp_v = kcv(w_pp)

    import contextlib
    es = contextlib.ExitStack()
    with es:
        def sb(name, shape, dt):
            return es.enter_context(nc.sbuf_tensor(name, list(shape), dt))

        hT = sb("hT", [128, 8, NC], F32)
        nT = sb("nT", [128, 8, NC], BF16)
        A16 = sb("A16", [128, 16, NC], BF16)
        A8a = sb("A8a", [128, 8, NC], BF16)
        A8b = sb("A8b", [128, 8, NC], BF16)
        HB = sb("HB", [128, 9216], BF16)
        S32 = sb("S32", [128, H_A, 2, 512], F32)
        Sbf = sb("Sbf", [128, 2, 512], BF16)
        Ss32 = sb("Ss32", [128, 2, 2, 512], F32)
        Ssb = sb("Ssb", [128, 2, 2, 512], BF16)
        SL = sb("SL", [128, R_SLAB, 4096], BF16)
        T4 = sb("T4", [128, 4, 1024], F32)
        pT = sb("pT", [128, 2, NC], BF16)
        qm = sb("qm", [128, 2, NSAMP, SC], BF16)
        scm = sb("scm", [128, 4, 512], BF16)
        yatm = sb("yatm", [128, 4, 512], BF16)
        junkb = sb("junkb", [128, 512], BF16)
        stat = sb("stat", [128, 64], F32)
        cmisc = sb("cmisc_sb", [128, cm_w], F32)
        identb = sb("identb", [128, 128], BF16)
        onesb = sb("onesb", [128, 128], BF16)
        qmaskb = sb("qmaskb", [128, NSAMP, SC], BF16)
        gcols = sb("gcols_sb", [128, 40], F32)
        ggm = sb("ggm_sb", [128, D], F32)
        gfin = sb("gfin_sb", [128, D], F32)
        wsTb = sb("wsTb", [128, 8, 128], BF16)
        wsTsb = sb("wsTsb", [128, 8, SC], BF16)
        bsr = sb("bsr", [128, 8, 128], F32)
        bsrs = sb("bsrs", [128, 8, SC], F32)
        PS = es.enter_context(nc.psum_tensor("PS", [128, 8, 512], F32))

        def cmv(name):
            o, w = cm_off[name]
            return cmisc[:, o:o + w]

        identf = cmv("identf")
        cmask = cmv("cmask")
        rowsc = cmv("rowsc")
        maskTs = cmv("maskTs").rearrange("p (h n) -> p h n", h=H_A)
        qdec = cmv("qdec").rearrange("p (h n) -> p h n", h=H_A)
        qdec_s = cmv("qdec_s").rearrange("p (h n) -> p h n", h=H_A)
        stdec = cmv("stdec"); stdec_s = cmv("stdec_s")
        kmask = cmv("kmask")

        cosv = A8b[:, 0:4, :].bitcast(F32)
        A8b_f = A8b[:, :, :].rearrange("p k n -> p (k n)").bitcast(F32)
        cosT = A8b_f[:, 0:NC]
        sinT = A8b_f[:, NC:2 * NC]
        qkT = HB[:, 0:4 * NC].rearrange("p (k n) -> p k n", k=4)
        o1 = 4 * NC
        ktm = HB[:, o1:o1 + 5 * 256].rearrange("p (c n) -> p c n", c=5)
        o2 = o1 + 5 * 256
        vtm = HB[:, o2:o2 + 5 * 512].rearrange("p (c n) -> p c n", c=5)
        o3 = o2 + 5 * 512
        gsm = HB[:, o3:o3 + 5 * 512].rearrange("p (c n) -> p c n", c=5)
        assert o3 + 5 * 512 <= 9216
        gvtm = HB[:, 0:5 * 1024].rearrange("p (c n) -> p c n", c=5)
        sga = HB[:, 0:8 * NC].rearrange("p (k n) -> p k n", k=8)
        sgb = HB[:, 8 * NC:16 * NC].rearrange("p (k n) -> p k n", k=8)
        km = A8a[:, :, :].rearrange("p k n -> p (k n)")[:, 0:NSAMP * 256].rearrange("p (i n) -> p i n", i=NSAMP)

        HB_B_KEYS = [("qk", i, g) for i in range(4) for g in range(2)] + \
                    [(n, c) for n in ("ktm", "vtm", "gs") for c in range(5)]
        HB_C_KEYS = [("gv", c) for c in range(5)]
        HB_D_KEYS = [(n, f_, g) for n in ("sga", "sgb") for f_ in range(8) for g in range(2)]
        A8A_KEYS = [("A8a", k, g) for k in range(8) for g in range(2)]
        A8B_KEYS = [("A8b", k, g) for k in range(8) for g in range(2)]

        ring_cnt = {"s": 0, "p": 0}
        SBANK = 7

        def ps1(ring=None):
            b = ring_cnt["s"] % 7
            ring_cnt["s"] += 1
            return b

        def ps2(ring=None):
            b = 2 * (ring_cnt["p"] % 3)
            ring_cnt["p"] += 1
            return b

        def psk(b, n=1):
            return [("ps", b + i) for i in range(n)]

        def NTK(g):
            return [("nT", k_, g) for k_ in range(8)]

        fine_next = [False]

        def t4k(sl):
            return [("T4", sl), ("T4", sl, "b")]

        def mm_group(out_ap, pairs, reads, bank, nb=1, transpose=False, ident=None):
            pairs = list(pairs)

            def fn(pe, pairs=pairs, out_ap=out_ap):
                last = None
                n = len(pairs)
                for i, (l, r) in enumerate(pairs):
                    last = pe.matmul(out_ap, l, r, start=(i == 0), stop=(i == n - 1))
                return last
            return S.add("pe", fn, reads=reads, writes=psk(bank, nb))

        def mm_multi(items, reads, bank, nb=1):
            items = list(items)

            def fn(pe, items=items):
                last = None
                for (o, l, r) in items:
                    last = pe.matmul(o, l, r, start=True, stop=True)
                return last
            return S.add("pe", fn, reads=reads, writes=psk(bank, nb))

        def tr_multi(items, reads, bank, nb=1):
            items = list(items)

            def fn(pe, items=items):
                last = None
                for (o, i_, idn) in items:
                    last = pe.transpose(o, i_, idn)
                return last
            return S.add("pe", fn, reads=reads, writes=psk(bank, nb))

        def act(out, in_, func, reads, writes, **kw):
            def fn(e, out=out, in_=in_, func=func, kw=kw):
                return e.activation(out, in_, func, **kw)
            return S.add("act", fn, reads=reads, writes=writes)

        def tt(out, in0, in1, op, reads, writes):
            def fn(e, out=out, in0=in0, in1=in1, op=op):
                return e.tensor_tensor(out, in0, in1, op)
            return S.add("dve", fn, reads=reads, writes=writes)

        def stt(out, in0, scalar, in1, op0, op1, reads, writes):
            def fn(e, out=out, in0=in0, scalar=scalar, in1=in1, op0=op0, op1=op1):
                return e.scalar_tensor_tensor(out, in0, scalar, in1, op0, op1)
            return S.add("dve", fn, reads=reads, writes=writes)

        def ts(out, in0, s1, s2, op0, op1, reads, writes):
            def fn(e, out=out, in0=in0, s1=s1, s2=s2, op0=op0, op1=op1):
                if op1 is None:
                    return e.tensor_scalar(out, in0, s1, None, op0)
                return e.tensor_scalar(out, in0, s1, s2, op0, op1)
            return S.add("dve", fn, reads=reads, writes=writes)

        def recip(out, in_, reads, writes):
            def fn(e, out=out, in_=in_):
                return e.reciprocal(out, in_)
            return S.add("dve", fn, reads=reads, writes=writes)

        def dcopy(out, in_, reads, writes):
            def fn(e, out=out, in_=in_):
                return e.tensor_copy(out, in_)
            return S.add("dve", fn, reads=reads, writes=writes)

        def dma(q, out, in_, reads, writes, slot):
            def fn(e, out=out, in_=in_):
                return e.dma_start(out=out, in_=in_)
            return S.add(q, fn, reads=reads, writes=writes, dma_slot=slot)

        def dump(name, ap, shape, reads, bf=False):
            if not DEBUG:
                return
            d = dout("dbg_" + name, shape)
            dbg_outs[name] = shape
            dma("pool" if bf else "sp", d, ap, reads, [("dbg", name)], ("dbg", name))

        slab_cnt = [0]
        slab_ids = {}
        cur_pass = [0]

        pend_st = []

        def flush_store():
            idx_, slot_, sz_ = pend_st.pop(0)
            dma("pool", wscr[idx_, :, 0:sz_], SL[:, slot_, 0:sz_], [("slab", slot_)], [("wscr", idx_)],
                ("wst", slot_))

        def load_slab(pieces, shape, sid=None):
            slot = slab_cnt[0] % R_SLAB
            slab_cnt[0] += 1
            k, n = shape
            while any(ps_[1] == slot for ps_ in pend_st):
                flush_store()
            view = SL[:, slot, 0:k * n].rearrange("p (k n) -> p k n", k=k)
            if sid is None:
                sid = tuple((a, b, str(src.tensor.name), str(src.offset), str(src.ap)) for (a, b), src in pieces)
            first = sid not in slab_ids
            if first:
                slab_ids[sid] = len(slab_ids)
                assert len(slab_ids) <= NSLABS
            idx = slab_ids[sid]
            if first or not USE_WSCR:
                def fn(e, pieces=pieces, view=view):
                    insts = []
                    for (a, b), src in pieces:
                        insts.append(e.dma_start(out=view[:, :, a:b], in_=src))
                    return insts
                S.add("pool", fn, reads=[], writes=[("slab", slot)], dma_slot=("slab", slot), ninc=len(pieces))
                if USE_WSCR and npass > 1:
                    pend_st.append((idx, slot, k * n))
                    while len(pend_st) > 2:
                        flush_store()
            else:
                dma("pool", SL[:, slot, 0:k * n], wscr[idx, :, 0:k * n], [("wscr", idx)], [("slab", slot)],
                    ("slab", slot))
            return view, ("slab", slot)

        def wslab(wv, c0, ncols=512, k0=0, k=8):
            return load_slab([((0, ncols), wv[:, k0:k0 + k, c0:c0 + ncols])], (k, ncols))

        dma("sp", cmisc[:, :], cm_d[:, :], [], [("cm",)], ("c0",))
        dma("sp", gcols[:, :], gcols_d[:, :], [], [("gcols",)], ("c1",))
        dma("sp", ggm[:, :], ggm_d[:, :], [], [("ggm",)], ("c2",))
        dma("sp", gfin[:, :], gfin_d[:, :], [], [("gfin",)], ("c3",))
        dma("sp", bsr[:, :, :], bsr_d.rearrange("p (g n) -> p g n", g=8), [], [("bsr",)], ("c4",))
        dma("sp", bsrs[:, :, :], bsrs_d.rearrange("p (g n) -> p g n", g=8), [], [("bsrs",)], ("c5",))
        T4f = T4[:, :, :].rearrange("p a b -> p (a b)")
        dma("sp", T4[:, 0, :], wsT_d[:, :], [], t4k(0), ("c6",))
        dma("sp", T4[:, 1, 0:8 * SC], wsTs_d[:, :], [], [("T4", 1)], ("c7",))
        dcopy(identb[:, :], identf, [("cm",)], [("identb",)])
        S.add("dve", lambda e: e.memset(onesb[:, :], 1.0), reads=[], writes=[("onesb",)])
        dma("pool", qmaskb[:, :, :], qmask_d.rearrange("p (i n) -> p i n", i=NSAMP), [], [("qmaskb",)], ("c8",))
        dma("sp", T4[:, 2, 0:128], tril_d[:, :], [], [("T4", 2)], ("c9",))
        tril = T4[:, 2, 0:128]
        tt(wsTb[:, :, :], T4[:, 0, :].rearrange("p (g n) -> p g n", g=8),
           tril.unsqueeze(1).broadcast_to([128, 8, 128]), ALU.mult,
           t4k(0) + [("T4", 2)], [("wsTb",)])
        tt(wsTsb[:, :, :], T4[:, 1, 0:8 * SC].rearrange("p (g n) -> p g n", g=8),
           cmv("bms").unsqueeze(1).broadcast_to([128, 8, SC]), ALU.mult,
           [("T4", 1), ("cm",)], [("wsTsb",)])

        gmix = gcols[:, 0:8]; gmlp = gcols[:, 8:16]; gple = gcols[:, 16:24]; gret = gcols[:, 24:40]

        def groups(last):
            g = [(0, 0, NS)]
            if last:
                g.append((1, NS, SC))
            return g

        def chunks(last):
            cs = [(c, c * 128, 128) for c in range(NS // 128)]
            if last:
                cs.append((4, NS, SC))
            return cs

        def hkeys(g):
            return [("hT", k, g) for k in range(8)]

        def rmsnorm_fm(gcol, last):
            for (g, c0, n) in groups(last):
                b = ps1("D")
                for k in range(8):
                    act(A16[:, k, c0:c0 + n], hT[:, k, c0:c0 + n], AF.Square, [("hT", k, g)], [("A16", k, g)])
                    S.add("pe", lambda pe, k=k, b=b, c0=c0, n=n: pe.matmul(
                        PS[:, b, 0:n], onesb[:, :], A16[:, k, c0:c0 + n], start=(k == 0), stop=(k == 7)),
                        reads=[("A16", k, g), ("onesb",)], writes=psk(b), accum=(k > 0))
                act(T4[:, 3, 0:n], PS[:, b, 0:n], AF.Sqrt, psk(b), [("T4", 3)], scale=1.0 / D, bias=EPS)
                recip(T4[:, 3, 512:512 + n], T4[:, 3, 0:n], [("T4", 3)], [("T4", 3, "b")])
                for k in range(8):
                    stt(nT[:, k, c0:c0 + n], hT[:, k, c0:c0 + n], gcol[:, k:k + 1], T4[:, 3, 512:512 + n],
                        ALU.mult, ALU.mult, [("hT", k, g), ("T4", 3, "b"), ("gcols",)], [("nT", k, g)])
                if g == 0:
                    fine_next[0] = True

        def fm_proj(slab, skey, ft_cols, g, c0, n, bank, rhs_buf=None, rkeys=None, K=8):
            rb = nT if rhs_buf is None else rhs_buf
            rk = NTK(g) if rkeys is None else rkeys
            a, b_ = ft_cols
            if fine_next[0] and rhs_buf is None and g == 0:
                fine_next[0] = False
                op = None
                for k in range(K):
                    op = S.add("pe", lambda pe, k=k: pe.matmul(
                        PS[:, bank, 0:n], slab[:, k, a:b_], rb[:, k, c0:c0 + n], start=(k == 0), stop=(k == K - 1)),
                        reads=[skey, rk[k]], writes=psk(bank), accum=(k > 0))
                return op
            return mm_group(PS[:, bank, 0:n], [(slab[:, k, a:b_], rb[:, k, c0:c0 + n]) for k in range(K)],
                            [skey] + rk, bank)

        for p in range(npass):
            if STOP == "setup":
                break
            last = (p == NPASS - 1)
            while pend_st:
                flush_store()
            t0 = p * NS
            grp = groups(last)
            chk = chunks(last)

            S.alias([("cs",)], A8B_KEYS)
            dma("sp", cosT[:, 0:NS], cos_d[:, t0:t0 + NS], [], [("cs",)], ("cs0",))
            dma("sp", sinT[:, 0:NS], sin_d[:, t0:t0 + NS], [], [("cs", 1)], ("cs1",))
            dma("sp", cosT[:, NS:NC], cos_d[:, SEQ:SEQ + SC], [], [("cs", 2)], ("cs2",))
            dma("sp", sinT[:, NS:NC], sin_d[:, SEQ:SEQ + SC], [], [("cs", 3)], ("cs3",))
            CSK = [("cs",), ("cs", 1), ("cs", 2), ("cs", 3)]
            for (c, c0, n) in chunks(p == 0):
                g = 0 if c < 4 else 1
                slot = c % 2
                src = xp[t0 + c0:t0 + c0 + n, :] if c < 4 else xs[:, :]
                if not (p > 0 and c < 2):
                    dma("sp", T4[0:n, slot, :], src, [], t4k(slot), ("xs", slot))
                b = ps2("D")
                tr_multi([(PS[:, b + k // 4, (k % 4) * 128:(k % 4) * 128 + n],
                           T4[0:n, slot, k * 128:(k + 1) * 128], identf[0:n, 0:n]) for k in range(8)],
                         t4k(slot) + [("cm",)], b, 2)
                src_ps = PS[:, b:b + 2, :].rearrange("p b (j n) -> p (b j) n", j=4)[:, :, 0:n]
                act(hT[:, :, c0:c0 + n], src_ps, AF.Copy, psk(b, 2), hkeys(g))
            rmsnorm_fm(gmix, p == 0)
            if p == 0:
                dump("hT0", hT[:, :, 0:NS], [128, 8, NS], hkeys(0))
                dump("nT0", nT[:, :, 0:NS], [128, 8, NS], NTK(0), bf=True)

            if STOP == "A":
                break
            S.alias(HB_B_KEYS, HB_D_KEYS + HB_C_KEYS)
            S.alias([("km",)], A8A_KEYS)
            from collections import deque
            pending = deque()

            def pump(k=1):
                for _ in range(k):
                    if pending:
                        pending.popleft()()

            slabs = {}

            def F1(h, with_s):
                qc0 = h * 256
                kc0 = 1024 + h * 256
                slab_qk, kqk = load_slab([((0, 256), w_in_v[:, :, qc0:qc0 + 256]),
                                          ((256, 512), w_in_v[:, :, kc0:kc0 + 256])], (8, 512))
                slabs[h] = (wslab(w_in_v, 2048 + h * 512), wslab(w_in_v, 4096 + h * 512))
                for (g, c0, n) in groups(with_s):
                    for which in range(2):
                        b = ps2()
                        fm_proj(slab_qk, kqk, (which * 256, which * 256 + 128), g, c0, n, b)
                        fm_proj(slab_qk, kqk, (which * 256 + 128, which * 256 + 256), g, c0, n, b + 1)
                        x1 = PS[:, b, 0:n]; x2 = PS[:, b + 1, 0:n]
                        cs_ = cosT[:, c0:c0 + n]; sn_ = sinT[:, c0:c0 + n]
                        t1 = T4[:, 0, 0:n]; t2 = T4[:, 0, 512:512 + n]; t3 = T4[:, 1, 0:n]; t4 = T4[:, 1, 512:512 + n]
                        tt(t1, x1, cs_, ALU.mult, psk(b) + CSK, [("T4", 0)])
                        tt(t2, x2, sn_, ALU.mult, psk(b + 1) + CSK, [("T4", 0, "b")])
                        tt(t3, x2, cs_, ALU.mult, psk(b + 1) + CSK, [("T4", 1)])
                        tt(t4, x1, sn_, ALU.mult, psk(b) + CSK, [("T4", 1, "b")])
                        if which == 0:
                            tt(t1, t1, t2, ALU.subtract, [("T4", 0), ("T4", 0, "b")], [("T4", 0)])
                            tt(t3, t3, t4, ALU.add, [("T4", 1), ("T4", 1, "b")], [("T4", 1)])
                            dq = qdec[:, h, :] if g == 0 else qdec_s[:, h, :]
                            tt(qkT[:, 0, c0:c0 + n], t1, dq, ALU.mult, [("T4", 0), ("cm",)], [("qk", 0, g)])
                            tt(qkT[:, 1, c0:c0 + n], t3, dq, ALU.mult, [("T4", 1), ("cm",)], [("qk", 1, g)])
                        else:
                            tt(qkT[:, 2, c0:c0 + n], t1, t2, ALU.subtract,
                               [("T4", 0), ("T4", 0, "b")], [("qk", 2, g)])
                            tt(qkT[:, 3, c0:c0 + n], t3, t4, ALU.add,
                               [("T4", 1), ("T4", 1, "b")], [("qk", 3, g)])
                        pump()

            def F2(h, with_s):
                (slab_v, kv), (slab_g, kg) = slabs[h]
                for (c, c0, n) in chunks(with_s):
                    g = 0 if c < 4 else 1
                    bv = ps1()
                    mm_group(PS[0:n, bv, :], [(nT[:, k, c0:c0 + n], slab_v[:, k, :]) for k in range(8)],
                             NTK(g) + [kv], bv)
                    act(vtm[0:n, c, :], PS[0:n, bv, :], AF.Copy, psk(bv), [("vtm", c)])
                    pump()
                    bg = ps1()
                    mm_group(PS[0:n, bg, :], [(nT[:, k, c0:c0 + n], slab_g[:, k, :]) for k in range(8)],
                             NTK(g) + [kg], bg)
                    act(gsm[0:n, c, :], PS[0:n, bg, :], AF.Silu, psk(bg), [("gs", c)])
                    pump()
                b = ps1()
                pst = PS[:, b, :].bitcast(BF16)
                tr_multi([(pst[:, c * 256 + dc * 128:c * 256 + (dc + 1) * 128], qkT[:, 2 + dc, c * 128:(c + 1) * 128],
                           identb[:, :]) for c in range(4) for dc in range(2)],
                         [("qk", 2, 0), ("qk", 3, 0), ("identb",)], b)
                sdb = stdec[:, h * 4:(h + 1) * 4].unsqueeze(2).broadcast_to([128, 4, 256])
                tt(ktm[:, 0:4, :], pst[:, 0:1024].rearrange("p (c n) -> p c n", c=4), sdb, ALU.mult,
                   psk(b) + [("cm",)], [("ktm", c) for c in range(4)])
                if with_s:
                    b = ps1()
                    pst = PS[:, b, :].bitcast(BF16)
                    tr_multi([(pst[0:SC, dc * 128:(dc + 1) * 128], qkT[:, 2 + dc, NS:NC], identb[:, :])
                              for dc in range(2)], [("qk", 2, 1), ("qk", 3, 1), ("identb",)], b)
                    act(ktm[0:SC, 4, :], pst[0:SC, 0:256], AF.Copy, psk(b) + [("cm",)], [("ktm", 4)],
                        scale=stdec_s[0:SC, h:h + 1])

            def B1(h):
                if p > 0:
                    act(Sbf[:, :, :], S32[:, h, :, :], AF.Copy, [("S32", h)], [("Sbf",)])
                QK = [("qk", i, 0) for i in range(4)]
                sb_ = []
                for ck in range(4):
                    nq = NS - ck * 128
                    bsc = ps1()
                    sb_.append(bsc)
                    mm_group(PS[:, bsc, 0:nq],
                             [(qkT[:, 2 + dc, ck * 128:(ck + 1) * 128], qkT[:, dc, ck * 128:NS]) for dc in range(2)],
                             QK, bsc)
                LV = int(os.environ.get("MK_B1", "99"))
                if LV < 1:
                    return
                for ck in range(4):
                    nq = NS - ck * 128
                    stt(scm[:, ck, 0:nq], PS[:, sb_[ck], 0:nq], rowsc[:, h * 4 + ck:h * 4 + ck + 1], cmask[:, 0:nq],
                        ALU.mult, ALU.mult, psk(sb_[ck]) + [("cm",)], [("scm", ck)])
                if LV < 2:
                    return
                bu = ps2()
                for dc in range(2):
                    mm_group(PS[:, bu + dc, :],
                             [(ktm[:, c, dc * 128:(dc + 1) * 128], vtm[:, c, :]) for c in range(4)],
                             [("ktm", c) for c in range(4)] + [("vtm", c) for c in range(4)], bu + dc)
                Sv = S32[:, h, :, :]
                Uv = PS[:, bu:bu + 2, :]
                if p == 0:
                    dcopy(Sv, Uv, psk(bu, 2), [("S32", h)])
                else:
                    stt(Sv, Sv, chdec[h], Uv, ALU.mult, ALU.add, psk(bu, 2) + [("S32", h)], [("S32", h)])
                if LV < 3:
                    return
                ob = []
                for cq in range(4):
                    bo = ps1()
                    ob.append(bo)
                    pairs = [(scm[:, ck, (cq - ck) * 128:(cq - ck + 1) * 128], vtm[:, ck, :]) for ck in range(cq + 1)]
                    rk = [("scm", ck) for ck in range(cq + 1)] + [("vtm", ck) for ck in range(cq + 1)]
                    if p > 0:
                        pairs += [(qkT[:, dc, cq * 128:(cq + 1) * 128], Sbf[:, dc, :]) for dc in range(2)]
                        rk += [("qk", 0, 0), ("qk", 1, 0), ("Sbf",)]
                    mm_group(PS[:, bo, :], pairs, rk, bo)
                if LV < 5:
                    return
                for cq in range(4):
                    act(junkb[:, 0:512], PS[:, ob[cq], :], AF.Square, psk(ob[cq]), [("junkb",), ("stat", 0, cq)],
                        accum_out=stat[:, cq:cq + 1])
                act(stat[:, 4:8], stat[:, 0:4], AF.Sqrt, [("stat", 0, cq) for cq in range(4)], [("stat", 1)],
                    scale=1.0 / DV, bias=EPS)
                recip(stat[:, 8:12], stat[:, 4:8], [("stat", 1)], [("stat", 2)])
                for cq in range(4):
                    stt(yatm[:, cq, :], PS[:, ob[cq], :], stat[:, 8 + cq:9 + cq], gsm[:, cq, :],
                        ALU.mult, ALU.mult, psk(ob[cq]) + [("stat", 2), ("gs", cq)], [("yatm", cq)])

            def B2(h):
                bt = ps2()
                pst = PS[:, bt:bt + 2, :].rearrange("p b n -> p (b n)").bitcast(BF16)
                pv = pst.rearrange("p (b j c n) -> p b j c n", b=2, j=4, c=4)
                items = []
                for j in range(4):
                    bb = PS[:, bt + j // 2, :].bitcast(BF16)
                    for cq in range(4):
                        o0 = (j % 2) * 512 + cq * 128
                        items.append((bb[:, o0:o0 + 128], yatm[:, cq, j * 128:(j + 1) * 128], identb[:, :]))
                tr_multi(items, [("yatm", cq) for cq in range(4)] + [("identb",)], bt, 2)
                for half in range(2):
                    bb = PS[:, bt + half, :].bitcast(BF16)
                    srcv = bb[:, 0:1024].rearrange("p (j n) -> p j n", j=2)
                    j0 = h * 4 + half * 2
                    grb = gret[:, j0:j0 + 2].unsqueeze(2).broadcast_to([128, 2, NS])
                    tt(A16[:, j0:j0 + 2, 0:NS], srcv, grb, ALU.mult,
                       psk(bt + half) + [("gcols",)], [("A16", j0 + j, 0) for j in range(2)])
                if last:
                    dst = s_p[h].rearrange("(dc p) v -> p dc v", p=128)
                    dma("sp", dst, S32[:, h, :, :], [("S32", h)], [("s_p_out",)], ("spout",))

            def sample_steps(h):
                def ptt(out, in0, in1, op, reads, writes):
                    S.add("pool", lambda e, out=out, in0=in0, in1=in1, op=op: e.tensor_tensor(out, in0, in1, op),
                          reads=reads, writes=writes)
                for dc in range(2):
                    ptt(qm[:, dc, :, :], qkT[:, dc, NS:NC].unsqueeze(1).broadcast_to([128, NSAMP, SC]),
                        qmaskb[:, :, :], ALU.mult, [("qk", dc, 1), ("qmaskb",)], [("qm", dc)])
                ptt(km[0:SC, :, :], ktm[0:SC, 4, :].unsqueeze(1).broadcast_to([SC, NSAMP, 256]),
                    kmask[0:SC, :].unsqueeze(2).broadcast_to([SC, NSAMP, 256]), ALU.mult,
                    [("ktm", 4), ("cm",)], [("km",)])

                def head():
                    bsc = ps1()
                    mm_group(PS[0:SC, bsc, 0:SC],
                             [(qkT[:, 2 + dc, NS:NC], qkT[:, dc, NS:NC]) for dc in range(2)],
                             [("qk", i, 1) for i in range(4)], bsc)
                    tt(scm[0:SC, 0, 0:SC], PS[0:SC, bsc, 0:SC], maskTs[0:SC, h, :], ALU.mult,
                       psk(bsc) + [("cm",)], [("scm", 0)])
                    S.add("pe", lambda pe: pe.matmul(PS[0:SC, SBANK, :], scm[0:SC, 0, 0:SC],
                                                     vtm[0:SC, 4, :], start=True, stop=False),
                          reads=[("scm", 0), ("vtm", 4)], writes=psk(SBANK))
                pending.append(head)

                def sload(i):
                    sl = i % 2
                    src = st_in[i, h].rearrange("(dc p) v -> p dc v", p=128)
                    dma("sp", Ss32[:, sl, :, :], src, [], [("Ss32", sl)], ("ssin", sl))
                    act(Ssb[:, sl, :, :], Ss32[:, sl, :, :], AF.Copy, [("Ss32", sl)], [("Ssb", sl)])

                def cross(i):
                    sl = i % 2
                    lastg = (i == NSAMP - 1)

                    def fn(pe, i=i, sl=sl, lastg=lastg):
                        r = None
                        for dc in range(2):
                            r = pe.matmul(PS[0:SC, SBANK, :], qm[:, dc, i, :], Ssb[:, sl, dc, :],
                                          start=False, stop=(lastg and dc == 1))
                        return r
                    S.add("pe", fn, reads=[("qm", 0), ("qm", 1), ("Ssb", sl)], writes=psk(SBANK), accum=True)

                def one(i):
                    sl = i % 2
                    if i == 0:
                        sload(0)
                    if i >= 1:
                        cross(i - 1)
                    if i + 1 < NSAMP:
                        sload(i + 1)
                    bu = ps2()
                    for dc in range(2):
                        mm_group(PS[:, bu + dc, :], [(km[0:SC, i, dc * 128:(dc + 1) * 128], vtm[0:SC, 4, :])],
                                 [("km",), ("vtm", 4)], bu + dc)
                    stt(Ss32[:, sl, :, :], Ss32[:, sl, :, :], chdec_s[h], PS[:, bu:bu + 2, :],
                        ALU.mult, ALU.add, psk(bu, 2) + [("Ss32", sl)], [("Ss32", sl)])
                    dst = s_s[i, h].rearrange("(dc p) v -> p dc v", p=128)
                    dma("sp", dst, Ss32[:, sl, :, :], [("Ss32", sl)], [("s_s_out", sl)], ("ssout", sl))
                for i in range(NSAMP):
                    pending.append(lambda i=i: one(i))

                def tail():
                    np_ = SC
                    cross(NSAMP - 1)
                    act(junkb[0:np_, 0:512], PS[0:np_, SBANK, :], AF.Square, psk(SBANK), [("junkb",), ("stat", "s0")],
                        accum_out=stat[0:np_, 40:41])
                    act(stat[0:np_, 41:42], stat[0:np_, 40:41], AF.Sqrt, [("stat", "s0")], [("stat", "s1")],
                        scale=1.0 / DV, bias=EPS)
                    recip(stat[0:np_, 42:43], stat[0:np_, 41:42], [("stat", "s1")], [("stat", "s2")])
                    stt(yatm[0:np_, 0, :], PS[0:np_, SBANK, :], stat[0:np_, 42:43], gsm[0:np_, 4, :],
                        ALU.mult, ALU.mult, psk(SBANK) + [("stat", "s2"), ("gs", 4)], [("yatm", 0)])
                    bt = ps1()
                    pst = PS[:, bt, :].bitcast(BF16)
                    tr_multi([(pst[:, j * 128:j * 128 + np_], yatm[0:np_, 0, j * 128:(j + 1) * 128],
                               identb[0:np_, 0:np_]) for j in range(4)],
                             [("yatm", 0), ("identb",)], bt)
                    srcv = pst[:, 0:512].rearrange("p (j n) -> p j n", j=4)[:, :, 0:np_]
                    grb = gret[:, h * 4:(h + 1) * 4].unsqueeze(2).broadcast_to([128, 4, np_])
                    tt(A16[:, h * 4:(h + 1) * 4, NS:NC], srcv, grb, ALU.mult,
                       psk(bt) + [("gcols",)], [("A16", h * 4 + j, 1) for j in range(4)])
                pending.append(tail)

            horder = [(p + i) % H_A for i in range(H_A)]
            BST = os.environ.get("MK_BSTOP", "")
            for hi, h in enumerate(horder):
                ws_ = (hi == 0)
                if hi == 0:
                    F1(h, ws_)
                if BST == "F1":
                    break
                F2(h, ws_)
                if BST == "F2":
                    break
                if ws_ and BST != "nosamp":
                    sample_steps(h)
                if BST == "samp":
                    break
                B1(h)
                if BST == "B1":
                    break
                if hi + 1 < H_A:
                    F1(horder[hi + 1], False)
                B2(h)
                if BST == "B2":
                    break
            while pending:
                pump()
            if p == 0:
                dump("yaT0", A16[:, :, 0:NS], [128, 16, NS], [("A16", k, 0) for k in range(16)], bf=True)

            if STOP == "B":
                break
            S.alias(HB_C_KEYS, HB_B_KEYS)
            s0, k0_ = wslab(w_in_v, 7168)
            s1, k1_ = wslab(w_in_v, 7168 + 512)
            for (c, c0, n) in chk:
                g = 0 if c < 4 else 1
                b = ps2("D")
                for s_, (sl_, sk_) in enumerate(((s0, k0_), (s1, k1_))):
                    mm_group(PS[0:n, b + s_, :], [(nT[:, k, c0:c0 + n], sl_[:, k, :]) for k in range(8)],
                             NTK(g) + [sk_], b + s_)
                tsl = c % 2
                tg = T4[0:n, tsl, :]
                act(tg, PS[0:n, b:b + 2, :].rearrange("p b n -> p (b n)"), AF.Gelu, psk(b, 2), t4k(tsl))
                S.add("dve", lambda e, tg=tg, n=n: e.bn_stats(stat[0:n, 8:14], tg[:, 0:512]),
                      reads=t4k(tsl), writes=[("stat", "bs0")])
                S.add("dve", lambda e, tg=tg, n=n: e.bn_stats(stat[0:n, 14:20], tg[:, 512:1024]),
                      reads=t4k(tsl), writes=[("stat", "bs1")])
                S.add("dve", lambda e, n=n: e.bn_aggr(stat[0:n, 20:22], stat[0:n, 8:20]),
                      reads=[("stat", "bs0"), ("stat", "bs1")], writes=[("stat", "mv")])
                act(stat[0:n, 22:23], stat[0:n, 21:22], AF.Sqrt, [("stat", "mv")], [("stat", "sd")], bias=EPS)
                recip(stat[0:n, 23:24], stat[0:n, 22:23], [("stat", "sd")], [("stat", "rs")])
                stt(tg, tg, stat[0:n, 20:21], ggm[0:n, :], ALU.subtract, ALU.mult,
                    t4k(tsl) + [("stat", "mv"), ("ggm",)], t4k(tsl))
                if c < 4:
                    ts(gvtm[0:n, c, :], tg, stat[0:n, 23:24], None, ALU.mult, None,
                       t4k(tsl) + [("stat", "rs")], [("gv", c)])
                else:
                    ts(tg, tg, stat[0:n, 23:24], None, ALU.mult, None,
                       t4k(tsl) + [("stat", "rs")], t4k(tsl))
                    dma("sp", gv_s[:, :], tg, t4k(tsl), [("gv_s_out",)], ("gvout",))
                    dcopy(gvtm[0:n, c, :], tg, t4k(tsl), [("gv", c)])
            S.alias(A8A_KEYS, [("km",)])
            for s_ in range(2):
                sl_, sk_ = wslab(w_in_v, 6144 + s_ * 512)
                for ft in range(4):
                    for (g, c0, n) in grp:
                        b = ps1("D")
                        fm_proj(sl_, sk_, (ft * 128, ft * 128 + 128), g, c0, n, b)
                        act(A8a[:, s_ * 4 + ft, c0:c0 + n], PS[:, b, 0:n], AF.Gelu, psk(b),
                            [("A8a", s_ * 4 + ft, g)])
            S.alias(A8B_KEYS, CSK)
            for (c, c0, n) in chk:
                g = 0 if c < 4 else 1
                b = ps2("D")
                items = []
                for gg in range(8):
                    o_ = PS[:, b + gg // 4, (gg % 4) * 128:(gg % 4) * 128 + n]
                    if c < 4:
                        items.append((o_, gvtm[:, c, gg * 128:(gg + 1) * 128], wsTb[:, gg, :]))
                    else:
                        items.append((o_, gvtm[0:SC, c, gg * 128:(gg + 1) * 128], wsTsb[0:SC, gg, :]))
                mm_multi(items, [("gv", c), ("wsTb",), ("wsTsb",)], b, 2)
                srcv = PS[:, b:b + 2, :].rearrange("p b (j n) -> p (b j) n", j=4)[:, :, 0:n]
                tsl = 2 + c % 2
                tmp = T4[:, tsl, 0:8 * n].rearrange("p (g n) -> p g n", g=8)
                bias_ = bsr[:, :, :] if c < 4 else bsrs[:, :, :]
                tt(tmp, srcv, bias_, ALU.add, psk(b, 2) + [("bsr",), ("bsrs",)], t4k(tsl))
                tt(A8b[:, :, c0:c0 + n], tmp, A8a[:, :, c0:c0 + n], ALU.mult,
                   t4k(tsl) + [("A8a", k, g) for k in range(8)], [("A8b", k, g) for k in range(8)])
            if p == 0:
                dump("ybT0", A8b[:, :, 0:NS], [128, 8, NS], [("A8b", k, 0) for k in range(8)], bf=True)

            if STOP == "C":
                break
            S.alias(HB_D_KEYS, HB_C_KEYS)
            for which, (buf, nm) in enumerate(((sga, "sga"), (sgb, "sgb"))):
                for s_ in range(2):
                    sl_, sk_ = wslab(w_in_v, 8192 + which * 1024 + s_ * 512)
                    for ft in range(4):
                        for (g, c0, n) in grp:
                            b = ps1("D")
                            fm_proj(sl_, sk_, (ft * 128, ft * 128 + 128), g, c0, n, b)
                            act(buf[:, s_ * 4 + ft, c0:c0 + n], PS[:, b, 0:n], AF.Sigmoid, psk(b),
                                [(nm, s_ * 4 + ft, g)])
            for hf in range(2):
                wb_, kb_ = wslab(w_br_b_v, hf * 512)
                for pr in range(2):
                    wa_, ka_ = wslab(w_br_a_v, hf * 512 + pr * 256, ncols=256, k=16)
                    for fl in range(2):
                        f_ = hf * 4 + pr * 2 + fl
                        for (g, c0, n) in grp:
                            b = ps2("D")
                            mm_group(PS[:, b, 0:n],
                                     [(wa_[:, k, fl * 128:(fl + 1) * 128], A16[:, k, c0:c0 + n]) for k in range(16)],
                                     [ka_] + [("A16", k, g) for k in range(16)], b)
                            fcol = (pr * 2 + fl) * 128
                            mm_group(PS[:, b + 1, 0:n],
                                     [(wb_[:, k, fcol:fcol + 128], A8b[:, k, c0:c0 + n]) for k in range(8)],
                                     [kb_] + [("A8b", k, g) for k in range(8)], b + 1)
                            m1 = T4[:, 0, 0:n]; m2 = T4[:, 1, 0:n]
                            tt(m1, PS[:, b, 0:n], sga[:, f_, c0:c0 + n], ALU.mult,
                               psk(b) + [("sga", f_, g)], [("T4", 0)])
                            tt(m2, PS[:, b + 1, 0:n], sgb[:, f_, c0:c0 + n], ALU.mult,
                               psk(b + 1) + [("sgb", f_, g)], [("T4", 1)])
                            tt(A8a[:, f_, c0:c0 + n], m1, m2, ALU.add, [("T4", 0), ("T4", 1)],
                               [("A8a", f_, g)])
            if p == 0:
                dump("mrgT0", A8a[:, :, 0:NS], [128, 8, NS], [("A8a", k, 0) for k in range(8)], bf=True)
            for s_ in range(2):
                sl_, sk_ = wslab(w_o_v, s_ * 512)
                for ft in range(4):
                    f_ = s_ * 4 + ft
                    for (g, c0, n) in grp:
                        b = ps1("D")
                        fm_proj(sl_, sk_, (ft * 128, ft * 128 + 128), g, c0, n, b,
                                rhs_buf=A8a, rkeys=[("A8a", k, g) for k in range(8)])
                        tt(hT[:, f_, c0:c0 + n], PS[:, b, 0:n], hT[:, f_, c0:c0 + n], ALU.add,
                           psk(b) + [("hT", f_, g)], [("hT", f_, g)])
            if p == 0:
                dump("hT1", hT[:, :, 0:NS], [128, 8, NS], hkeys(0))

            if STOP == "D":
                break
            rmsnorm_fm(gmlp, last)
            for ffg in range(2):
                for s_ in range(4):
                    sl_, sk_ = wslab(w_up_v, ffg * 2048 + s_ * 512)
                    for ft in range(4):
                        kk = s_ * 4 + ft
                        for (g, c0, n) in grp:
                            b = ps1("D")
                            fm_proj(sl_, sk_, (ft * 128, ft * 128 + 128), g, c0, n, b)
                            tsl = kk % 2
                            r_ = T4[:, tsl, 0:n]
                            act(r_, PS[:, b, 0:n], AF.Relu, psk(b), [("T4", tsl)])
                            tt(A16[:, kk, c0:c0 + n], r_, r_, ALU.mult, [("T4", tsl)], [("A16", kk, g)])
                for cq in range(4):
                    sl_, sk_ = wslab(w_down_v, cq * 256, ncols=256, k0=ffg * 16, k=16)
                    for fl in range(2):
                        f_ = cq * 2 + fl
                        for (g, c0, n) in grp:
                            b = ps1("D")
                            mm_group(PS[:, b, 0:n],
                                     [(sl_[:, k, fl * 128:(fl + 1) * 128], A16[:, k, c0:c0 + n]) for k in range(16)],
                                     [sk_] + [("A16", k, g) for k in range(16)], b)
                            tt(hT[:, f_, c0:c0 + n], PS[:, b, 0:n], hT[:, f_, c0:c0 + n], ALU.add,
                               psk(b) + [("hT", f_, g)], [("hT", f_, g)])
            if p == 0:
                dump("hT2", hT[:, :, 0:NS], [128, 8, NS], hkeys(0))

            rmsnorm_fm(gple, last)
            for (c, c0, n) in chk:
                slot = c % 2
                src = pp[t0 + c0:t0 + c0 + n, :] if c < 4 else psm[:, :]
                dma("sp", T4[0:n, slot, 0:256], src, [], [("T4", slot)], ("xs", slot))
                b = ps1("D")
                tr_multi([(PS[:, b, dc * 128:dc * 128 + n], T4[0:n, slot, dc * 128:(dc + 1) * 128],
                           identf[0:n, 0:n]) for dc in range(2)], [("T4", slot), ("cm",)], b)
                srcv = PS[:, b, 0:256].rearrange("p (j n) -> p j n", j=2)[:, :, 0:n]
                act(pT[:, :, c0:c0 + n], srcv, AF.Copy, psk(b), [("pT", 0 if c < 4 else 1)])
            if p + 1 < npass:
                for c in range(2):
                    t1_ = (p + 1) * NS + c * 128
                    dma("sp", T4[0:128, c, :], xp[t1_:t1_ + 128, :], [], t4k(c), ("xs", c))
            wpp_, kpp_ = load_slab([((0, 1024), w_pp_v[:, :, :])], (2, 1024))
            for s_ in range(2):
                sl_, sk_ = wslab(w_pg_v, s_ * 512)
                for ft in range(4):
                    f_ = s_ * 4 + ft
                    for (g, c0, n) in grp:
                        b = ps2("D")
                        fm_proj(sl_, sk_, (ft * 128, ft * 128 + 128), g, c0, n, b)
                        mm_group(PS[:, b + 1, 0:n],
                                 [(wpp_[:, k, f_ * 128:(f_ + 1) * 128], pT[:, k, c0:c0 + n]) for k in range(2)],
                                 [kpp_, ("pT", g)], b + 1)
                        sg_ = T4[:, 2, 0:n]
                        act(sg_, PS[:, b, 0:n], AF.Sigmoid, psk(b), [("T4", 2)])
                        tt(sg_, PS[:, b + 1, 0:n], sg_, ALU.mult, psk(b + 1) + [("T4", 2)], [("T4", 2)])
                        tt(hT[:, f_, c0:c0 + n], sg_, hT[:, f_, c0:c0 + n], ALU.add,
                           [("T4", 2), ("hT", f_, g)], [("hT", f_, g)])

            for (c, c0, n) in chk:
                g = 0 if c < 4 else 1
                b = ps2("D")
                tr_multi([(PS[0:n, b + k // 4, (k % 4) * 128:(k % 4 + 1) * 128], hT[:, k, c0:c0 + n], identf)
                          for k in range(8)], hkeys(g) + [("cm",)], b, 2)
                hv = PS[0:n, b:b + 2, :].rearrange("p b n -> p (b n)")
                act(A8b[0:n, 0:2, 0:512], PS[0:n, b:b + 2, :], AF.Square, psk(b, 2),
                    [("A8b", 0, 0), ("A8b", 1, 0), ("stat", "f0")], accum_out=stat[0:n, 32:33])
                act(stat[0:n, 33:34], stat[0:n, 32:33], AF.Sqrt, [("stat", "f0")], [("stat", "f1")],
                    scale=1.0 / D, bias=EPS)
                recip(stat[0:n, 34:35], stat[0:n, 33:34], [("stat", "f1")], [("stat", "f2")])
                slot = 2 + c % 2
                stt(T4[0:n, slot, :], hv, stat[0:n, 34:35], gfin[0:n, :], ALU.mult, ALU.mult,
                    psk(b, 2) + [("stat", "f2"), ("gfin",)], t4k(slot))
                dst = y_p[t0 + c0:t0 + c0 + n, :] if c < 4 else y_s[:, :]
                dma("sp", dst, T4[0:n, slot, :], t4k(slot), [("y_out", slot)], ("yout", slot))

        while pend_st:
            flush_store()
        eng_sems = {}
        for e in ("pe", "act", "dve", "pool"):
            eng_sems[e] = es.enter_context(nc.semaphore("sem_" + e))
        slot_sems = {}
        for i, s_ in enumerate(sorted(S.slot_cnt.keys(), key=str)):
            slot_sems[s_] = es.enter_context(nc.semaphore("dsem_%d" % i))
        final_slots = [s_ for s_ in S.slot_cnt if s_[0] in ("yout", "ssout", "spout", "gvout", "dbg")]
        S.emit(nc, eng_sems, slot_sems, final_slots)
    return nc, dbg_outs


_CACHE = {}


def kernel(x_prompt, x_sample, p_prompt, p_sample, state_ret, g_mix, w_in, g_ret, w_s, b_s,
           g_gm, w_br_a, w_br_b, w_o, g_mlp, w_up, w_down, g_ple, w_pg, w_pp, g_final):
    f = np.float32
    A = lambda a: np.ascontiguousarray(np.asarray(a), dtype=f)
    cosT, sinT, cm, chdec, chdec_s = _host_consts()
    cm_off, cm_w = _cm_layout(cm)
    cmisc = np.concatenate([cm[k] for k in CM_ORDER], axis=1).astype(f)

    npass = DBG_PASSES if DEBUG else NPASS
    nc, dbg = build_program(cm_off, cm_w, chdec, chdec_s, npass=npass)

    def col(gv, n):
        return np.asarray(gv, dtype=f).reshape(n, 128).T
    gcols = np.concatenate([col(g_mix[0], 8), col(g_mlp[0], 8), col(g_ple[0], 8), col(g_ret[0], 16)], axis=1)
    ggm_rep = np.broadcast_to(np.asarray(g_gm[0], dtype=f)[None, :], (128, D))
    gfin_rep = np.broadcast_to(np.asarray(g_final, dtype=f)[None, :], (128, D))
    ws = np.asarray(w_s[0], dtype=f)
    wsT = np.transpose(ws, (2, 0, 1)).reshape(128, 8 * 128)
    w4 = np.transpose(ws[:, :4, :4], (2, 0, 1))
    wsTs = np.zeros((128, 8, SC), f)
    wsTs[:SC] = np.tile(w4, (NSAMP, 1, NSAMP))
    bs = np.asarray(b_s[0], dtype=f)
    bs_rep = np.broadcast_to(bs[None, :, :], (128, 8, 128)).reshape(128, -1)
    bs_rep_s = np.broadcast_to(np.tile(bs[:, :4], (1, NSAMP))[None], (128, 8, SC)).reshape(128, -1)

    shared = {
        "w_in": A(w_in[0]), "w_br_a": A(w_br_a[0]), "w_br_b": A(w_br_b[0]), "w_o": A(w_o[0]),
        "w_up": A(w_up[0]), "w_down": A(w_down[0]), "w_pg": A(w_pg[0]), "w_pp": A(w_pp[0]),
        "gcols": A(gcols), "ggm_rep": A(ggm_rep), "gfin_rep": A(gfin_rep),
        "wsT": A(wsT), "wsTs": A(wsTs.reshape(128, -1)), "bs_rep": A(bs_rep), "bs_rep_s": A(bs_rep_s),
        "cosT": A(cosT), "sinT": A(sinT), "cmisc": A(cmisc),
        "qmask": A(cm["qmask"]), "tril": A(cm["tril"]),
    }
    xpn = np.asarray(x_prompt); xsn = np.asarray(x_sample)
    ppn = np.asarray(p_prompt); psn = np.asarray(p_sample); stn = np.asarray(state_ret)
    in_maps = []
    for c in range(N_CORES):
        m = dict(shared)
        m["xp"] = A(xpn[c]); m["pp"] = A(ppn[0, c])
        m["xs"] = A(xsn[c * NSAMP:(c + 1) * NSAMP].reshape(SC, D))
        m["ps"] = A(psn[0, c * NSAMP:(c + 1) * NSAMP].reshape(SC, 256))
        m["st"] = A(stn[0, c * NSAMP:(c + 1) * NSAMP])
        in_maps.append(m)
    res = run_bass_kernel_spmd(nc, in_maps, core_ids=list(range(N_CORES)))
    R = res.results
    y_prompt = np.stack([R[c]["y_p"] for c in range(N_CORES)], axis=0).astype(f)
    y_sample = np.concatenate([R[c]["y_s"].reshape(NSAMP, 4, D) for c in range(N_CORES)], axis=0).astype(f)
    s_prompt = np.stack([R[c]["s_p"] for c in range(N_CORES)], axis=0)[None].astype(f)
    s_sample = np.concatenate([R[c]["s_s"] for c in range(N_CORES)], axis=0)[None].astype(f)
    gv_sample = np.concatenate([R[c]["gv_s"].reshape(NSAMP, 4, D) for c in range(N_CORES)], axis=0)[None].astype(f)
    if DEBUG:
        _CACHE["dbg"] = {k: [R[c]["dbg_" + k] for c in range(N_CORES)] for k in dbg}
    return (y_prompt, y_sample, s_prompt, s_sample, gv_sample)
```

```python
import os
import math
import numpy as np
import concourse.bass as bass
import concourse.mybir as mybir
from concourse.bass_utils import run_bass_kernel_spmd

F32 = mybir.dt.float32
BF16 = mybir.dt.bfloat16
AF = mybir.ActivationFunctionType
ALU = mybir.AluOpType

N_CORES = 8
D = 1024
SEQ = 2048
NS = 512
NPASS = SEQ // NS
NSAMP = 16
SC = 64
NC = NS + SC
H_A, DK, DV = 4, 256, 512
PAST = 16384
EPS = 1e-6
R_SLAB = 4
SAME_ENGINE_ALL = bool(int(os.environ.get("MK_SEA", "0")))
USE_WSCR = True
DEBUG = bool(int(os.environ.get("MK_DEBUG", "0")))
DBG_PASSES = int(os.environ.get("MK_PASSES", str(NPASS)))
STOP = os.environ.get("MK_STOP", "")


class Op:
    __slots__ = ("eng", "fn", "idx", "waits", "signal", "val", "dma", "slot", "ninc", "small")

    def __init__(self, eng, fn, idx):
        self.eng = eng
        self.fn = fn
        self.idx = idx
        self.waits = []
        self.signal = False
        self.val = None
        self.dma = False
        self.slot = None
        self.ninc = 1
        self.small = False


class Sched:
    ENGS = ("pe", "act", "dve", "pool", "sp")

    def __init__(self):
        self.ops = {e: [] for e in self.ENGS}
        self.state = {}
        self.slot_cnt = {}
        self.last_dma = {}

    def _st(self, k):
        st = self.state.get(k)
        if st is None:
            st = [None, []]
            self.state[k] = st
        return st

    def add(self, eng, fn, reads=(), writes=(), dma_slot=None, ninc=1, accum=False):
        op = Op(eng, fn, len(self.ops[eng]))
        op.small = any(k[0] == "stat" for k in writes)
        deps = []
        if eng == "pe" and not accum:
            for k in writes:
                st = self.state.get(k)
                if k[0] == "ps" and st is not None and st[0] is not None and st[0].eng == "pe" and not st[1]:
                    raise AssertionError("PSUM bank %s overwritten by PE before its result was read" % (k,))
        for k in reads:
            st = self._st(k)
            if st[0] is not None:
                deps.append((st[0], True))
        for k in writes:
            st = self._st(k)
            if st[0] is not None:
                deps.append((st[0], False))
            for r in st[1]:
                deps.append((r, False))
        best = {}
        for d, raw in deps:
            if d.dma or d.eng == eng:
                continue
            b_ = best.get(d.eng)
            if b_ is None or d.idx > b_.idx:
                best[d.eng] = d
        deps = [(d, raw) for d, raw in deps if d.dma or d.eng == eng or best[d.eng] is d]
        seen = set()
        for d, raw in deps:
            if d is op or id(d) in seen:
                continue
            if (not d.dma) and d.eng == eng:
                if eng == "pe":
                    continue
                if not SAME_ENGINE_ALL and not (raw and d.small):
                    continue
            seen.add(id(d))
            op.waits.append(d)
            d.signal = True
        for k in reads:
            self._st(k)[1].append(op)
        for k in writes:
            st = self._st(k)
            st[0] = op
            st[1] = []
        if dma_slot is not None:
            op.dma = True
            op.slot = dma_slot
            op.ninc = ninc
            n = self.slot_cnt.get(dma_slot, 0) + ninc
            self.slot_cnt[dma_slot] = n
            op.val = 16 * n
            self.last_dma[dma_slot] = op
        self.ops[eng].append(op)
        return op

    def alias(self, new_keys, old_keys):
        best = {}
        dmas = {}
        for k in old_keys:
            st = self.state.get(k)
            if st is None:
                continue
            cands = list(st[1])
            if st[0] is not None:
                cands.append(st[0])
            for o in cands:
                if o.dma:
                    dmas[id(o)] = o
                else:
                    b = best.get(o.eng)
                    if b is None or o.idx > b.idx:
                        best[o.eng] = o
            self.state[k] = [None, []]
        fence = list(best.values()) + list(dmas.values())
        for k in new_keys:
            st = self._st(k)
            have = {id(o) for o in st[1]}
            st[1] = list(st[1]) + [o for o in fence if id(o) not in have]

    def emit(self, nc, eng_sems, slot_sems, final_slots):
        for e in ("pe", "act", "dve", "pool"):
            c = 0
            for op in self.ops[e]:
                if op.dma:
                    continue
                if op.signal:
                    c += 1
                    op.val = c
        fin = Op("sp", None, len(self.ops["sp"]))
        for s in final_slots:
            if s in self.last_dma:
                fin.waits.append(self.last_dma[s])
        self.ops["sp"].append(fin)

        def sem_of(d):
            return slot_sems[d.slot] if d.dma else eng_sems[d.eng]

        def make(e):
            def body(eng):
                waited = {}
                for op in self.ops[e]:
                    for d in op.waits:
                        key = ("s", d.slot) if d.dma else ("e", d.eng)
                        if waited.get(key, 0) >= d.val:
                            continue
                        eng.wait_ge(sem_of(d), d.val)
                        waited[key] = d.val
                    if op.fn is None:
                        continue
                    inst = op.fn(eng)
                    if op.dma:
                        insts = inst if isinstance(inst, (list, tuple)) else [inst]
                        assert len(insts) == op.ninc
                        for i_ in insts:
                            i_.then_inc(slot_sems[op.slot], 16)
                    elif op.signal:
                        inst.then_inc(eng_sems[e], 1)
            return body

        with nc.Block() as block:
            block.tensor(make("pe"))
            block.scalar(make("act"))
            block.vector(make("dve"))
            block.gpsimd(make("pool"))
            block.sync(make("sp"))


def _host_consts():
    f = np.float32
    half = DK // 2
    inv = (np.float32(10000.0) ** (-np.arange(half, dtype=f) / np.float32(half))).astype(f)
    pos_p = np.arange(SEQ, dtype=f)
    pos_s = (np.float32(PAST) + np.arange(4, dtype=f)).astype(f)
    pos_s = np.tile(pos_s, NSAMP)
    pos = np.concatenate([pos_p, pos_s]).astype(f)
    ang = (pos[None, :] * inv[:, None]).astype(f)
    cosT = np.cos(ang.astype(np.float64)).astype(f)
    sinT = np.sin(ang.astype(np.float64)).astype(f)

    lg = np.log(1.0 - 2.0 ** (-5.0 - np.arange(H_A, dtype=np.float64)))
    idx = np.arange(128, dtype=np.float64)
    js = (np.arange(SC) % 4).astype(np.float64)
    smp = np.arange(SC) // 4

    cm = {}
    cm["identf"] = np.eye(128, dtype=f)
    cm["tril"] = (idx[:, None] <= idx[None, :]).astype(f)
    maskTs = np.zeros((128, H_A, SC), f)
    qdec = np.zeros((128, H_A, NS), f)
    qdec_s = np.zeros((128, H_A, SC), f)
    stdec = np.zeros((128, H_A * 4), f)
    rowsc = np.zeros((128, H_A * 4), f)
    stdec_s = np.zeros((128, H_A), f)
    tl = np.arange(NS, dtype=np.float64)
    for h in range(H_A):
        ms = ((smp[None, :] == smp[:, None]) & (js[None, :] >= js[:, None])) * \
            np.exp(-(js[:, None] + 1.0) * lg[h]) / 16.0
        maskTs[:SC, h, :] = ms.astype(f)
        qdec[:, h, :] = np.exp((tl[None, :] + 1.0) * lg[h]).astype(f)
        qdec_s[:, h, :] = np.exp((js[None, :] + 1.0) * lg[h]).astype(f)
        for ck in range(4):
            kl = ck * 128 + idx
            stdec[:, h * 4 + ck] = (np.exp((NS - 1.0 - kl) * lg[h]) / 16.0).astype(f)
            rowsc[:, h * 4 + ck] = (np.exp(-(kl + 1.0) * lg[h]) / 16.0).astype(f)
        stdec_s[:SC, h] = (np.exp((3.0 - js) * lg[h]) / 16.0).astype(f)
    cmask = np.ones((128, NS), f)
    cmask[:, :128] = (idx[None, :] >= idx[:, None]).astype(f)
    cm["cmask"] = cmask
    cm["rowsc"] = rowsc
    cm["maskTs"] = maskTs.reshape(128, -1)
    cm["qdec"] = qdec.reshape(128, -1)
    cm["qdec_s"] = qdec_s.reshape(128, -1)
    cm["stdec"] = stdec
    cm["stdec_s"] = stdec_s
    qmask = np.zeros((128, NSAMP, SC), f)
    for i in range(NSAMP):
        qmask[:, i, 4 * i:4 * i + 4] = 1.0
    cm["qmask"] = qmask.reshape(128, -1)
    kmask = np.zeros((128, NSAMP), f)
    for i in range(NSAMP):
        kmask[4 * i:4 * i + 4, i] = 1.0
    cm["kmask"] = kmask
    bms = np.zeros((128, SC), f)
    bms[:SC, :] = ((smp[:, None] == smp[None, :]) & (js[:, None] <= js[None, :])).astype(f)
    cm["bms"] = bms
    chdec = [float(np.exp(float(NS) * lg[h])) for h in range(H_A)]
    chdec_s = [float(np.exp(4.0 * lg[h])) for h in range(H_A)]
    return cosT, sinT, cm, chdec, chdec_s


CM_ORDER = ["identf", "cmask", "rowsc", "maskTs", "qdec", "qdec_s", "stdec", "stdec_s",
            "kmask", "bms"]


def _cm_layout(cm):
    off = {}
    o = 0
    for k in CM_ORDER:
        w = cm[k].shape[1]
        off[k] = (o, w)
        o += w
    return off, o


def build_program(cm_off, cm_w, chdec, chdec_s, npass=NPASS):
    nc = bass.Bass("TRN2", target_bir_lowering=False)
    S = Sched()

    def din(name, shape, dt=F32):
        return nc.dram_tensor(name, list(shape), dt, kind="ExternalInput").ap()

    def dout(name, shape, dt=F32):
        return nc.dram_tensor(name, list(shape), dt, kind="ExternalOutput").ap()

    xp = din("xp", [SEQ, D]); pp = din("pp", [SEQ, 256])
    xs = din("xs", [SC, D]); psm = din("ps", [SC, 256])
    st_in = din("st", [NSAMP, H_A, DK, DV])
    w_in = din("w_in", [D, 10240]); w_br_a = din("w_br_a", [2048, D]); w_br_b = din("w_br_b", [D, D])
    w_o = din("w_o", [D, D]); w_up = din("w_up", [D, 4096]); w_down = din("w_down", [4096, D])
    w_pg = din("w_pg", [D, D]); w_pp = din("w_pp", [256, D])
    gcols_d = din("gcols", [128, 40])
    ggm_d = din("ggm_rep", [128, D]); gfin_d = din("gfin_rep", [128, D])
    wsT_d = din("wsT", [128, 8 * 128]); wsTs_d = din("wsTs", [128, 8 * SC])
    bsr_d = din("bs_rep", [128, 8 * 128]); bsrs_d = din("bs_rep_s", [128, 8 * SC])
    cos_d = din("cosT", [128, SEQ + SC]); sin_d = din("sinT", [128, SEQ + SC])
    cm_d = din("cmisc", [128, cm_w])
    qmask_d = din("qmask", [128, NSAMP * SC])
    tril_d = din("tril", [128, 128])

    y_p = dout("y_p", [SEQ, D]); y_s = dout("y_s", [SC, D])
    s_p = dout("s_p", [H_A, DK, DV]); s_s = dout("s_s", [NSAMP, H_A, DK, DV])
    gv_s = dout("gv_s", [SC, D])

    dbg_outs = {}
    NSLABS = 47
    wscr = nc.dram_tensor("wscr", [NSLABS, 128, 4096], BF16, kind="Internal").ap()

    def kcv(w):
        return w.rearrange("(kc p) f -> p kc f", p=128)

    w_in_v = kcv(w_in); w_br_a_v = kcv(w_br_a); w_br_b_v = kcv(w_br_b); w_o_v = kcv(w_o)
    w_up_v = kcv(w_up); w_down_v = kcv(w_down); w_pg_v = kcv(w_pg); w_pp_v = kcv(w_pp)

    import contextlib
    es = contextlib.ExitStack()
    with es:
        def sb(name, shape, dt):
            return es.enter_context(nc.sbuf_tensor(name, list(shape), dt))

        hT = sb("hT", [128, 8, NC], F32)
        nT = sb("nT", [128, 8, NC], BF16)
        A16 = sb("A16", [128, 16, NC], BF16)
        A8a = sb("A8a", [128, 8, NC], BF16)
        A8b = sb("A8b", [128, 8, NC], BF16)
        HB = sb("HB", [128, 9216], BF16)
        S32 = sb("S32", [128, H_A, 2, 512], F32)
        Sbf = sb("Sbf", [128, 2, 512], BF16)
        Ss32 = sb("Ss32", [128, 2, 2, 512], F32)
        Ssb = sb("Ssb", [128, 2, 2, 512], BF16)
        SL = sb("SL", [128, R_SLAB, 4096], BF16)
        T4 = sb("T4", [128, 4, 1024], F32)
        pT = sb("pT", [128, 2, NC], BF16)
        qm = sb("qm", [128, 2, NSAMP, SC], BF16)
        scm = sb("scm", [128, 4, 512], BF16)
        yatm = sb("yatm", [128, 4, 512], BF16)
        junkb = sb("junkb", [128, 512], BF16)
        stat = sb("stat", [128, 64], F32)
        cmisc = sb("cmisc_sb", [128, cm_w], F32)
        identb = sb("identb", [128, 128], BF16)
        onesb = sb("onesb", [128, 128], BF16)
        qmaskb = sb("qmaskb", [128, NSAMP, SC], BF16)
        gcols = sb("gcols_sb", [128, 40], F32)
        ggm = sb("ggm_sb", [128, D], F32)
        gfin = sb("gfin_sb", [128, D], F32)
        wsTb = sb("wsTb", [128, 8, 128], BF16)
        wsTsb = sb("wsTsb", [128, 8, SC], BF16)
        bsr = sb("bsr", [128, 8, 128], F32)
        bsrs = sb("bsrs", [128, 8, SC], F32)
        PS = es.enter_context(nc.psum_tensor("PS", [128, 8, 512], F32))

        def cmv(name):
            o, w = cm_off[name]
            return cmisc[:, o:o + w]

        identf = cmv("identf")
        cmask = cmv("cmask")
        rowsc = cmv("rowsc")
        maskTs = cmv("maskTs").rearrange("p (h n) -> p h n", h=H_A)
        qdec = cmv("qdec").rearrange("p (h n) -> p h n", h=H_A)
        qdec_s = cmv("qdec_s").rearrange("p (h n) -> p h n", h=H_A)
        stdec = cmv("stdec"); stdec_s = cmv("stdec_s")
        kmask = cmv("kmask")

        cosv = A8b[:, 0:4, :].bitcast(F32)
        A8b_f = A8b[:, :, :].rearrange("p k n -> p (k n)").bitcast(F32)
        cosT = A8b_f[:, 0:NC]
        sinT = A8b_f[:, NC:2 * NC]
        qkT = HB[:, 0:4 * NC].rearrange("p (k n) -> p k n", k=4)
        o1 = 4 * NC
        ktm = HB[:, o1:o1 + 5 * 256].rearrange("p (c n) -> p c n", c=5)
        o2 = o1 + 5 * 256
        vtm = HB[:, o2:o2 + 5 * 512].rearrange("p (c n) -> p c n", c=5)
        o3 = o2 + 5 * 512
        gsm = HB[:, o3:o3 + 5 * 512].rearrange("p (c n) -> p c n", c=5)
        assert o3 + 5 * 512 <= 9216
        gvtm = HB[:, 0:5 * 1024].rearrange("p (c n) -> p c n", c=5)
        sga = HB[:, 0:8 * NC].rearrange("p (k n) -> p k n", k=8)
        sgb = HB[:, 8 * NC:16 * NC].rearrange("p (k n) -> p k n", k=8)
        km = A8a[:, :, :].rearrange("p k n -> p (k n)")[:, 0:NSAMP * 256].rearrange("p (i n) -> p i n", i=NSAMP)

        HB_B_KEYS = [("qk", i, g) for i in range(4) for g in range(2)] + \
                    [(n, c) for n in ("ktm", "vtm", "gs") for c in range(5)]
        HB_C_KEYS = [("gv", c) for c in range(5)]
        HB_D_KEYS = [(n, f_, g) for n in ("sga", "sgb") for f_ in range(8) for g in range(2)]
        A8A_KEYS = [("A8a", k, g) for k in range(8) for g in range(2)]
        A8B_KEYS = [("A8b", k, g) for k in range(8) for g in range(2)]

        ring_cnt = {"s": 0, "p": 0}
        SBANK = 7

        def ps1(ring=None):
            b = ring_cnt["s"] % 7
            ring_cnt["s"] += 1
            return b

        def ps2(ring=None):
            b = 2 * (ring_cnt["p"] % 3)
            ring_cnt["p"] += 1
            return b

        def psk(b, n=1):
            return [("ps", b + i) for i in range(n)]

        def NTK(g):
            return [("nT", k_, g) for k_ in range(8)]

        fine_next = [False]

        def t4k(sl):
            return [("T4", sl), ("T4", sl, "b")]

        def mm_group(out_ap, pairs, reads, bank, nb=1, transpose=False, ident=None):
            pairs = list(pairs)

            def fn(pe, pairs=pairs, out_ap=out_ap):
                last = None
                n = len(pairs)
                for i, (l, r) in enumerate(pairs):
                    last = pe.matmul(out_ap, l, r, start=(i == 0), stop=(i == n - 1))
                return last
            return S.add("pe", fn, reads=reads, writes=psk(bank, nb))

        def mm_multi(items, reads, bank, nb=1):
            items = list(items)

            def fn(pe, items=items):
                last = None
                for (o, l, r) in items:
                    last = pe.matmul(o, l, r, start=True, stop=True)
                return last
            return S.add("pe", fn, reads=reads, writes=psk(bank, nb))

        def tr_multi(items, reads, bank, nb=1):
            items = list(items)

            def fn(pe, items=items):
                last = None
                for (o, i_, idn) in items:
                    last = pe.transpose(o, i_, idn)
                return last
            return S.add("pe", fn, reads=reads, writes=psk(bank, nb))

        def act(out, in_, func, reads, writes, **kw):
            def fn(e, out=out, in_=in_, func=func, kw=kw):
                return e.activation(out, in_, func, **kw)
            return S.add("act", fn, reads=reads, writes=writes)

        def tt(out, in0, in1, op, reads, writes):
            def fn(e, out=out, in0=in0, in1=in1, op=op):
                return e.tensor_tensor(out, in0, in1, op)
            return S.add("dve", fn, reads=reads, writes=writes)

        def stt(out, in0, scalar, in1, op0, op1, reads, writes):
            def fn(e, out=out, in0=in0, scalar=scalar, in1=in1, op0=op0, op1=op1):
                return e.scalar_tensor_tensor(out, in0, scalar, in1, op0, op1)
            return S.add("dve", fn, reads=reads, writes=writes)

        def ts(out, in0, s1, s2, op0, op1, reads, writes):
            def fn(e, out=out, in0=in0, s1=s1, s2=s2, op0=op0, op1=op1):
                if op1 is None:
                    return e.tensor_scalar(out, in0, s1, None, op0)
                return e.tensor_scalar(out, in0, s1, s2, op0, op1)
            return S.add("dve", fn, reads=reads, writes=writes)

        def recip(out, in_, reads, writes):
            def fn(e, out=out, in_=in_):
                return e.reciprocal(out, in_)
            return S.add("dve", fn, reads=reads, writes=writes)

        def dcopy(out, in_, reads, writes):
            def fn(e, out=out, in_=in_):
                return e.tensor_copy(out, in_)
            return S.add("dve", fn, reads=reads, writes=writes)

        def dma(q, out, in_, reads, writes, slot):
            def fn(e, out=out, in_=in_):
                return e.dma_start(out=out, in_=in_)
            return S.add(q, fn, reads=reads, writes=writes, dma_slot=slot)

        def dump(name, ap, shape, reads, bf=False):
            if not DEBUG:
                return
            d = dout("dbg_" + name, shape)
            dbg_outs[name] = shape
            dma("pool" if bf else "sp", d, ap, reads, [("dbg", name)], ("dbg", name))

        slab_cnt = [0]
        slab_ids = {}
        cur_pass = [0]

        pend_st = []

        def flush_store():
            idx_, slot_, sz_ = pend_st.pop(0)
            dma("pool", wscr[idx_, :, 0:sz_], SL[:, slot_, 0:sz_], [("slab", slot_)], [("wscr", idx_)],
                ("wst", slot_))

        def load_slab(pieces, shape, sid=None):
            slot = slab_cnt[0] % R_SLAB
            slab_cnt[0] += 1
            k, n = shape
            while any(ps_[1] == slot for ps_ in pend_st):
                flush_store()
            view = SL[:, slot, 0:k * n].rearrange("p (k n) -> p k n", k=k)
            if sid is None:
                sid = tuple((a, b, str(src.tensor.name), str(src.offset), str(src.ap)) for (a, b), src in pieces)
            first = sid not in slab_ids
            if first:
                slab_ids[sid] = len(slab_ids)
                assert len(slab_ids) <= NSLABS
            idx = slab_ids[sid]
            if first or not USE_WSCR:
                def fn(e, pieces=pieces, view=view):
                    insts = []
                    for (a, b), src in pieces:
                        insts.append(e.dma_start(out=view[:, :, a:b], in_=src))
                    return insts
                S.add("pool", fn, reads=[], writes=[("slab", slot)], dma_slot=("slab", slot), ninc=len(pieces))
                if USE_WSCR and npass > 1:
                    pend_st.append((idx, slot, k * n))
                    while len(pend_st) > 2:
                        flush_store()
            else:
                dma("pool", SL[:, slot, 0:k * n], wscr[idx, :, 0:k * n], [("wscr", idx)], [("slab", slot)],
                    ("slab", slot))
            return view, ("slab", slot)

        def wslab(wv, c0, ncols=512, k0=0, k=8):
            return load_slab([((0, ncols), wv[:, k0:k0 + k, c0:c0 + ncols])], (k, ncols))

        dma("sp", cmisc[:, :], cm_d[:, :], [], [("cm",)], ("c0",))
        dma("sp", gcols[:, :], gcols_d[:, :], [], [("gcols",)], ("c1",))
        dma("sp", ggm[:, :], ggm_d[:, :], [], [("ggm",)], ("c2",))
        dma("sp", gfin[:, :], gfin_d[:, :], [], [("gfin",)], ("c3",))
        dma("sp", bsr[:, :, :], bsr_d.rearrange("p (g n) -> p g n", g=8), [], [("bsr",)], ("c4",))
        dma("sp", bsrs[:, :, :], bsrs_d.rearrange("p (g n) -> p g n", g=8), [], [("bsrs",)], ("c5",))
        T4f = T4[:, :, :].rearrange("p a b -> p (a b)")
        dma("sp", T4[:, 0, :], wsT_d[:, :], [], t4k(0), ("c6",))
        dma("sp", T4[:, 1, 0:8 * SC], wsTs_d[:, :], [], [("T4", 1)], ("c7",))
        dcopy(identb[:, :], identf, [("cm",)], [("identb",)])
        S.add("dve", lambda e: e.memset(onesb[:, :], 1.0), reads=[], writes=[("onesb",)])
        dma("pool", qmaskb[:, :, :], qmask_d.rearrange("p (i n) -> p i n", i=NSAMP), [], [("qmaskb",)], ("c8",))
        dma("sp", T4[:, 2, 0:128], tril_d[:, :], [], [("T4", 2)], ("c9",))
        tril = T4[:, 2, 0:128]
        tt(wsTb[:, :, :], T4[:, 0, :].rearrange("p (g n) -> p g n", g=8),
           tril.unsqueeze(1).broadcast_to([128, 8, 128]), ALU.mult,
           t4k(0) + [("T4", 2)], [("wsTb",)])
        tt(wsTsb[:, :, :], T4[:, 1, 0:8 * SC].rearrange("p (g n) -> p g n", g=8),
           cmv("bms").unsqueeze(1).broadcast_to([128, 8, SC]), ALU.mult,
           [("T4", 1), ("cm",)], [("wsTsb",)])

        gmix = gcols[:, 0:8]; gmlp = gcols[:, 8:16]; gple = gcols[:, 16:24]; gret = gcols[:, 24:40]

        def groups(last):
            g = [(0, 0, NS)]
            if last:
                g.append((1, NS, SC))
            return g

        def chunks(last):
            cs = [(c, c * 128, 128) for c in range(NS // 128)]
            if last:
                cs.append((4, NS, SC))
            return cs

        def hkeys(g):
            return [("hT", k, g) for k in range(8)]

        def rmsnorm_fm(gcol, last):
            for (g, c0, n) in groups(last):
                b = ps1("D")
                for k in range(8):
                    act(A16[:, k, c0:c0 + n], hT[:, k, c0:c0 + n], AF.Square, [("hT", k, g)], [("A16", k, g)])
                    S.add("pe", lambda pe, k=k, b=b, c0=c0, n=n: pe.matmul(
                        PS[:, b, 0:n], onesb[:, :], A16[:, k, c0:c0 + n], start=(k == 0), stop=(k == 7)),
                        reads=[("A16", k, g), ("onesb",)], writes=psk(b), accum=(k > 0))
                act(T4[:, 3, 0:n], PS[:, b, 0:n], AF.Sqrt, psk(b), [("T4", 3)], scale=1.0 / D, bias=EPS)
                recip(T4[:, 3, 512:512 + n], T4[:, 3, 0:n], [("T4", 3)], [("T4", 3, "b")])
                for k in range(8):
                    stt(nT[:, k, c0:c0 + n], hT[:, k, c0:c0 + n], gcol[:, k:k + 1], T4[:, 3, 512:512 + n],
                        ALU.mult, ALU.mult, [("hT", k, g), ("T4", 3, "b"), ("gcols",)], [("nT", k, g)])
                if g == 0:
                    fine_next[0] = True

        def fm_proj(slab, skey, ft_cols, g, c0, n, bank, rhs_buf=None, rkeys=None, K=8):
            rb = nT if rhs_buf is None else rhs_buf
            rk = NTK(g) if rkeys is None else rkeys
            a, b_ = ft_cols
            if fine_next[0] and rhs_buf is None and g == 0:
                fine_next[0] = False
                op = None
                for k in range(K):
                    op = S.add("pe", lambda pe, k=k: pe.matmul(
                        PS[:, bank, 0:n], slab[:, k, a:b_], rb[:, k, c0:c0 + n], start=(k == 0), stop=(k == K - 1)),
                        reads=[skey, rk[k]], writes=psk(bank), accum=(k > 0))
                return op
            return mm_group(PS[:, bank, 0:n], [(slab[:, k, a:b_], rb[:, k, c0:c0 + n]) for k in range(K)],
                            [skey] + rk, bank)

        for p in range(npass):
            if STOP == "setup":
                break
            last = (p == NPASS - 1)
            while pend_st:
                flush_store()
            t0 = p * NS
            grp = groups(last)
            chk = chunks(last)

            S.alias([("cs",)], A8B_KEYS)
            dma("sp", cosT[:, 0:NS], cos_d[:, t0:t0 + NS], [], [("cs",)], ("cs0",))
            dma("sp", sinT[:, 0:NS], sin_d[:, t0:t0 + NS], [], [("cs", 1)], ("cs1",))
            dma("sp", cosT[:, NS:NC], cos_d[:, SEQ:SEQ + SC], [], [("cs", 2)], ("cs2",))
            dma("sp", sinT[:, NS:NC], sin_d[:, SEQ:SEQ + SC], [], [("cs", 3)], ("cs3",))
            CSK = [("cs",), ("cs", 1), ("cs", 2), ("cs", 3)]
            for (c, c0, n) in chunks(p == 0):
                g = 0 if c < 4 else 1
                slot = c % 2
                src = xp[t0 + c0:t0 + c0 + n, :] if c < 4 else xs[:, :]
                if not (p > 0 and c < 2):
                    dma("sp", T4[0:n, slot, :], src, [], t4k(slot), ("xs", slot))
                b = ps2("D")
                tr_multi([(PS[:, b + k // 4, (k % 4) * 128:(k % 4) * 128 + n],
                           T4[0:n, slot, k * 128:(k + 1) * 128], identf[0:n, 0:n]) for k in range(8)],
                         t4k(slot) + [("cm",)], b, 2)
                src_ps = PS[:, b:b + 2, :].rearrange("p b (j n) -> p (b j) n", j=4)[:, :, 0:n]
                act(hT[:, :, c0:c0 + n], src_ps, AF.Copy, psk(b, 2), hkeys(g))
            rmsnorm_fm(gmix, p == 0)
            if p == 0:
                dump("hT0", hT[:, :, 0:NS], [128, 8, NS], hkeys(0))
                dump("nT0", nT[:, :, 0:NS], [128, 8, NS], NTK(0), bf=True)

            if STOP == "A":
                break
            S.alias(HB_B_KEYS, HB_D_KEYS + HB_C_KEYS)
            S.alias([("km",)], A8A_KEYS)
            from collections import deque
            pending = deque()

            pump_cnt = [0]

            def pump(k=1, force=False):
                for _ in range(k):
                    pump_cnt[0] += 1
                    if not force and pump_cnt[0] % 3 == 0:
                        continue
                    if pending:
                        pending.popleft()()

            slabs = {}

            def F1(h, with_s):
                qc0 = h * 256
                kc0 = 1024 + h * 256
                slab_qk, kqk = load_slab([((0, 256), w_in_v[:, :, qc0:qc0 + 256]),
                                          ((256, 512), w_in_v[:, :, kc0:kc0 + 256])], (8, 512))
                slabs[h] = (wslab(w_in_v, 2048 + h * 512), wslab(w_in_v, 4096 + h * 512))
                for (g, c0, n) in groups(with_s):
                    for which in range(2):
                        b = ps2()
                        fm_proj(slab_qk, kqk, (which * 256, which * 256 + 128), g, c0, n, b)
                        fm_proj(slab_qk, kqk, (which * 256 + 128, which * 256 + 256), g, c0, n, b + 1)
                        x1 = PS[:, b, 0:n]; x2 = PS[:, b + 1, 0:n]
                        cs_ = cosT[:, c0:c0 + n]; sn_ = sinT[:, c0:c0 + n]
                        t1 = T4[:, 0, 0:n]; t2 = T4[:, 0, 512:512 + n]; t3 = T4[:, 1, 0:n]; t4 = T4[:, 1, 512:512 + n]
                        tt(t1, x1, cs_, ALU.mult, psk(b) + CSK, [("T4", 0)])
                        tt(t2, x2, sn_, ALU.mult, psk(b + 1) + CSK, [("T4", 0, "b")])
                        tt(t3, x2, cs_, ALU.mult, psk(b + 1) + CSK, [("T4", 1)])
                        tt(t4, x1, sn_, ALU.mult, psk(b) + CSK, [("T4", 1, "b")])
                        if which == 0:
                            tt(t1, t1, t2, ALU.subtract, [("T4", 0), ("T4", 0, "b")], [("T4", 0)])
                            tt(t3, t3, t4, ALU.add, [("T4", 1), ("T4", 1, "b")], [("T4", 1)])
                            dq = qdec[:, h, :] if g == 0 else qdec_s[:, h, :]
                            tt(qkT[:, 0, c0:c0 + n], t1, dq, ALU.mult, [("T4", 0), ("cm",)], [("qk", 0, g)])
                            tt(qkT[:, 1, c0:c0 + n], t3, dq, ALU.mult, [("T4", 1), ("cm",)], [("qk", 1, g)])
                        else:
                            tt(qkT[:, 2, c0:c0 + n], t1, t2, ALU.subtract,
                               [("T4", 0), ("T4", 0, "b")], [("qk", 2, g)])
                            tt(qkT[:, 3, c0:c0 + n], t3, t4, ALU.add,
                               [("T4", 1), ("T4", 1, "b")], [("qk", 3, g)])
                        pump()

            def F2(h, with_s):
                (slab_v, kv), (slab_g, kg) = slabs[h]
                for (c, c0, n) in chunks(with_s):
                    g = 0 if c < 4 else 1
                    bv = ps1()
                    mm_group(PS[0:n, bv, :], [(nT[:, k, c0:c0 + n], slab_v[:, k, :]) for k in range(8)],
                             NTK(g) + [kv], bv)
                    act(vtm[0:n, c, :], PS[0:n, bv, :], AF.Copy, psk(bv), [("vtm", c)])
                    pump()
                    bg = ps1()
                    mm_group(PS[0:n, bg, :], [(nT[:, k, c0:c0 + n], slab_g[:, k, :]) for k in range(8)],
                             NTK(g) + [kg], bg)
                    act(gsm[0:n, c, :], PS[0:n, bg, :], AF.Silu, psk(bg), [("gs", c)])
                    pump()
                b = ps1()
                pst = PS[:, b, :].bitcast(BF16)
                tr_multi([(pst[:, c * 256 + dc * 128:c * 256 + (dc + 1) * 128], qkT[:, 2 + dc, c * 128:(c + 1) * 128],
                           identb[:, :]) for c in range(4) for dc in range(2)],
                         [("qk", 2, 0), ("qk", 3, 0), ("identb",)], b)
                sdb = stdec[:, h * 4:(h + 1) * 4].unsqueeze(2).broadcast_to([128, 4, 256])
                tt(ktm[:, 0:4, :], pst[:, 0:1024].rearrange("p (c n) -> p c n", c=4), sdb, ALU.mult,
                   psk(b) + [("cm",)], [("ktm", c) for c in range(4)])
                if with_s:
                    b = ps1()
                    pst = PS[:, b, :].bitcast(BF16)
                    tr_multi([(pst[0:SC, dc * 128:(dc + 1) * 128], qkT[:, 2 + dc, NS:NC], identb[:, :])
                              for dc in range(2)], [("qk", 2, 1), ("qk", 3, 1), ("identb",)], b)
                    act(ktm[0:SC, 4, :], pst[0:SC, 0:256], AF.Copy, psk(b) + [("cm",)], [("ktm", 4)],
                        scale=stdec_s[0:SC, h:h + 1])

            def B1(h):
                if p > 0:
                    act(Sbf[:, :, :], S32[:, h, :, :], AF.Copy, [("S32", h)], [("Sbf",)])
                QK = [("qk", i, 0) for i in range(4)]
                sb_ = []
                for ck in range(4):
                    nq = NS - ck * 128
                    bsc = ps1()
                    sb_.append(bsc)
                    mm_group(PS[:, bsc, 0:nq],
                             [(qkT[:, 2 + dc, ck * 128:(ck + 1) * 128], qkT[:, dc, ck * 128:NS]) for dc in range(2)],
                             QK, bsc)
                LV = int(os.environ.get("MK_B1", "99"))
                if LV < 1:
                    return
                for ck in range(4):
                    nq = NS - ck * 128
                    stt(scm[:, ck, 0:nq], PS[:, sb_[ck], 0:nq], rowsc[:, h * 4 + ck:h * 4 + ck + 1], cmask[:, 0:nq],
                        ALU.mult, ALU.mult, psk(sb_[ck]) + [("cm",)], [("scm", ck)])
                if LV < 2:
                    return
                bu = ps2()
                for dc in range(2):
                    mm_group(PS[:, bu + dc, :],
                             [(ktm[:, c, dc * 128:(dc + 1) * 128], vtm[:, c, :]) for c in range(4)],
                             [("ktm", c) for c in range(4)] + [("vtm", c) for c in range(4)], bu + dc)
                Sv = S32[:, h, :, :]
                Uv = PS[:, bu:bu + 2, :]
                if p == 0:
                    dcopy(Sv, Uv, psk(bu, 2), [("S32", h)])
                else:
                    stt(Sv, Sv, chdec[h], Uv, ALU.mult, ALU.add, psk(bu, 2) + [("S32", h)], [("S32", h)])
                if LV < 3:
                    return
                ob = []
                for cq in range(4):
                    bo = ps1()
                    ob.append(bo)
                    pairs = [(scm[:, ck, (cq - ck) * 128:(cq - ck + 1) * 128], vtm[:, ck, :]) for ck in range(cq + 1)]
                    rk = [("scm", ck) for ck in range(cq + 1)] + [("vtm", ck) for ck in range(cq + 1)]
                    if p > 0:
                        pairs += [(qkT[:, dc, cq * 128:(cq + 1) * 128], Sbf[:, dc, :]) for dc in range(2)]
                        rk += [("qk", 0, 0), ("qk", 1, 0), ("Sbf",)]
                    mm_group(PS[:, bo, :], pairs, rk, bo)
                if LV < 5:
                    return
                for cq in range(4):
                    act(junkb[:, 0:512], PS[:, ob[cq], :], AF.Square, psk(ob[cq]), [("junkb",), ("stat", 0, cq)],
                        accum_out=stat[:, cq:cq + 1])
                act(stat[:, 4:8], stat[:, 0:4], AF.Sqrt, [("stat", 0, cq) for cq in range(4)], [("stat", 1)],
                    scale=1.0 / DV, bias=EPS)
                recip(stat[:, 8:12], stat[:, 4:8], [("stat", 1)], [("stat", 2)])
                for cq in range(4):
                    stt(yatm[:, cq, :], PS[:, ob[cq], :], stat[:, 8 + cq:9 + cq], gsm[:, cq, :],
                        ALU.mult, ALU.mult, psk(ob[cq]) + [("stat", 2), ("gs", cq)], [("yatm", cq)])

            def B2(h):
                bt = ps2()
                pst = PS[:, bt:bt + 2, :].rearrange("p b n -> p (b n)").bitcast(BF16)
                pv = pst.rearrange("p (b j c n) -> p b j c n", b=2, j=4, c=4)
                items = []
                for j in range(4):
                    bb = PS[:, bt + j // 2, :].bitcast(BF16)
                    for cq in range(4):
                        o0 = (j % 2) * 512 + cq * 128
                        items.append((bb[:, o0:o0 + 128], yatm[:, cq, j * 128:(j + 1) * 128], identb[:, :]))
                tr_multi(items, [("yatm", cq) for cq in range(4)] + [("identb",)], bt, 2)
                for half in range(2):
                    bb = PS[:, bt + half, :].bitcast(BF16)
                    srcv = bb[:, 0:1024].rearrange("p (j n) -> p j n", j=2)
                    j0 = h * 4 + half * 2
                    grb = gret[:, j0:j0 + 2].unsqueeze(2).broadcast_to([128, 2, NS])
                    tt(A16[:, j0:j0 + 2, 0:NS], srcv, grb, ALU.mult,
                       psk(bt + half) + [("gcols",)], [("A16", j0 + j, 0) for j in range(2)])
                if last:
                    dst = s_p[h].rearrange("(dc p) v -> p dc v", p=128)
                    dma("sp", dst, S32[:, h, :, :], [("S32", h)], [("s_p_out",)], ("spout",))

            def sample_steps(h):
                def ptt(out, in0, in1, op, reads, writes):
                    S.add("pool", lambda e, out=out, in0=in0, in1=in1, op=op: e.tensor_tensor(out, in0, in1, op),
                          reads=reads, writes=writes)
                for dc in range(2):
                    ptt(qm[:, dc, :, :], qkT[:, dc, NS:NC].unsqueeze(1).broadcast_to([128, NSAMP, SC]),
                        qmaskb[:, :, :], ALU.mult, [("qk", dc, 1), ("qmaskb",)], [("qm", dc)])
                ptt(km[0:SC, :, :], ktm[0:SC, 4, :].unsqueeze(1).broadcast_to([SC, NSAMP, 256]),
                    kmask[0:SC, :].unsqueeze(2).broadcast_to([SC, NSAMP, 256]), ALU.mult,
                    [("ktm", 4), ("cm",)], [("km",)])

                def head():
                    bsc = ps1()
                    mm_group(PS[0:SC, bsc, 0:SC],
                             [(qkT[:, 2 + dc, NS:NC], qkT[:, dc, NS:NC]) for dc in range(2)],
                             [("qk", i, 1) for i in range(4)], bsc)
                    tt(scm[0:SC, 0, 0:SC], PS[0:SC, bsc, 0:SC], maskTs[0:SC, h, :], ALU.mult,
                       psk(bsc) + [("cm",)], [("scm", 0)])
                    S.add("pe", lambda pe: pe.matmul(PS[0:SC, SBANK, :], scm[0:SC, 0, 0:SC],
                                                     vtm[0:SC, 4, :], start=True, stop=False),
                          reads=[("scm", 0), ("vtm", 4)], writes=psk(SBANK))
                pending.append(head)

                def sload(i):
                    sl = i % 2
                    src = st_in[i, h].rearrange("(dc p) v -> p dc v", p=128)
                    dma("sp", Ss32[:, sl, :, :], src, [], [("Ss32", sl)], ("ssin", sl))
                    act(Ssb[:, sl, :, :], Ss32[:, sl, :, :], AF.Copy, [("Ss32", sl)], [("Ssb", sl)])

                def cross(i):
                    sl = i % 2
                    lastg = (i == NSAMP - 1)

                    def fn(pe, i=i, sl=sl, lastg=lastg):
                        r = None
                        for dc in range(2):
                            r = pe.matmul(PS[0:SC, SBANK, :], qm[:, dc, i, :], Ssb[:, sl, dc, :],
                                          start=False, stop=(lastg and dc == 1))
                        return r
                    S.add("pe", fn, reads=[("qm", 0), ("qm", 1), ("Ssb", sl)], writes=psk(SBANK), accum=True)

                def one(i):
                    sl = i % 2
                    if i == 0:
                        sload(0)
                    if i >= 1:
                        cross(i - 1)
                    if i + 1 < NSAMP:
                        sload(i + 1)
                    bu = ps2()
                    for dc in range(2):
                        mm_group(PS[:, bu + dc, :], [(km[0:SC, i, dc * 128:(dc + 1) * 128], vtm[0:SC, 4, :])],
                                 [("km",), ("vtm", 4)], bu + dc)
                    stt(Ss32[:, sl, :, :], Ss32[:, sl, :, :], chdec_s[h], PS[:, bu:bu + 2, :],
                        ALU.mult, ALU.add, psk(bu, 2) + [("Ss32", sl)], [("Ss32", sl)])
                    dst = s_s[i, h].rearrange("(dc p) v -> p dc v", p=128)
                    dma("sp", dst, Ss32[:, sl, :, :], [("Ss32", sl)], [("s_s_out", sl)], ("ssout", sl))
                for i in range(NSAMP):
                    pending.append(lambda i=i: one(i))

                def tail():
                    np_ = SC
                    cross(NSAMP - 1)
                    act(junkb[0:np_, 0:512], PS[0:np_, SBANK, :], AF.Square, psk(SBANK), [("junkb",), ("stat", "s0")],
                        accum_out=stat[0:np_, 40:41])
                    act(stat[0:np_, 41:42], stat[0:np_, 40:41], AF.Sqrt, [("stat", "s0")], [("stat", "s1")],
                        scale=1.0 / DV, bias=EPS)
                    recip(stat[0:np_, 42:43], stat[0:np_, 41:42], [("stat", "s1")], [("stat", "s2")])
                    stt(yatm[0:np_, 0, :], PS[0:np_, SBANK, :], stat[0:np_, 42:43], gsm[0:np_, 4, :],
                        ALU.mult, ALU.mult, psk(SBANK) + [("stat", "s2"), ("gs", 4)], [("yatm", 0)])
                    bt = ps1()
                    pst = PS[:, bt, :].bitcast(BF16)
                    tr_multi([(pst[:, j * 128:j * 128 + np_], yatm[0:np_, 0, j * 128:(j + 1) * 128],
                               identb[0:np_, 0:np_]) for j in range(4)],
                             [("yatm", 0), ("identb",)], bt)
                    srcv = pst[:, 0:512].rearrange("p (j n) -> p j n", j=4)[:, :, 0:np_]
                    grb = gret[:, h * 4:(h + 1) * 4].unsqueeze(2).broadcast_to([128, 4, np_])
                    tt(A16[:, h * 4:(h + 1) * 4, NS:NC], srcv, grb, ALU.mult,
                       psk(bt) + [("gcols",)], [("A16", h * 4 + j, 1) for j in range(4)])
                pending.append(tail)

            horder = [(p + i) % H_A for i in range(H_A)]
            BST = os.environ.get("MK_BSTOP", "")
            for hi, h in enumerate(horder):
                ws_ = (hi == 0)
                if hi == 0:
                    F1(h, ws_)
                if BST == "F1":
                    break
                F2(h, ws_)
                if BST == "F2":
                    break
                if ws_ and BST != "nosamp":
                    sample_steps(h)
                if BST == "samp":
                    break
                B1(h)
                if BST == "B1":
                    break
                if hi + 1 < H_A:
                    F1(horder[hi + 1], False)
                B2(h)
                if BST == "B2":
                    break
            while pending:
                pump(force=True)
            if p == 0:
                dump("yaT0", A16[:, :, 0:NS], [128, 16, NS], [("A16", k, 0) for k in range(16)], bf=True)

            if STOP == "B":
                break
            S.alias(HB_C_KEYS, HB_B_KEYS)
            s0, k0_ = wslab(w_in_v, 7168)
            s1, k1_ = wslab(w_in_v, 7168 + 512)
            for (c, c0, n) in chk:
                g = 0 if c < 4 else 1
                b = ps2("D")
                for s_, (sl_, sk_) in enumerate(((s0, k0_), (s1, k1_))):
                    mm_group(PS[0:n, b + s_, :], [(nT[:, k, c0:c0 + n], sl_[:, k, :]) for k in range(8)],
                             NTK(g) + [sk_], b + s_)
                tsl = c % 2
                tg = T4[0:n, tsl, :]
                act(tg, PS[0:n, b:b + 2, :].rearrange("p b n -> p (b n)"), AF.Gelu, psk(b, 2), t4k(tsl))
                S.add("dve", lambda e, tg=tg, n=n: e.bn_stats(stat[0:n, 8:14], tg[:, 0:512]),
                      reads=t4k(tsl), writes=[("stat", "bs0")])
                S.add("dve", lambda e, tg=tg, n=n: e.bn_stats(stat[0:n, 14:20], tg[:, 512:1024]),
                      reads=t4k(tsl), writes=[("stat", "bs1")])
                S.add("dve", lambda e, n=n: e.bn_aggr(stat[0:n, 20:22], stat[0:n, 8:20]),
                      reads=[("stat", "bs0"), ("stat", "bs1")], writes=[("stat", "mv")])
                act(stat[0:n, 22:23], stat[0:n, 21:22], AF.Sqrt, [("stat", "mv")], [("stat", "sd")], bias=EPS)
                recip(stat[0:n, 23:24], stat[0:n, 22:23], [("stat", "sd")], [("stat", "rs")])
                stt(tg, tg, stat[0:n, 20:21], ggm[0:n, :], ALU.subtract, ALU.mult,
                    t4k(tsl) + [("stat", "mv"), ("ggm",)], t4k(tsl))
                if c < 4:
                    ts(gvtm[0:n, c, :], tg, stat[0:n, 23:24], None, ALU.mult, None,
                       t4k(tsl) + [("stat", "rs")], [("gv", c)])
                else:
                    ts(tg, tg, stat[0:n, 23:24], None, ALU.mult, None,
                       t4k(tsl) + [("stat", "rs")], t4k(tsl))
                    dma("sp", gv_s[:, :], tg, t4k(tsl), [("gv_s_out",)], ("gvout",))
                    dcopy(gvtm[0:n, c, :], tg, t4k(tsl), [("gv", c)])
            S.alias(A8A_KEYS, [("km",)])
            for s_ in range(2):
                sl_, sk_ = wslab(w_in_v, 6144 + s_ * 512)
                for ft in range(4):
                    for (g, c0, n) in grp:
                        b = ps1("D")
                        fm_proj(sl_, sk_, (ft * 128, ft * 128 + 128), g, c0, n, b)
                        act(A8a[:, s_ * 4 + ft, c0:c0 + n], PS[:, b, 0:n], AF.Gelu, psk(b),
                            [("A8a", s_ * 4 + ft, g)])
            S.alias(A8B_KEYS, CSK)
            for (c, c0, n) in chk:
                g = 0 if c < 4 else 1
                b = ps2("D")
                items = []
                for gg in range(8):
                    o_ = PS[:, b + gg // 4, (gg % 4) * 128:(gg % 4) * 128 + n]
                    if c < 4:
                        items.append((o_, gvtm[:, c, gg * 128:(gg + 1) * 128], wsTb[:, gg, :]))
                    else:
                        items.append((o_, gvtm[0:SC, c, gg * 128:(gg + 1) * 128], wsTsb[0:SC, gg, :]))
                mm_multi(items, [("gv", c), ("wsTb",), ("wsTsb",)], b, 2)
                srcv = PS[:, b:b + 2, :].rearrange("p b (j n) -> p (b j) n", j=4)[:, :, 0:n]
                tsl = 2 + c % 2
                tmp = T4[:, tsl, 0:8 * n].rearrange("p (g n) -> p g n", g=8)
                bias_ = bsr[:, :, :] if c < 4 else bsrs[:, :, :]
                tt(tmp, srcv, bias_, ALU.add, psk(b, 2) + [("bsr",), ("bsrs",)], t4k(tsl))
                tt(A8b[:, :, c0:c0 + n], tmp, A8a[:, :, c0:c0 + n], ALU.mult,
                   t4k(tsl) + [("A8a", k, g) for k in range(8)], [("A8b", k, g) for k in range(8)])
            if p == 0:
                dump("ybT0", A8b[:, :, 0:NS], [128, 8, NS], [("A8b", k, 0) for k in range(8)], bf=True)

            if STOP == "C":
                break
            S.alias(HB_D_KEYS, HB_C_KEYS)
            for which, (buf, nm) in enumerate(((sga, "sga"), (sgb, "sgb"))):
                for s_ in range(2):
                    sl_, sk_ = wslab(w_in_v, 8192 + which * 1024 + s_ * 512)
                    for ft in range(4):
                        for (g, c0, n) in grp:
                            b = ps1("D")
                            fm_proj(sl_, sk_, (ft * 128, ft * 128 + 128), g, c0, n, b)
                            act(buf[:, s_ * 4 + ft, c0:c0 + n], PS[:, b, 0:n], AF.Sigmoid, psk(b),
                                [(nm, s_ * 4 + ft, g)])
            for hf in range(2):
                wb_, kb_ = wslab(w_br_b_v, hf * 512)
                for pr in range(2):
                    wa_, ka_ = wslab(w_br_a_v, hf * 512 + pr * 256, ncols=256, k=16)
                    for fl in range(2):
                        f_ = hf * 4 + pr * 2 + fl
                        for (g, c0, n) in grp:
                            b = ps2("D")
                            mm_group(PS[:, b, 0:n],
                                     [(wa_[:, k, fl * 128:(fl + 1) * 128], A16[:, k, c0:c0 + n]) for k in range(16)],
                                     [ka_] + [("A16", k, g) for k in range(16)], b)
                            fcol = (pr * 2 + fl) * 128
                            mm_group(PS[:, b + 1, 0:n],
                                     [(wb_[:, k, fcol:fcol + 128], A8b[:, k, c0:c0 + n]) for k in range(8)],
                                     [kb_] + [("A8b", k, g) for k in range(8)], b + 1)
                            m1 = T4[:, 0, 0:n]; m2 = T4[:, 1, 0:n]
                            tt(m1, PS[:, b, 0:n], sga[:, f_, c0:c0 + n], ALU.mult,
                               psk(b) + [("sga", f_, g)], [("T4", 0)])
                            tt(m2, PS[:, b + 1, 0:n], sgb[:, f_, c0:c0 + n], ALU.mult,
                               psk(b + 1) + [("sgb", f_, g)], [("T4", 1)])
                            tt(A8a[:, f_, c0:c0 + n], m1, m2, ALU.add, [("T4", 0), ("T4", 1)],
                               [("A8a", f_, g)])
            if p == 0:
                dump("mrgT0", A8a[:, :, 0:NS], [128, 8, NS], [("A8a", k, 0) for k in range(8)], bf=True)
            for s_ in range(2):
                sl_, sk_ = wslab(w_o_v, s_ * 512)
                for ft in range(4):
                    f_ = s_ * 4 + ft
                    for (g, c0, n) in grp:
                        b = ps1("D")
                        fm_proj(sl_, sk_, (ft * 128, ft * 128 + 128), g, c0, n, b,
                                rhs_buf=A8a, rkeys=[("A8a", k, g) for k in range(8)])
                        tt(hT[:, f_, c0:c0 + n], PS[:, b, 0:n], hT[:, f_, c0:c0 + n], ALU.add,
                           psk(b) + [("hT", f_, g)], [("hT", f_, g)])
            if p == 0:
                dump("hT1", hT[:, :, 0:NS], [128, 8, NS], hkeys(0))

            if STOP == "D":
                break
            rmsnorm_fm(gmlp, last)
            for ffg in range(2):
                for s_ in range(4):
                    sl_, sk_ = wslab(w_up_v, ffg * 2048 + s_ * 512)
                    for ft in range(4):
                        kk = s_ * 4 + ft
                        for (g, c0, n) in grp:
                            b = ps1("D")
                            fm_proj(sl_, sk_, (ft * 128, ft * 128 + 128), g, c0, n, b)
                            tsl = kk % 2
                            r_ = T4[:, tsl, 0:n]
                            act(r_, PS[:, b, 0:n], AF.Relu, psk(b), [("T4", tsl)])
                            tt(A16[:, kk, c0:c0 + n], r_, r_, ALU.mult, [("T4", tsl)], [("A16", kk, g)])
                for cq in range(4):
                    sl_, sk_ = wslab(w_down_v, cq * 256, ncols=256, k0=ffg * 16, k=16)
                    for fl in range(2):
                        f_ = cq * 2 + fl
                        for (g, c0, n) in grp:
                            b = ps1("D")
                            mm_group(PS[:, b, 0:n],
                                     [(sl_[:, k, fl * 128:(fl + 1) * 128], A16[:, k, c0:c0 + n]) for k in range(16)],
                                     [sk_] + [("A16", k, g) for k in range(16)], b)
                            tt(hT[:, f_, c0:c0 + n], PS[:, b, 0:n], hT[:, f_, c0:c0 + n], ALU.add,
                               psk(b) + [("hT", f_, g)], [("hT", f_, g)])
            if p == 0:
                dump("hT2", hT[:, :, 0:NS], [128, 8, NS], hkeys(0))

            rmsnorm_fm(gple, last)
            for (c, c0, n) in chk:
                slot = c % 2
                src = pp[t0 + c0:t0 + c0 + n, :] if c < 4 else psm[:, :]
                dma("sp", T4[0:n, slot, 0:256], src, [], [("T4", slot)], ("xs", slot))
                b = ps1("D")
                tr_multi([(PS[:, b, dc * 128:dc * 128 + n], T4[0:n, slot, dc * 128:(dc + 1) * 128],
                           identf[0:n, 0:n]) for dc in range(2)], [("T4", slot), ("cm",)], b)
                srcv = PS[:, b, 0:256].rearrange("p (j n) -> p j n", j=2)[:, :, 0:n]
                act(pT[:, :, c0:c0 + n], srcv, AF.Copy, psk(b), [("pT", 0 if c < 4 else 1)])
            if p + 1 < npass:
                for c in range(2):
                    t1_ = (p + 1) * NS + c * 128
                    dma("sp", T4[0:128, c, :], xp[t1_:t1_ + 128, :], [], t4k(c), ("xs", c))
            wpp_, kpp_ = load_slab([((0, 1024), w_pp_v[:, :, :])], (2, 1024))
            for s_ in range(2):
                sl_, sk_ = wslab(w_pg_v, s_ * 512)
                for ft in range(4):
                    f_ = s_ * 4 + ft
                    for (g, c0, n) in grp:
                        b = ps2("D")
                        fm_proj(sl_, sk_, (ft * 128, ft * 128 + 128), g, c0, n, b)
                        mm_group(PS[:, b + 1, 0:n],
                                 [(wpp_[:, k, f_ * 128:(f_ + 1) * 128], pT[:, k, c0:c0 + n]) for k in range(2)],
                                 [kpp_, ("pT", g)], b + 1)
                        sg_ = T4[:, 2, 0:n]
                        act(sg_, PS[:, b, 0:n], AF.Sigmoid, psk(b), [("T4", 2)])
                        tt(sg_, PS[:, b + 1, 0:n], sg_, ALU.mult, psk(b + 1) + [("T4", 2)], [("T4", 2)])
                        tt(hT[:, f_, c0:c0 + n], sg_, hT[:, f_, c0:c0 + n], ALU.add,
                           [("T4", 2), ("hT", f_, g)], [("hT", f_, g)])

            for (c, c0, n) in chk:
                g = 0 if c < 4 else 1
                b = ps2("D")
                tr_multi([(PS[0:n, b + k // 4, (k % 4) * 128:(k % 4 + 1) * 128], hT[:, k, c0:c0 + n], identf)
                          for k in range(8)], hkeys(g) + [("cm",)], b, 2)
                hv = PS[0:n, b:b + 2, :].rearrange("p b n -> p (b n)")
                act(A8b[0:n, 0:2, 0:512], PS[0:n, b:b + 2, :], AF.Square, psk(b, 2),
                    [("A8b", 0, 0), ("A8b", 1, 0), ("stat", "f0")], accum_out=stat[0:n, 32:33])
                act(stat[0:n, 33:34], stat[0:n, 32:33], AF.Sqrt, [("stat", "f0")], [("stat", "f1")],
                    scale=1.0 / D, bias=EPS)
                recip(stat[0:n, 34:35], stat[0:n, 33:34], [("stat", "f1")], [("stat", "f2")])
                slot = 2 + c % 2
                stt(T4[0:n, slot, :], hv, stat[0:n, 34:35], gfin[0:n, :], ALU.mult, ALU.mult,
                    psk(b, 2) + [("stat", "f2"), ("gfin",)], t4k(slot))
                dst = y_p[t0 + c0:t0 + c0 + n, :] if c < 4 else y_s[:, :]
                dma("sp", dst, T4[0:n, slot, :], t4k(slot), [("y_out", slot)], ("yout", slot))

        while pend_st:
            flush_store()
        eng_sems = {}
        for e in ("pe", "act", "dve", "pool"):
            eng_sems[e] = es.enter_context(nc.semaphore("sem_" + e))
        slot_sems = {}
        for i, s_ in enumerate(sorted(S.slot_cnt.keys(), key=str)):
            slot_sems[s_] = es.enter_context(nc.semaphore("dsem_%d" % i))
        final_slots = [s_ for s_ in S.slot_cnt if s_[0] in ("yout", "ssout", "spout", "gvout", "dbg")]
        S.emit(nc, eng_sems, slot_sems, final_slots)
    return nc, dbg_outs


_CACHE = {}


def kernel(x_prompt, x_sample, p_prompt, p_sample, state_ret, g_mix, w_in, g_ret, w_s, b_s,
           g_gm, w_br_a, w_br_b, w_o, g_mlp, w_up, w_down, g_ple, w_pg, w_pp, g_final):
    f = np.float32
    A = lambda a: np.ascontiguousarray(np.asarray(a), dtype=f)
    cosT, sinT, cm, chdec, chdec_s = _host_consts()
    cm_off, cm_w = _cm_layout(cm)
    cmisc = np.concatenate([cm[k] for k in CM_ORDER], axis=1).astype(f)

    npass = DBG_PASSES if DEBUG else NPASS
    nc, dbg = build_program(cm_off, cm_w, chdec, chdec_s, npass=npass)

    def col(gv, n):
        return np.asarray(gv, dtype=f).reshape(n, 128).T
    gcols = np.concatenate([col(g_mix[0], 8), col(g_mlp[0], 8), col(g_ple[0], 8), col(g_ret[0], 16)], axis=1)
    ggm_rep = np.broadcast_to(np.asarray(g_gm[0], dtype=f)[None, :], (128, D))
    gfin_rep = np.broadcast_to(np.asarray(g_final, dtype=f)[None, :], (128, D))
    ws = np.asarray(w_s[0], dtype=f)
    wsT = np.transpose(ws, (2, 0, 1)).reshape(128, 8 * 128)
    w4 = np.transpose(ws[:, :4, :4], (2, 0, 1))
    wsTs = np.zeros((128, 8, SC), f)
    wsTs[:SC] = np.tile(w4, (NSAMP, 1, NSAMP))
    bs = np.asarray(b_s[0], dtype=f)
    bs_rep = np.broadcast_to(bs[None, :, :], (128, 8, 128)).reshape(128, -1)
    bs_rep_s = np.broadcast_to(np.tile(bs[:, :4], (1, NSAMP))[None], (128, 8, SC)).reshape(128, -1)

    shared = {
        "w_in": A(w_in[0]), "w_br_a": A(w_br_a[0]), "w_br_b": A(w_br_b[0]), "w_o": A(w_o[0]),
        "w_up": A(w_up[0]), "w_down": A(w_down[0]), "w_pg": A(w_pg[0]), "w_pp": A(w_pp[0]),
        "gcols": A(gcols), "ggm_rep": A(ggm_rep), "gfin_rep": A(gfin_rep),
        "wsT": A(wsT), "wsTs": A(wsTs.reshape(128, -1)), "bs_rep": A(bs_rep), "bs_rep_s": A(bs_rep_s),
        "cosT": A(cosT), "sinT": A(sinT), "cmisc": A(cmisc),
        "qmask": A(cm["qmask"]), "tril": A(cm["tril"]),
    }
    xpn = np.asarray(x_prompt); xsn = np.asarray(x_sample)
    ppn = np.asarray(p_prompt); psn = np.asarray(p_sample); stn = np.asarray(state_ret)
    in_maps = []
    for c in range(N_CORES):
        m = dict(shared)
        m["xp"] = A(xpn[c]); m["pp"] = A(ppn[0, c])
        m["xs"] = A(xsn[c * NSAMP:(c + 1) * NSAMP].reshape(SC, D))
        m["ps"] = A(psn[0, c * NSAMP:(c + 1) * NSAMP].reshape(SC, 256))
        m["st"] = A(stn[0, c * NSAMP:(c + 1) * NSAMP])
        in_maps.append(m)
    res = run_bass_kernel_spmd(nc, in_maps, core_ids=list(range(N_CORES)))
    R = res.results
    y_prompt = np.stack([R[c]["y_p"] for c in range(N_CORES)], axis=0).astype(f)
    y_sample = np.concatenate([R[c]["y_s"].reshape(NSAMP, 4, D) for c in range(N_CORES)], axis=0).astype(f)
    s_prompt = np.stack([R[c]["s_p"] for c in range(N_CORES)], axis=0)[None].astype(f)
    s_sample = np.concatenate([R[c]["s_s"] for c in range(N_CORES)], axis=0)[None].astype(f)
    gv_sample = np.concatenate([R[c]["gv_s"].reshape(NSAMP, 4, D) for c in range(N_CORES)], axis=0)[None].astype(f)
    if DEBUG:
        _CACHE["dbg"] = {k: [R[c]["dbg_" + k] for c in range(N_CORES)] for k in dbg}
    return (y_prompt, y_sample, s_prompt, s_sample, gv_sample)
```

```python
import os
import math
import numpy as np
import concourse.bass as bass
import concourse.mybir as mybir
from concourse.bass_utils import run_bass_kernel_spmd

F32 = mybir.dt.float32
BF16 = mybir.dt.bfloat16
AF = mybir.ActivationFunctionType
ALU = mybir.AluOpType

N_CORES = 8
D = 1024
SEQ = 2048
NS = 512
NPASS = SEQ // NS
NSAMP = 16
SC = 64
NC = NS + SC
H_A, DK, DV = 4, 256, 512
PAST = 16384
EPS = 1e-6
R_SLAB = 4
SAME_ENGINE_ALL = bool(int(os.environ.get("MK_SEA", "0")))
USE_WSCR = True
DEBUG = bool(int(os.environ.get("MK_DEBUG", "0")))
DBG_PASSES = int(os.environ.get("MK_PASSES", str(NPASS)))
STOP = os.environ.get("MK_STOP", "")


class Op:
    __slots__ = ("eng", "fn", "idx", "waits", "signal", "val", "dma", "slot", "ninc", "small")

    def __init__(self, eng, fn, idx):
        self.eng = eng
        self.fn = fn
        self.idx = idx
        self.waits = []
        self.signal = False
        self.val = None
        self.dma = False
        self.slot = None
        self.ninc = 1
        self.small = False


class Sched:
    ENGS = ("pe", "act", "dve", "pool", "sp")

    def __init__(self):
        self.ops = {e: [] for e in self.ENGS}
        self.state = {}
        self.slot_cnt = {}
        self.last_dma = {}

    def _st(self, k):
        st = self.state.get(k)
        if st is None:
            st = [None, []]
            self.state[k] = st
        return st

    def add(self, eng, fn, reads=(), writes=(), dma_slot=None, ninc=1, accum=False):
        op = Op(eng, fn, len(self.ops[eng]))
        op.small = any(k[0] == "stat" for k in writes)
        deps = []
        if eng == "pe" and not accum:
            for k in writes:
                st = self.state.get(k)
                if k[0] == "ps" and st is not None and st[0] is not None and st[0].eng == "pe" and not st[1]:
                    raise AssertionError("PSUM bank %s overwritten by PE before its result was read" % (k,))
        for k in reads:
            st = self._st(k)
            if st[0] is not None:
                deps.append((st[0], True))
        for k in writes:
            st = self._st(k)
            if st[0] is not None:
                deps.append((st[0], False))
            for r in st[1]:
                deps.append((r, False))
        best = {}
        for d, raw in deps:
            if d.dma or d.eng == eng:
                continue
            b_ = best.get(d.eng)
            if b_ is None or d.idx > b_.idx:
                best[d.eng] = d
        deps = [(d, raw) for d, raw in deps if d.dma or d.eng == eng or best[d.eng] is d]
        seen = set()
        for d, raw in deps:
            if d is op or id(d) in seen:
                continue
            if (not d.dma) and d.eng == eng:
                if eng == "pe":
                    continue
                if not SAME_ENGINE_ALL and not (raw and d.small):
                    continue
            seen.add(id(d))
            op.waits.append(d)
            d.signal = True
        for k in reads:
            self._st(k)[1].append(op)
        for k in writes:
            st = self._st(k)
            st[0] = op
            st[1] = []
        if dma_slot is not None:
            op.dma = True
            op.slot = dma_slot
            op.ninc = ninc
            n = self.slot_cnt.get(dma_slot, 0) + ninc
            self.slot_cnt[dma_slot] = n
            op.val = 16 * n
            self.last_dma[dma_slot] = op
        self.ops[eng].append(op)
        return op

    def alias(self, new_keys, old_keys):
        best = {}
        dmas = {}
        for k in old_keys:
            st = self.state.get(k)
            if st is None:
                continue
            cands = list(st[1])
            if st[0] is not None:
                cands.append(st[0])
            for o in cands:
                if o.dma:
                    dmas[id(o)] = o
                else:
                    b = best.get(o.eng)
                    if b is None or o.idx > b.idx:
                        best[o.eng] = o
            self.state[k] = [None, []]
        fence = list(best.values()) + list(dmas.values())
        for k in new_keys:
            st = self._st(k)
            have = {id(o) for o in st[1]}
            st[1] = list(st[1]) + [o for o in fence if id(o) not in have]

    def emit(self, nc, eng_sems, slot_sems, final_slots):
        for e in ("pe", "act", "dve", "pool"):
            c = 0
            for op in self.ops[e]:
                if op.dma:
                    continue
                if op.signal:
                    c += 1
                    op.val = c
        fin = Op("sp", None, len(self.ops["sp"]))
        for s in final_slots:
            if s in self.last_dma:
                fin.waits.append(self.last_dma[s])
        self.ops["sp"].append(fin)

        def sem_of(d):
            return slot_sems[d.slot] if d.dma else eng_sems[d.eng]

        def make(e):
            def body(eng):
                waited = {}
                for op in self.ops[e]:
                    for d in op.waits:
                        key = ("s", d.slot) if d.dma else ("e", d.eng)
                        if waited.get(key, 0) >= d.val:
                            continue
                        eng.wait_ge(sem_of(d), d.val)
                        waited[key] = d.val
                    if op.fn is None:
                        continue
                    inst = op.fn(eng)
                    if op.dma:
                        insts = inst if isinstance(inst, (list, tuple)) else [inst]
                        assert len(insts) == op.ninc
                        for i_ in insts:
                            i_.then_inc(slot_sems[op.slot], 16)
                    elif op.signal:
                        inst.then_inc(eng_sems[e], 1)
            return body

        with nc.Block() as block:
            block.tensor(make("pe"))
            block.scalar(make("act"))
            block.vector(make("dve"))
            block.gpsimd(make("pool"))
            block.sync(make("sp"))


def _host_consts():
    f = np.float32
    half = DK // 2
    inv = (np.float32(10000.0) ** (-np.arange(half, dtype=f) / np.float32(half))).astype(f)
    pos_p = np.arange(SEQ, dtype=f)
    pos_s = (np.float32(PAST) + np.arange(4, dtype=f)).astype(f)
    pos_s = np.tile(pos_s, NSAMP)
    pos = np.concatenate([pos_p, pos_s]).astype(f)
    ang = (pos[None, :] * inv[:, None]).astype(f)
    cosT = np.cos(ang.astype(np.float64)).astype(f)
    sinT = np.sin(ang.astype(np.float64)).astype(f)

    lg = np.log(1.0 - 2.0 ** (-5.0 - np.arange(H_A, dtype=np.float64)))
    idx = np.arange(128, dtype=np.float64)
    js = (np.arange(SC) % 4).astype(np.float64)
    smp = np.arange(SC) // 4

    cm = {}
    cm["identf"] = np.eye(128, dtype=f)
    cm["tril"] = (idx[:, None] <= idx[None, :]).astype(f)
    maskTs = np.zeros((128, H_A, SC), f)
    qdec = np.zeros((128, H_A, NS), f)
    qdec_s = np.zeros((128, H_A, SC), f)
    stdec = np.zeros((128, H_A * 4), f)
    rowsc = np.zeros((128, H_A * 4), f)
    stdec_s = np.zeros((128, H_A), f)
    tl = np.arange(NS, dtype=np.float64)
    for h in range(H_A):
        ms = ((smp[None, :] == smp[:, None]) & (js[None, :] >= js[:, None])) * \
            np.exp(-(js[:, None] + 1.0) * lg[h]) / 16.0
        maskTs[:SC, h, :] = ms.astype(f)
        qdec[:, h, :] = np.exp((tl[None, :] + 1.0) * lg[h]).astype(f)
        qdec_s[:, h, :] = np.exp((js[None, :] + 1.0) * lg[h]).astype(f)
        for ck in range(4):
            kl = ck * 128 + idx
            stdec[:, h * 4 + ck] = (np.exp((NS - 1.0 - kl) * lg[h]) / 16.0).astype(f)
            rowsc[:, h * 4 + ck] = (np.exp(-(kl + 1.0) * lg[h]) / 16.0).astype(f)
        stdec_s[:SC, h] = (np.exp((3.0 - js) * lg[h]) / 16.0).astype(f)
    cmask = np.ones((128, NS), f)
    cmask[:, :128] = (idx[None, :] >= idx[:, None]).astype(f)
    cm["cmask"] = cmask
    cm["rowsc"] = rowsc
    cm["maskTs"] = maskTs.reshape(128, -1)
    cm["qdec"] = qdec.reshape(128, -1)
    cm["qdec_s"] = qdec_s.reshape(128, -1)
    cm["stdec"] = stdec
    cm["stdec_s"] = stdec_s
    qmask = np.zeros((128, NSAMP, SC), f)
    for i in range(NSAMP):
        qmask[:, i, 4 * i:4 * i + 4] = 1.0
    cm["qmask"] = qmask.reshape(128, -1)
    kmask = np.zeros((128, NSAMP), f)
    for i in range(NSAMP):
        kmask[4 * i:4 * i + 4, i] = 1.0
    cm["kmask"] = kmask
    bms = np.zeros((128, SC), f)
    bms[:SC, :] = ((smp[:, None] == smp[None, :]) & (js[:, None] <= js[None, :])).astype(f)
    cm["bms"] = bms
    chdec = [float(np.exp(float(NS) * lg[h])) for h in range(H_A)]
    chdec_s = [float(np.exp(4.0 * lg[h])) for h in range(H_A)]
    return cosT, sinT, cm, chdec, chdec_s


CM_ORDER = ["identf", "cmask", "rowsc", "maskTs", "qdec", "qdec_s", "stdec", "stdec_s",
            "kmask", "bms"]


def _cm_layout(cm):
    off = {}
    o = 0
    for k in CM_ORDER:
        w = cm[k].shape[1]
        off[k] = (o, w)
        o += w
    return off, o


def build_program(cm_off, cm_w, chdec, chdec_s, npass=NPASS):
    nc = bass.Bass("TRN2", target_bir_lowering=False)
    S = Sched()

    def din(name, shape, dt=F32):
        return nc.dram_tensor(name, list(shape), dt, kind="ExternalInput").ap()

    def dout(name, shape, dt=F32):
        return nc.dram_tensor(name, list(shape), dt, kind="ExternalOutput").ap()

    xp = din("xp", [SEQ, D]); pp = din("pp", [SEQ, 256])
    xs = din("xs", [SC, D]); psm = din("ps", [SC, 256])
    st_in = din("st", [NSAMP, H_A, DK, DV])
    w_in = din("w_in", [D, 10240]); w_br_a = din("w_br_a", [2048, D]); w_br_b = din("w_br_b", [D, D])
    w_o = din("w_o", [D, D]); w_up = din("w_up", [D, 4096]); w_down = din("w_down", [4096, D])
    w_pg = din("w_pg", [D, D]); w_pp = din("w_pp", [256, D])
    gcols_d = din("gcols", [128, 40])
    ggm_d = din("ggm_rep", [128, D]); gfin_d = din("gfin_rep", [128, D])
    wsT_d = din("wsT", [128, 8 * 128]); wsTs_d = din("wsTs", [128, 8 * SC])
    bsr_d = din("bs_rep", [128, 8 * 128]); bsrs_d = din("bs_rep_s", [128, 8 * SC])
    cos_d = din("cosT", [128, SEQ + SC]); sin_d = din("sinT", [128, SEQ + SC])
    cm_d = din("cmisc", [128, cm_w])
    qmask_d = din("qmask", [128, NSAMP * SC])
    tril_d = din("tril", [128, 128])

    y_p = dout("y_p", [SEQ, D]); y_s = dout("y_s", [SC, D])
    s_p = dout("s_p", [H_A, DK, DV]); s_s = dout("s_s", [NSAMP, H_A, DK, DV])
    gv_s = dout("gv_s", [SC, D])

    dbg_outs = {}
    NSLABS = 47
    wscr = nc.dram_tensor("wscr", [NSLABS, 128, 4096], BF16, kind="Internal").ap()

    def kcv(w):
        return w.rearrange("(kc p) f -> p kc f", p=128)

    w_in_v = kcv(w_in); w_br_a_v = kcv(w_br_a); w_br_b_v = kcv(w_br_b); w_o_v = kcv(w_o)
    w_up_v = kcv(w_up); w_down_v = kcv(w_down); w_pg_v = kcv(w_pg); w_pp_v = kcv(w_pp)

    import contextlib
    es = contextlib.ExitStack()
    with es:
        def sb(name, shape, dt):
            return es.enter_context(nc.sbuf_tensor(name, list(shape), dt))

        hT = sb("hT", [128, 8, NC], F32)
        nT = sb("nT", [128, 8, NC], BF16)
        A16 = sb("A16", [128, 16, NC], BF16)
        A8a = sb("A8a", [128, 8, NC], BF16)
        A8b = sb("A8b", [128, 8, NC], BF16)
        HB = sb("HB", [128, 9216], BF16)
        S32 = sb("S32", [128, H_A, 2, 512], F32)
        Sbf = sb("Sbf", [128, 2, 512], BF16)
        Ss32 = sb("Ss32", [128, 2, 2, 512], F32)
        Ssb = sb("Ssb", [128, 2, 2, 512], BF16)
        SL = sb("SL", [128, R_SLAB, 4096], BF16)
        T4 = sb("T4", [128, 4, 1024], F32)
        pT = sb("pT", [128, 2, NC], BF16)
        qm = sb("qm", [128, 2, NSAMP, SC], BF16)
        scm = sb("scm", [128, 4, 512], BF16)
        yatm = sb("yatm", [128, 4, 512], BF16)
        scm_s = sb("scm_s", [128, SC], BF16)
        junkb = sb("junkb", [128, 512], BF16)
        stat = sb("stat", [128, 64], F32)
        cmisc = sb("cmisc_sb", [128, cm_w], F32)
        identb = sb("identb", [128, 128], BF16)
        onesb = sb("onesb", [128, 128], BF16)
        qmaskb = sb("qmaskb", [128, NSAMP, SC], BF16)
        gcols = sb("gcols_sb", [128, 40], F32)
        ggm = sb("ggm_sb", [128, D], F32)
        gfin = sb("gfin_sb", [128, D], F32)
        wsTb = sb("wsTb", [128, 8, 128], BF16)
        wsTsb = sb("wsTsb", [128, 8, SC], BF16)
        bsr = sb("bsr", [128, 8, 128], F32)
        bsrs = sb("bsrs", [128, 8, SC], F32)
        PS = es.enter_context(nc.psum_tensor("PS", [128, 8, 512], F32))

        def cmv(name):
            o, w = cm_off[name]
            return cmisc[:, o:o + w]

        identf = cmv("identf")
        cmask = cmv("cmask")
        rowsc = cmv("rowsc")
        maskTs = cmv("maskTs").rearrange("p (h n) -> p h n", h=H_A)
        qdec = cmv("qdec").rearrange("p (h n) -> p h n", h=H_A)
        qdec_s = cmv("qdec_s").rearrange("p (h n) -> p h n", h=H_A)
        stdec = cmv("stdec"); stdec_s = cmv("stdec_s")
        kmask = cmv("kmask")

        cosv = A8b[:, 0:4, :].bitcast(F32)
        A8b_f = A8b[:, :, :].rearrange("p k n -> p (k n)").bitcast(F32)
        cosT = A8b_f[:, 0:NC]
        sinT = A8b_f[:, NC:2 * NC]
        qkT = HB[:, 0:4 * NC].rearrange("p (k n) -> p k n", k=4)
        o1 = 4 * NC
        ktm = HB[:, o1:o1 + 5 * 256].rearrange("p (c n) -> p c n", c=5)
        o2 = o1 + 5 * 256
        vtm = HB[:, o2:o2 + 5 * 512].rearrange("p (c n) -> p c n", c=5)
        o3 = o2 + 5 * 512
        gsm = HB[:, o3:o3 + 5 * 512].rearrange("p (c n) -> p c n", c=5)
        assert o3 + 5 * 512 <= 9216
        gvtm = HB[:, 0:5 * 1024].rearrange("p (c n) -> p c n", c=5)
        sga = HB[:, 0:8 * NC].rearrange("p (k n) -> p k n", k=8)
        sgb = HB[:, 8 * NC:16 * NC].rearrange("p (k n) -> p k n", k=8)
        km = A8a[:, :, :].rearrange("p k n -> p (k n)")[:, 0:NSAMP * 256].rearrange("p (i n) -> p i n", i=NSAMP)

        HB_B_KEYS = [("qk", i, g) for i in range(4) for g in range(2)] + \
                    [(n, c) for n in ("ktm", "vtm", "gs") for c in range(5)]
        HB_C_KEYS = [("gv", c) for c in range(5)]
        HB_D_KEYS = [(n, f_, g) for n in ("sga", "sgb") for f_ in range(8) for g in range(2)]
        A8A_KEYS = [("A8a", k, g) for k in range(8) for g in range(2)]
        A8B_KEYS = [("A8b", k, g) for k in range(8) for g in range(2)]

        ring_cnt = {"s": 0, "p": 0}
        SBANK = 7

        def ps1(ring=None):
            b = ring_cnt["s"] % 7
            ring_cnt["s"] += 1
            return b

        def ps2(ring=None):
            b = 2 * (ring_cnt["p"] % 3)
            ring_cnt["p"] += 1
            return b

        def psk(b, n=1):
            return [("ps", b + i) for i in range(n)]

        def NTK(g):
            return [("nT", k_, g) for k_ in range(8)]

        fine_next = [False]

        def t4k(sl):
            return [("T4", sl), ("T4", sl, "b")]

        def mm_group(out_ap, pairs, reads, bank, nb=1, transpose=False, ident=None):
            pairs = list(pairs)

            def fn(pe, pairs=pairs, out_ap=out_ap):
                last = None
                n = len(pairs)
                for i, (l, r) in enumerate(pairs):
                    last = pe.matmul(out_ap, l, r, start=(i == 0), stop=(i == n - 1))
                return last
            return S.add("pe", fn, reads=reads, writes=psk(bank, nb))

        def mm_multi(items, reads, bank, nb=1):
            items = list(items)

            def fn(pe, items=items):
                last = None
                for (o, l, r) in items:
                    last = pe.matmul(o, l, r, start=True, stop=True)
                return last
            return S.add("pe", fn, reads=reads, writes=psk(bank, nb))

        def tr_multi(items, reads, bank, nb=1):
            items = list(items)

            def fn(pe, items=items):
                last = None
                for (o, i_, idn) in items:
                    last = pe.transpose(o, i_, idn)
                return last
            return S.add("pe", fn, reads=reads, writes=psk(bank, nb))

        def act(out, in_, func, reads, writes, **kw):
            def fn(e, out=out, in_=in_, func=func, kw=kw):
                return e.activation(out, in_, func, **kw)
            return S.add("act", fn, reads=reads, writes=writes)

        def tt(out, in0, in1, op, reads, writes):
            def fn(e, out=out, in0=in0, in1=in1, op=op):
                return e.tensor_tensor(out, in0, in1, op)
            return S.add("dve", fn, reads=reads, writes=writes)

        def stt(out, in0, scalar, in1, op0, op1, reads, writes):
            def fn(e, out=out, in0=in0, scalar=scalar, in1=in1, op0=op0, op1=op1):
                return e.scalar_tensor_tensor(out, in0, scalar, in1, op0, op1)
            return S.add("dve", fn, reads=reads, writes=writes)

        def ts(out, in0, s1, s2, op0, op1, reads, writes):
            def fn(e, out=out, in0=in0, s1=s1, s2=s2, op0=op0, op1=op1):
                if op1 is None:
                    return e.tensor_scalar(out, in0, s1, None, op0)
                return e.tensor_scalar(out, in0, s1, s2, op0, op1)
            return S.add("dve", fn, reads=reads, writes=writes)

        def recip(out, in_, reads, writes):
            def fn(e, out=out, in_=in_):
                return e.reciprocal(out, in_)
            return S.add("dve", fn, reads=reads, writes=writes)

        def dcopy(out, in_, reads, writes):
            def fn(e, out=out, in_=in_):
                return e.tensor_copy(out, in_)
            return S.add("dve", fn, reads=reads, writes=writes)

        def dma(q, out, in_, reads, writes, slot):
            def fn(e, out=out, in_=in_):
                return e.dma_start(out=out, in_=in_)
            return S.add(q, fn, reads=reads, writes=writes, dma_slot=slot)

        def dump(name, ap, shape, reads, bf=False):
            if not DEBUG:
                return
            d = dout("dbg_" + name, shape)
            dbg_outs[name] = shape
            dma("pool" if bf else "sp", d, ap, reads, [("dbg", name)], ("dbg", name))

        slab_cnt = [0]
        slab_ids = {}
        cur_pass = [0]

        pend_st = []

        def flush_store():
            idx_, slot_, sz_ = pend_st.pop(0)
            dma("pool", wscr[idx_, :, 0:sz_], SL[:, slot_, 0:sz_], [("slab", slot_)], [("wscr", idx_)],
                ("wst", slot_))

        def load_slab(pieces, shape, sid=None):
            slot = slab_cnt[0] % R_SLAB
            slab_cnt[0] += 1
            k, n = shape
            while any(ps_[1] == slot for ps_ in pend_st):
                flush_store()
            view = SL[:, slot, 0:k * n].rearrange("p (k n) -> p k n", k=k)
            if sid is None:
                sid = tuple((a, b, str(src.tensor.name), str(src.offset), str(src.ap)) for (a, b), src in pieces)
            first = sid not in slab_ids
            if first:
                slab_ids[sid] = len(slab_ids)
                assert len(slab_ids) <= NSLABS
            idx = slab_ids[sid]
            if first or not USE_WSCR:
                def fn(e, pieces=pieces, view=view):
                    insts = []
                    for (a, b), src in pieces:
                        insts.append(e.dma_start(out=view[:, :, a:b], in_=src))
                    return insts
                S.add("pool", fn, reads=[], writes=[("slab", slot)], dma_slot=("slab", slot), ninc=len(pieces))
                if USE_WSCR and npass > 1:
                    pend_st.append((idx, slot, k * n))
                    while len(pend_st) > 2:
                        flush_store()
            else:
                dma("pool", SL[:, slot, 0:k * n], wscr[idx, :, 0:k * n], [("wscr", idx)], [("slab", slot)],
                    ("slab", slot))
            return view, ("slab", slot)

        def wslab(wv, c0, ncols=512, k0=0, k=8):
            return load_slab([((0, ncols), wv[:, k0:k0 + k, c0:c0 + ncols])], (k, ncols))

        dma("sp", cmisc[:, :], cm_d[:, :], [], [("cm",)], ("c0",))
        dma("sp", gcols[:, :], gcols_d[:, :], [], [("gcols",)], ("c1",))
        dma("sp", ggm[:, :], ggm_d[:, :], [], [("ggm",)], ("c2",))
        dma("sp", gfin[:, :], gfin_d[:, :], [], [("gfin",)], ("c3",))
        dma("sp", bsr[:, :, :], bsr_d.rearrange("p (g n) -> p g n", g=8), [], [("bsr",)], ("c4",))
        dma("sp", bsrs[:, :, :], bsrs_d.rearrange("p (g n) -> p g n", g=8), [], [("bsrs",)], ("c5",))
        T4f = T4[:, :, :].rearrange("p a b -> p (a b)")
        dma("sp", T4[:, 0, :], wsT_d[:, :], [], t4k(0), ("c6",))
        dma("sp", T4[:, 1, 0:8 * SC], wsTs_d[:, :], [], [("T4", 1)], ("c7",))
        dcopy(identb[:, :], identf, [("cm",)], [("identb",)])
        S.add("dve", lambda e: e.memset(onesb[:, :], 1.0), reads=[], writes=[("onesb",)])
        dma("pool", qmaskb[:, :, :], qmask_d.rearrange("p (i n) -> p i n", i=NSAMP), [], [("qmaskb",)], ("c8",))
        dma("sp", T4[:, 2, 0:128], tril_d[:, :], [], [("T4", 2)], ("c9",))
        tril = T4[:, 2, 0:128]
        tt(wsTb[:, :, :], T4[:, 0, :].rearrange("p (g n) -> p g n", g=8),
           tril.unsqueeze(1).broadcast_to([128, 8, 128]), ALU.mult,
           t4k(0) + [("T4", 2)], [("wsTb",)])
        tt(wsTsb[:, :, :], T4[:, 1, 0:8 * SC].rearrange("p (g n) -> p g n", g=8),
           cmv("bms").unsqueeze(1).broadcast_to([128, 8, SC]), ALU.mult,
           [("T4", 1), ("cm",)], [("wsTsb",)])

        gmix = gcols[:, 0:8]; gmlp = gcols[:, 8:16]; gple = gcols[:, 16:24]; gret = gcols[:, 24:40]

        def groups(last):
            g = [(0, 0, NS)]
            if last:
                g.append((1, NS, SC))
            return g

        def chunks(last):
            cs = [(c, c * 128, 128) for c in range(NS // 128)]
            if last:
                cs.append((4, NS, SC))
            return cs

        def hkeys(g):
            return [("hT", k, g) for k in range(8)]

        def rmsnorm_fm(gcol, last):
            for (g, c0, n) in groups(last):
                b = ps1("D")
                for k in range(8):
                    act(A16[:, k, c0:c0 + n], hT[:, k, c0:c0 + n], AF.Square, [("hT", k, g)], [("A16", k, g)])
                    S.add("pe", lambda pe, k=k, b=b, c0=c0, n=n: pe.matmul(
                        PS[:, b, 0:n], onesb[:, :], A16[:, k, c0:c0 + n], start=(k == 0), stop=(k == 7)),
                        reads=[("A16", k, g), ("onesb",)], writes=psk(b), accum=(k > 0))
                act(T4[:, 3, 0:n], PS[:, b, 0:n], AF.Sqrt, psk(b), [("T4", 3)], scale=1.0 / D, bias=EPS)
                recip(T4[:, 3, 512:512 + n], T4[:, 3, 0:n], [("T4", 3)], [("T4", 3, "b")])
                for k in range(8):
                    stt(nT[:, k, c0:c0 + n], hT[:, k, c0:c0 + n], gcol[:, k:k + 1], T4[:, 3, 512:512 + n],
                        ALU.mult, ALU.mult, [("hT", k, g), ("T4", 3, "b"), ("gcols",)], [("nT", k, g)])
                if g == 0:
                    fine_next[0] = True

        def fm_proj(slab, skey, ft_cols, g, c0, n, bank, rhs_buf=None, rkeys=None, K=8):
            rb = nT if rhs_buf is None else rhs_buf
            rk = NTK(g) if rkeys is None else rkeys
            a, b_ = ft_cols
            if fine_next[0] and rhs_buf is None and g == 0:
                fine_next[0] = False
                op = None
                for k in range(K):
                    op = S.add("pe", lambda pe, k=k: pe.matmul(
                        PS[:, bank, 0:n], slab[:, k, a:b_], rb[:, k, c0:c0 + n], start=(k == 0), stop=(k == K - 1)),
                        reads=[skey, rk[k]], writes=psk(bank), accum=(k > 0))
                return op
            return mm_group(PS[:, bank, 0:n], [(slab[:, k, a:b_], rb[:, k, c0:c0 + n]) for k in range(K)],
                            [skey] + rk, bank)

        for p in range(npass):
            if STOP == "setup":
                break
            last = (p == NPASS - 1)
            while pend_st:
                flush_store()
            t0 = p * NS
            grp = groups(last)
            chk = chunks(last)

            S.alias([("cs",)], A8B_KEYS)
            dma("sp", cosT[:, 0:NS], cos_d[:, t0:t0 + NS], [], [("cs",)], ("cs0",))
            dma("sp", sinT[:, 0:NS], sin_d[:, t0:t0 + NS], [], [("cs", 1)], ("cs1",))
            dma("sp", cosT[:, NS:NC], cos_d[:, SEQ:SEQ + SC], [], [("cs", 2)], ("cs2",))
            dma("sp", sinT[:, NS:NC], sin_d[:, SEQ:SEQ + SC], [], [("cs", 3)], ("cs3",))
            CSK = [("cs",), ("cs", 1), ("cs", 2), ("cs", 3)]
            for (c, c0, n) in chunks(p == 0):
                g = 0 if c < 4 else 1
                slot = c % 2
                src = xp[t0 + c0:t0 + c0 + n, :] if c < 4 else xs[:, :]
                if not (p > 0 and c < 2):
                    dma("sp", T4[0:n, slot, :], src, [], t4k(slot), ("xs", slot))
                b = ps2("D")
                tr_multi([(PS[:, b + k // 4, (k % 4) * 128:(k % 4) * 128 + n],
                           T4[0:n, slot, k * 128:(k + 1) * 128], identf[0:n, 0:n]) for k in range(8)],
                         t4k(slot) + [("cm",)], b, 2)
                src_ps = PS[:, b:b + 2, :].rearrange("p b (j n) -> p (b j) n", j=4)[:, :, 0:n]
                act(hT[:, :, c0:c0 + n], src_ps, AF.Copy, psk(b, 2), hkeys(g))
            rmsnorm_fm(gmix, p == 0)
            if p == 0:
                dump("hT0", hT[:, :, 0:NS], [128, 8, NS], hkeys(0))
                dump("nT0", nT[:, :, 0:NS], [128, 8, NS], NTK(0), bf=True)

            if STOP == "A":
                break
            S.alias(HB_B_KEYS, HB_D_KEYS + HB_C_KEYS)
            S.alias([("km",)], A8A_KEYS)
            from collections import deque
            pending = deque()

            pump_cnt = [0]

            def pump(k=1, force=False):
                for _ in range(k):
                    pump_cnt[0] += 1
                    if not force and pump_cnt[0] % 2 == 0:
                        continue
                    if pending:
                        pending.popleft()()

            slabs = {}

            def F1(h, with_s):
                qc0 = h * 256
                kc0 = 1024 + h * 256
                slab_qk, kqk = load_slab([((0, 256), w_in_v[:, :, qc0:qc0 + 256]),
                                          ((256, 512), w_in_v[:, :, kc0:kc0 + 256])], (8, 512))
                slabs[h] = (wslab(w_in_v, 2048 + h * 512), wslab(w_in_v, 4096 + h * 512))
                for (g, c0, n) in groups(with_s):
                    for which in range(2):
                        b = ps2()
                        fm_proj(slab_qk, kqk, (which * 256, which * 256 + 128), g, c0, n, b)
                        fm_proj(slab_qk, kqk, (which * 256 + 128, which * 256 + 256), g, c0, n, b + 1)
                        x1 = PS[:, b, 0:n]; x2 = PS[:, b + 1, 0:n]
                        cs_ = cosT[:, c0:c0 + n]; sn_ = sinT[:, c0:c0 + n]
                        t1 = T4[:, 0, 0:n]; t2 = T4[:, 0, 512:512 + n]; t3 = T4[:, 1, 0:n]; t4 = T4[:, 1, 512:512 + n]
                        tt(t1, x1, cs_, ALU.mult, psk(b) + CSK, [("T4", 0)])
                        tt(t2, x2, sn_, ALU.mult, psk(b + 1) + CSK, [("T4", 0, "b")])
                        tt(t3, x2, cs_, ALU.mult, psk(b + 1) + CSK, [("T4", 1)])
                        tt(t4, x1, sn_, ALU.mult, psk(b) + CSK, [("T4", 1, "b")])
                        if which == 0:
                            tt(t1, t1, t2, ALU.subtract, [("T4", 0), ("T4", 0, "b")], [("T4", 0)])
                            tt(t3, t3, t4, ALU.add, [("T4", 1), ("T4", 1, "b")], [("T4", 1)])
                            dq = qdec[:, h, :] if g == 0 else qdec_s[:, h, :]
                            tt(qkT[:, 0, c0:c0 + n], t1, dq, ALU.mult, [("T4", 0), ("cm",)], [("qk", 0, g)])
                            tt(qkT[:, 1, c0:c0 + n], t3, dq, ALU.mult, [("T4", 1), ("cm",)], [("qk", 1, g)])
                        else:
                            tt(qkT[:, 2, c0:c0 + n], t1, t2, ALU.subtract,
                               [("T4", 0), ("T4", 0, "b")], [("qk", 2, g)])
                            tt(qkT[:, 3, c0:c0 + n], t3, t4, ALU.add,
                               [("T4", 1), ("T4", 1, "b")], [("qk", 3, g)])
                        pump()

            def F2(h, with_s):
                (slab_v, kv), (slab_g, kg) = slabs[h]
                for (c, c0, n) in chunks(with_s):
                    g = 0 if c < 4 else 1
                    bv = ps1()
                    mm_group(PS[0:n, bv, :], [(nT[:, k, c0:c0 + n], slab_v[:, k, :]) for k in range(8)],
                             NTK(g) + [kv], bv)
                    act(vtm[0:n, c, :], PS[0:n, bv, :], AF.Copy, psk(bv), [("vtm", c)])
                    pump()
                    bg = ps1()
                    mm_group(PS[0:n, bg, :], [(nT[:, k, c0:c0 + n], slab_g[:, k, :]) for k in range(8)],
                             NTK(g) + [kg], bg)
                    act(gsm[0:n, c, :], PS[0:n, bg, :], AF.Silu, psk(bg), [("gs", c)])
                    pump()
                b = ps1()
                pst = PS[:, b, :].bitcast(BF16)
                tr_multi([(pst[:, c * 256 + dc * 128:c * 256 + (dc + 1) * 128], qkT[:, 2 + dc, c * 128:(c + 1) * 128],
                           identb[:, :]) for c in range(4) for dc in range(2)],
                         [("qk", 2, 0), ("qk", 3, 0), ("identb",)], b)
                sdb = stdec[:, h * 4:(h + 1) * 4].unsqueeze(2).broadcast_to([128, 4, 256])
                tt(ktm[:, 0:4, :], pst[:, 0:1024].rearrange("p (c n) -> p c n", c=4), sdb, ALU.mult,
                   psk(b) + [("cm",)], [("ktm", c) for c in range(4)])
                if with_s:
                    b = ps1()
                    pst = PS[:, b, :].bitcast(BF16)
                    tr_multi([(pst[0:SC, dc * 128:(dc + 1) * 128], qkT[:, 2 + dc, NS:NC], identb[:, :])
                              for dc in range(2)], [("qk", 2, 1), ("qk", 3, 1), ("identb",)], b)
                    act(ktm[0:SC, 4, :], pst[0:SC, 0:256], AF.Copy, psk(b) + [("cm",)], [("ktm", 4)],
                        scale=stdec_s[0:SC, h:h + 1])

            def B1(h):
                if p > 0:
                    act(Sbf[:, :, :], S32[:, h, :, :], AF.Copy, [("S32", h)], [("Sbf",)])
                QK = [("qk", i, 0) for i in range(4)]
                sb_ = []
                for ck in range(4):
                    nq = NS - ck * 128
                    bsc = ps1()
                    sb_.append(bsc)
                    mm_group(PS[:, bsc, 0:nq],
                             [(qkT[:, 2 + dc, ck * 128:(ck + 1) * 128], qkT[:, dc, ck * 128:NS]) for dc in range(2)],
                             QK, bsc)
                LV = int(os.environ.get("MK_B1", "99"))
                if LV < 1:
                    return
                for ck in range(4):
                    nq = NS - ck * 128
                    stt(scm[:, ck, 0:nq], PS[:, sb_[ck], 0:nq], rowsc[:, h * 4 + ck:h * 4 + ck + 1], cmask[:, 0:nq],
                        ALU.mult, ALU.mult, psk(sb_[ck]) + [("cm",)], [("scm", ck)])
                pump()
                if LV < 2:
                    return
                bu = ps2()
                for dc in range(2):
                    mm_group(PS[:, bu + dc, :],
                             [(ktm[:, c, dc * 128:(dc + 1) * 128], vtm[:, c, :]) for c in range(4)],
                             [("ktm", c) for c in range(4)] + [("vtm", c) for c in range(4)], bu + dc)
                Sv = S32[:, h, :, :]
                Uv = PS[:, bu:bu + 2, :]
                if p == 0:
                    dcopy(Sv, Uv, psk(bu, 2), [("S32", h)])
                else:
                    stt(Sv, Sv, chdec[h], Uv, ALU.mult, ALU.add, psk(bu, 2) + [("S32", h)], [("S32", h)])
                if LV < 3:
                    return
                ob = []
                for cq in range(4):
                    bo = ps1()
                    ob.append(bo)
                    pairs = [(scm[:, ck, (cq - ck) * 128:(cq - ck + 1) * 128], vtm[:, ck, :]) for ck in range(cq + 1)]
                    rk = [("scm", ck) for ck in range(cq + 1)] + [("vtm", ck) for ck in range(cq + 1)]
                    if p > 0:
                        pairs += [(qkT[:, dc, cq * 128:(cq + 1) * 128], Sbf[:, dc, :]) for dc in range(2)]
                        rk += [("qk", 0, 0), ("qk", 1, 0), ("Sbf",)]
                    mm_group(PS[:, bo, :], pairs, rk, bo)
                if LV < 5:
                    return
                for cq in range(4):
                    act(junkb[:, 0:512], PS[:, ob[cq], :], AF.Square, psk(ob[cq]), [("junkb",), ("stat", 0, cq)],
                        accum_out=stat[:, cq:cq + 1])
                act(stat[:, 4:8], stat[:, 0:4], AF.Sqrt, [("stat", 0, cq) for cq in range(4)], [("stat", 1)],
                    scale=1.0 / DV, bias=EPS)
                recip(stat[:, 8:12], stat[:, 4:8], [("stat", 1)], [("stat", 2)])
                for cq in range(4):
                    stt(yatm[:, cq, :], PS[:, ob[cq], :], stat[:, 8 + cq:9 + cq], gsm[:, cq, :],
                        ALU.mult, ALU.mult, psk(ob[cq]) + [("stat", 2), ("gs", cq)], [("yatm", cq)])

                pump()

            def B2(h):
                bt = ps2()
                pst = PS[:, bt:bt + 2, :].rearrange("p b n -> p (b n)").bitcast(BF16)
                pv = pst.rearrange("p (b j c n) -> p b j c n", b=2, j=4, c=4)
                items = []
                for j in range(4):
                    bb = PS[:, bt + j // 2, :].bitcast(BF16)
                    for cq in range(4):
                        o0 = (j % 2) * 512 + cq * 128
                        items.append((bb[:, o0:o0 + 128], yatm[:, cq, j * 128:(j + 1) * 128], identb[:, :]))
                tr_multi(items, [("yatm", cq) for cq in range(4)] + [("identb",)], bt, 2)
                for half in range(2):
                    bb = PS[:, bt + half, :].bitcast(BF16)
                    srcv = bb[:, 0:1024].rearrange("p (j n) -> p j n", j=2)
                    j0 = h * 4 + half * 2
                    grb = gret[:, j0:j0 + 2].unsqueeze(2).broadcast_to([128, 2, NS])
                    tt(A16[:, j0:j0 + 2, 0:NS], srcv, grb, ALU.mult,
                       psk(bt + half) + [("gcols",)], [("A16", j0 + j, 0) for j in range(2)])
                pump()
                if last:
                    dst = s_p[h].rearrange("(dc p) v -> p dc v", p=128)
                    dma("sp", dst, S32[:, h, :, :], [("S32", h)], [("s_p_out",)], ("spout",))

            def sample_steps(h):
                def ptt(out, in0, in1, op, reads, writes):
                    S.add("pool", lambda e, out=out, in0=in0, in1=in1, op=op: e.tensor_tensor(out, in0, in1, op),
                          reads=reads, writes=writes)
                for dc in range(2):
                    ptt(qm[:, dc, :, :], qkT[:, dc, NS:NC].unsqueeze(1).broadcast_to([128, NSAMP, SC]),
                        qmaskb[:, :, :], ALU.mult, [("qk", dc, 1), ("qmaskb",)], [("qm", dc)])
                ptt(km[0:SC, :, :], ktm[0:SC, 4, :].unsqueeze(1).broadcast_to([SC, NSAMP, 256]),
                    kmask[0:SC, :].unsqueeze(2).broadcast_to([SC, NSAMP, 256]), ALU.mult,
                    [("ktm", 4), ("cm",)], [("km",)])

                def head():
                    bsc = ps1()
                    mm_group(PS[0:SC, bsc, 0:SC],
                             [(qkT[:, 2 + dc, NS:NC], qkT[:, dc, NS:NC]) for dc in range(2)],
                             [("qk", i, 1) for i in range(4)], bsc)
                    tt(scm_s[0:SC, 0:SC], PS[0:SC, bsc, 0:SC], maskTs[0:SC, h, :], ALU.mult,
                       psk(bsc) + [("cm",)], [("scm_s",)])
                    S.add("pe", lambda pe: pe.matmul(PS[0:SC, SBANK, :], scm_s[0:SC, 0:SC],
                                                     vtm[0:SC, 4, :], start=True, stop=False),
                          reads=[("scm_s",), ("vtm", 4)], writes=psk(SBANK))
                pending.append(head)

                def sload(i):
                    sl = i % 2
                    src = st_in[i, h].rearrange("(dc p) v -> p dc v", p=128)
                    dma("sp", Ss32[:, sl, :, :], src, [], [("Ss32", sl)], ("ssin", sl))
                    act(Ssb[:, sl, :, :], Ss32[:, sl, :, :], AF.Copy, [("Ss32", sl)], [("Ssb", sl)])

                def cross(i):
                    sl = i % 2
                    lastg = (i == NSAMP - 1)

                    def fn(pe, i=i, sl=sl, lastg=lastg):
                        r = None
                        for dc in range(2):
                            r = pe.matmul(PS[0:SC, SBANK, :], qm[:, dc, i, :], Ssb[:, sl, dc, :],
                                          start=False, stop=(lastg and dc == 1))
                        return r
                    S.add("pe", fn, reads=[("qm", 0), ("qm", 1), ("Ssb", sl)], writes=psk(SBANK), accum=True)

                def one(i):
                    sl = i % 2
                    if i == 0:
                        sload(0)
                    if i >= 1:
                        cross(i - 1)
                    if i + 1 < NSAMP:
                        sload(i + 1)
                    bu = ps2()
                    for dc in range(2):
                        mm_group(PS[:, bu + dc, :], [(km[0:SC, i, dc * 128:(dc + 1) * 128], vtm[0:SC, 4, :])],
                                 [("km",), ("vtm", 4)], bu + dc)
                    stt(Ss32[:, sl, :, :], Ss32[:, sl, :, :], chdec_s[h], PS[:, bu:bu + 2, :],
                        ALU.mult, ALU.add, psk(bu, 2) + [("Ss32", sl)], [("Ss32", sl)])
                    dst = s_s[i, h].rearrange("(dc p) v -> p dc v", p=128)
                    dma("sp", dst, Ss32[:, sl, :, :], [("Ss32", sl)], [("s_s_out", sl)], ("ssout", sl))
                for i in range(NSAMP):
                    pending.append(lambda i=i: one(i))

                def tail():
                    np_ = SC
                    cross(NSAMP - 1)
                    act(junkb[0:np_, 0:512], PS[0:np_, SBANK, :], AF.Square, psk(SBANK), [("junkb",), ("stat", "s0")],
                        accum_out=stat[0:np_, 40:41])
                    act(stat[0:np_, 41:42], stat[0:np_, 40:41], AF.Sqrt, [("stat", "s0")], [("stat", "s1")],
                        scale=1.0 / DV, bias=EPS)
                    recip(stat[0:np_, 42:43], stat[0:np_, 41:42], [("stat", "s1")], [("stat", "s2")])
                    ya_s = pT[0:np_, 0, 0:512]
                    stt(ya_s, PS[0:np_, SBANK, :], stat[0:np_, 42:43], gsm[0:np_, 4, :],
                        ALU.mult, ALU.mult, psk(SBANK) + [("stat", "s2"), ("gs", 4)], [("pT", 0)])
                    bt = ps1()
                    pst = PS[:, bt, :].bitcast(BF16)
                    tr_multi([(pst[:, j * 128:j * 128 + np_], pT[0:np_, 0, j * 128:(j + 1) * 128],
                               identb[0:np_, 0:np_]) for j in range(4)],
                             [("pT", 0), ("identb",)], bt)
                    srcv = pst[:, 0:512].rearrange("p (j n) -> p j n", j=4)[:, :, 0:np_]
                    grb = gret[:, h * 4:(h + 1) * 4].unsqueeze(2).broadcast_to([128, 4, np_])
                    tt(A16[:, h * 4:(h + 1) * 4, NS:NC], srcv, grb, ALU.mult,
                       psk(bt) + [("gcols",)], [("A16", h * 4 + j, 1) for j in range(4)])
                pending.append(tail)

            horder = [(p + i) % H_A for i in range(H_A)]
            BST = os.environ.get("MK_BSTOP", "")
            for hi, h in enumerate(horder):
                ws_ = (hi == 0)
                if hi == 0:
                    F1(h, ws_)
                if BST == "F1":
                    break
                F2(h, ws_)
                if BST == "F2":
                    break
                if ws_ and BST != "nosamp":
                    sample_steps(h)
                if BST == "samp":
                    break
                B1(h)
                if BST == "B1":
                    break
                if hi + 1 < H_A:
                    F1(horder[hi + 1], False)
                B2(h)
                if BST == "B2":
                    break
            while pending:
                pump(force=True)
            if p == 0:
                dump("yaT0", A16[:, :, 0:NS], [128, 16, NS], [("A16", k, 0) for k in range(16)], bf=True)

            if STOP == "B":
                break
            S.alias(HB_C_KEYS, HB_B_KEYS)
            s0, k0_ = wslab(w_in_v, 7168)
            s1, k1_ = wslab(w_in_v, 7168 + 512)
            for (c, c0, n) in chk:
                g = 0 if c < 4 else 1
                b = ps2("D")
                for s_, (sl_, sk_) in enumerate(((s0, k0_), (s1, k1_))):
                    mm_group(PS[0:n, b + s_, :], [(nT[:, k, c0:c0 + n], sl_[:, k, :]) for k in range(8)],
                             NTK(g) + [sk_], b + s_)
                tsl = c % 2
                tg = T4[0:n, tsl, :]
                act(tg, PS[0:n, b:b + 2, :].rearrange("p b n -> p (b n)"), AF.Gelu, psk(b, 2), t4k(tsl))
                S.add("dve", lambda e, tg=tg, n=n: e.bn_stats(stat[0:n, 8:14], tg[:, 0:512]),
                      reads=t4k(tsl), writes=[("stat", "bs0")])
                S.add("dve", lambda e, tg=tg, n=n: e.bn_stats(stat[0:n, 14:20], tg[:, 512:1024]),
                      reads=t4k(tsl), writes=[("stat", "bs1")])
                S.add("dve", lambda e, n=n: e.bn_aggr(stat[0:n, 20:22], stat[0:n, 8:20]),
                      reads=[("stat", "bs0"), ("stat", "bs1")], writes=[("stat", "mv")])
                act(stat[0:n, 22:23], stat[0:n, 21:22], AF.Sqrt, [("stat", "mv")], [("stat", "sd")], bias=EPS)
                recip(stat[0:n, 23:24], stat[0:n, 22:23], [("stat", "sd")], [("stat", "rs")])
                stt(tg, tg, stat[0:n, 20:21], ggm[0:n, :], ALU.subtract, ALU.mult,
                    t4k(tsl) + [("stat", "mv"), ("ggm",)], t4k(tsl))
                if c < 4:
                    ts(gvtm[0:n, c, :], tg, stat[0:n, 23:24], None, ALU.mult, None,
                       t4k(tsl) + [("stat", "rs")], [("gv", c)])
                else:
                    ts(tg, tg, stat[0:n, 23:24], None, ALU.mult, None,
                       t4k(tsl) + [("stat", "rs")], t4k(tsl))
                    dma("sp", gv_s[:, :], tg, t4k(tsl), [("gv_s_out",)], ("gvout",))
                    dcopy(gvtm[0:n, c, :], tg, t4k(tsl), [("gv", c)])
            S.alias(A8A_KEYS, [("km",)])
            for s_ in range(2):
                sl_, sk_ = wslab(w_in_v, 6144 + s_ * 512)
                for ft in range(4):
                    for (g, c0, n) in grp:
                        b = ps1("D")
                        fm_proj(sl_, sk_, (ft * 128, ft * 128 + 128), g, c0, n, b)
                        act(A8a[:, s_ * 4 + ft, c0:c0 + n], PS[:, b, 0:n], AF.Gelu, psk(b),
                            [("A8a", s_ * 4 + ft, g)])
            S.alias(A8B_KEYS, CSK)
            for (c, c0, n) in chk:
                g = 0 if c < 4 else 1
                b = ps2("D")
                items = []
                for gg in range(8):
                    o_ = PS[:, b + gg // 4, (gg % 4) * 128:(gg % 4) * 128 + n]
                    if c < 4:
                        items.append((o_, gvtm[:, c, gg * 128:(gg + 1) * 128], wsTb[:, gg, :]))
                    else:
                        items.append((o_, gvtm[0:SC, c, gg * 128:(gg + 1) * 128], wsTsb[0:SC, gg, :]))
                mm_multi(items, [("gv", c), ("wsTb",), ("wsTsb",)], b, 2)
                srcv = PS[:, b:b + 2, :].rearrange("p b (j n) -> p (b j) n", j=4)[:, :, 0:n]
                tsl = 2 + c % 2
                tmp = T4[:, tsl, 0:8 * n].rearrange("p (g n) -> p g n", g=8)
                bias_ = bsr[:, :, :] if c < 4 else bsrs[:, :, :]
                tt(tmp, srcv, bias_, ALU.add, psk(b, 2) + [("bsr",), ("bsrs",)], t4k(tsl))
                tt(A8b[:, :, c0:c0 + n], tmp, A8a[:, :, c0:c0 + n], ALU.mult,
                   t4k(tsl) + [("A8a", k, g) for k in range(8)], [("A8b", k, g) for k in range(8)])
            if p == 0:
                dump("ybT0", A8b[:, :, 0:NS], [128, 8, NS], [("A8b", k, 0) for k in range(8)], bf=True)

            if STOP == "C":
                break
            S.alias(HB_D_KEYS, HB_C_KEYS)
            for which, (buf, nm) in enumerate(((sga, "sga"), (sgb, "sgb"))):
                for s_ in range(2):
                    sl_, sk_ = wslab(w_in_v, 8192 + which * 1024 + s_ * 512)
                    for ft in range(4):
                        for (g, c0, n) in grp:
                            b = ps1("D")
                            fm_proj(sl_, sk_, (ft * 128, ft * 128 + 128), g, c0, n, b)
                            act(buf[:, s_ * 4 + ft, c0:c0 + n], PS[:, b, 0:n], AF.Sigmoid, psk(b),
                                [(nm, s_ * 4 + ft, g)])
            for hf in range(2):
                wb_, kb_ = wslab(w_br_b_v, hf * 512)
                for pr in range(2):
                    wa_, ka_ = wslab(w_br_a_v, hf * 512 + pr * 256, ncols=256, k=16)
                    for fl in range(2):
                        f_ = hf * 4 + pr * 2 + fl
                        for (g, c0, n) in grp:
                            b = ps2("D")
                            mm_group(PS[:, b, 0:n],
                                     [(wa_[:, k, fl * 128:(fl + 1) * 128], A16[:, k, c0:c0 + n]) for k in range(16)],
                                     [ka_] + [("A16", k, g) for k in range(16)], b)
                            fcol = (pr * 2 + fl) * 128
                            mm_group(PS[:, b + 1, 0:n],
                                     [(wb_[:, k, fcol:fcol + 128], A8b[:, k, c0:c0 + n]) for k in range(8)],
                                     [kb_] + [("A8b", k, g) for k in range(8)], b + 1)
                            m1 = T4[:, 0, 0:n]; m2 = T4[:, 1, 0:n]
                            tt(m1, PS[:, b, 0:n], sga[:, f_, c0:c0 + n], ALU.mult,
                               psk(b) + [("sga", f_, g)], [("T4", 0)])
                            tt(m2, PS[:, b + 1, 0:n], sgb[:, f_, c0:c0 + n], ALU.mult,
                               psk(b + 1) + [("sgb", f_, g)], [("T4", 1)])
                            tt(A8a[:, f_, c0:c0 + n], m1, m2, ALU.add, [("T4", 0), ("T4", 1)],
                               [("A8a", f_, g)])
            if p == 0:
                dump("mrgT0", A8a[:, :, 0:NS], [128, 8, NS], [("A8a", k, 0) for k in range(8)], bf=True)
            for s_ in range(2):
                sl_, sk_ = wslab(w_o_v, s_ * 512)
                for ft in range(4):
                    f_ = s_ * 4 + ft
                    for (g, c0, n) in grp:
                        b = ps1("D")
                        fm_proj(sl_, sk_, (ft * 128, ft * 128 + 128), g, c0, n, b,
                                rhs_buf=A8a, rkeys=[("A8a", k, g) for k in range(8)])
                        tt(hT[:, f_, c0:c0 + n], PS[:, b, 0:n], hT[:, f_, c0:c0 + n], ALU.add,
                           psk(b) + [("hT", f_, g)], [("hT", f_, g)])
            if p == 0:
                dump("hT1", hT[:, :, 0:NS], [128, 8, NS], hkeys(0))

            if STOP == "D":
                break
            rmsnorm_fm(gmlp, last)
            for ffg in range(2):
                for s_ in range(4):
                    sl_, sk_ = wslab(w_up_v, ffg * 2048 + s_ * 512)
                    for ft in range(4):
                        kk = s_ * 4 + ft
                        for (g, c0, n) in grp:
                            b = ps1("D")
                            fm_proj(sl_, sk_, (ft * 128, ft * 128 + 128), g, c0, n, b)
                            tsl = kk % 2
                            r_ = T4[:, tsl, 0:n]
                            act(r_, PS[:, b, 0:n], AF.Relu, psk(b), [("T4", tsl)])
                            tt(A16[:, kk, c0:c0 + n], r_, r_, ALU.mult, [("T4", tsl)], [("A16", kk, g)])
                for cq in range(4):
                    sl_, sk_ = wslab(w_down_v, cq * 256, ncols=256, k0=ffg * 16, k=16)
                    for fl in range(2):
                        f_ = cq * 2 + fl
                        for (g, c0, n) in grp:
                            b = ps1("D")
                            mm_group(PS[:, b, 0:n],
                                     [(sl_[:, k, fl * 128:(fl + 1) * 128], A16[:, k, c0:c0 + n]) for k in range(16)],
                                     [sk_] + [("A16", k, g) for k in range(16)], b)
                            tt(hT[:, f_, c0:c0 + n], PS[:, b, 0:n], hT[:, f_, c0:c0 + n], ALU.add,
                               psk(b) + [("hT", f_, g)], [("hT", f_, g)])
            if p == 0:
                dump("hT2", hT[:, :, 0:NS], [128, 8, NS], hkeys(0))

            rmsnorm_fm(gple, last)
            for (c, c0, n) in chk:
                slot = c % 2
                src = pp[t0 + c0:t0 + c0 + n, :] if c < 4 else psm[:, :]
                dma("sp", T4[0:n, slot, 0:256], src, [], [("T4", slot)], ("xs", slot))
                b = ps1("D")
                tr_multi([(PS[:, b, dc * 128:dc * 128 + n], T4[0:n, slot, dc * 128:(dc + 1) * 128],
                           identf[0:n, 0:n]) for dc in range(2)], [("T4", slot), ("cm",)], b)
                srcv = PS[:, b, 0:256].rearrange("p (j n) -> p j n", j=2)[:, :, 0:n]
                act(pT[:, :, c0:c0 + n], srcv, AF.Copy, psk(b), [("pT", 0 if c < 4 else 1)])
            if p + 1 < npass:
                for c in range(2):
                    t1_ = (p + 1) * NS + c * 128
                    dma("sp", T4[0:128, c, :], xp[t1_:t1_ + 128, :], [], t4k(c), ("xs", c))
            wpp_, kpp_ = load_slab([((0, 1024), w_pp_v[:, :, :])], (2, 1024))
            for s_ in range(2):
                sl_, sk_ = wslab(w_pg_v, s_ * 512)
                for ft in range(4):
                    f_ = s_ * 4 + ft
                    for (g, c0, n) in grp:
                        b = ps2("D")
                        fm_proj(sl_, sk_, (ft * 128, ft * 128 + 128), g, c0, n, b)
                        mm_group(PS[:, b + 1, 0:n],
                                 [(wpp_[:, k, f_ * 128:(f_ + 1) * 128], pT[:, k, c0:c0 + n]) for k in range(2)],
                                 [kpp_, ("pT", g)], b + 1)
                        sg_ = T4[:, 2, 0:n]
                        act(sg_, PS[:, b, 0:n], AF.Sigmoid, psk(b), [("T4", 2)])
                        tt(sg_, PS[:, b + 1, 0:n], sg_, ALU.mult, psk(b + 1) + [("T4", 2)], [("T4", 2)])
                        tt(hT[:, f_, c0:c0 + n], sg_, hT[:, f_, c0:c0 + n], ALU.add,
                           [("T4", 2), ("hT", f_, g)], [("hT", f_, g)])

            for (c, c0, n) in chk:
                g = 0 if c < 4 else 1
                b = ps2("D")
                tr_multi([(PS[0:n, b + k // 4, (k % 4) * 128:(k % 4 + 1) * 128], hT[:, k, c0:c0 + n], identf)
                          for k in range(8)], hkeys(g) + [("cm",)], b, 2)
                hv = PS[0:n, b:b + 2, :].rearrange("p b n -> p (b n)")
                act(A8b[0:n, 0:2, 0:512], PS[0:n, b:b + 2, :], AF.Square, psk(b, 2),
                    [("A8b", 0, 0), ("A8b", 1, 0), ("stat", "f0")], accum_out=stat[0:n, 32:33])
                act(stat[0:n, 33:34], stat[0:n, 32:33], AF.Sqrt, [("stat", "f0")], [("stat", "f1")],
                    scale=1.0 / D, bias=EPS)
                recip(stat[0:n, 34:35], stat[0:n, 33:34], [("stat", "f1")], [("stat", "f2")])
                slot = 2 + c % 2
                stt(T4[0:n, slot, :], hv, stat[0:n, 34:35], gfin[0:n, :], ALU.mult, ALU.mult,
                    psk(b, 2) + [("stat", "f2"), ("gfin",)], t4k(slot))
                dst = y_p[t0 + c0:t0 + c0 + n, :] if c < 4 else y_s[:, :]
                dma("sp", dst, T4[0:n, slot, :], t4k(slot), [("y_out", slot)], ("yout", slot))

        while pend_st:
            flush_store()
        eng_sems = {}
        for e in ("pe", "act", "dve", "pool"):
            eng_sems[e] = es.enter_context(nc.semaphore("sem_" + e))
        slot_sems = {}
        for i, s_ in enumerate(sorted(S.slot_cnt.keys(), key=str)):
            slot_sems[s_] = es.enter_context(nc.semaphore("dsem_%d" % i))
        final_slots = [s_ for s_ in S.slot_cnt if s_[0] in ("yout", "ssout", "spout", "gvout", "dbg")]
        S.emit(nc, eng_sems, slot_sems, final_slots)
    return nc, dbg_outs


_CACHE = {}


def kernel(x_prompt, x_sample, p_prompt, p_sample, state_ret, g_mix, w_in, g_ret, w_s, b_s,
           g_gm, w_br_a, w_br_b, w_o, g_mlp, w_up, w_down, g_ple, w_pg, w_pp, g_final):
    f = np.float32
    A = lambda a: np.ascontiguousarray(np.asarray(a), dtype=f)
    cosT, sinT, cm, chdec, chdec_s = _host_consts()
    cm_off, cm_w = _cm_layout(cm)
    cmisc = np.concatenate([cm[k] for k in CM_ORDER], axis=1).astype(f)

    npass = DBG_PASSES if DEBUG else NPASS
    nc, dbg = build_program(cm_off, cm_w, chdec, chdec_s, npass=npass)

    def col(gv, n):
        return np.asarray(gv, dtype=f).reshape(n, 128).T
    gcols = np.concatenate([col(g_mix[0], 8), col(g_mlp[0], 8), col(g_ple[0], 8), col(g_ret[0], 16)], axis=1)
    ggm_rep = np.broadcast_to(np.asarray(g_gm[0], dtype=f)[None, :], (128, D))
    gfin_rep = np.broadcast_to(np.asarray(g_final, dtype=f)[None, :], (128, D))
    ws = np.asarray(w_s[0], dtype=f)
    wsT = np.transpose(ws, (2, 0, 1)).reshape(128, 8 * 128)
    w4 = np.transpose(ws[:, :4, :4], (2, 0, 1))
    wsTs = np.zeros((128, 8, SC), f)
    wsTs[:SC] = np.tile(w4, (NSAMP, 1, NSAMP))
    bs = np.asarray(b_s[0], dtype=f)
    bs_rep = np.broadcast_to(bs[None, :, :], (128, 8, 128)).reshape(128, -1)
    bs_rep_s = np.broadcast_to(np.tile(bs[:, :4], (1, NSAMP))[None], (128, 8, SC)).reshape(128, -1)

    shared = {
        "w_in": A(w_in[0]), "w_br_a": A(w_br_a[0]), "w_br_b": A(w_br_b[0]), "w_o": A(w_o[0]),
        "w_up": A(w_up[0]), "w_down": A(w_down[0]), "w_pg": A(w_pg[0]), "w_pp": A(w_pp[0]),
        "gcols": A(gcols), "ggm_rep": A(ggm_rep), "gfin_rep": A(gfin_rep),
        "wsT": A(wsT), "wsTs": A(wsTs.reshape(128, -1)), "bs_rep": A(bs_rep), "bs_rep_s": A(bs_rep_s),
        "cosT": A(cosT), "sinT": A(sinT), "cmisc": A(cmisc),
        "qmask": A(cm["qmask"]), "tril": A(cm["tril"]),
    }
    xpn = np.asarray(x_prompt); xsn = np.asarray(x_sample)
    ppn = np.asarray(p_prompt); psn = np.asarray(p_sample); stn = np.asarray(state_ret)
    in_maps = []
    for c in range(N_CORES):
        m = dict(shared)
        m["xp"] = A(xpn[c]); m["pp"] = A(ppn[0, c])
        m["xs"] = A(xsn[c * NSAMP:(c + 1) * NSAMP].reshape(SC, D))
        m["ps"] = A(psn[0, c * NSAMP:(c + 1) * NSAMP].reshape(SC, 256))
        m["st"] = A(stn[0, c * NSAMP:(c + 1) * NSAMP])
        in_maps.append(m)
    res = run_bass_kernel_spmd(nc, in_maps, core_ids=list(range(N_CORES)))
    R = res.results
    y_prompt = np.stack([R[c]["y_p"] for c in range(N_CORES)], axis=0).astype(f)
    y_sample = np.concatenate([R[c]["y_s"].reshape(NSAMP, 4, D) for c in range(N_CORES)], axis=0).astype(f)
    s_prompt = np.stack([R[c]["s_p"] for c in range(N_CORES)], axis=0)[None].astype(f)
    s_sample = np.concatenate([R[c]["s_s"] for c in range(N_CORES)], axis=0)[None].astype(f)
    gv_sample = np.concatenate([R[c]["gv_s"].reshape(NSAMP, 4, D) for c in range(N_CORES)], axis=0)[None].astype(f)
    if DEBUG:
        _CACHE["dbg"] = {k: [R[c]["dbg_" + k] for c in range(N_CORES)] for k in dbg}
    return (y_prompt, y_sample, s_prompt, s_sample, gv_sample)
```
